# Optimizing a Trainium2 kernel written in Bass

```python
import math
import jax, jax.numpy as jnp
from jax import lax
import numpy as np

D_MODEL = 2048
BATCH = 32
SEQ = 256
DEPTH = 4
DEC_BATCH = 8
DEC_SEQ = 2048
PAST_LEN = 256

GRID_W = 64
BLOCK = 128
EPS = 1e-6
NEG_INF = -1e30
RET_HEADS = 4
RET_DK = 128
RET_DV = 128
RET_WIDTH = RET_HEADS * RET_DV
LRU_WIDTH = 512
LRU_BLOCKS = 4
LRU_BLOCK_DIM = LRU_WIDTH // LRU_BLOCKS
CONV_W = 4
CONV_LEFT = 2
LRU_C = 8.0
ATT_Q_HEADS = 4
ATT_KV_HEADS = 2
ATT_GROUP = ATT_Q_HEADS // ATT_KV_HEADS
HEAD_DIM = 128
WINDOW = 128
ROPE_BASE = 10000.0
SSM_WIDTH = 512
SSM_GROUP = 16
SSM_GROUPS = SSM_WIDTH // SSM_GROUP
SSM_STATE = 64
N_BRANCH = 4
BRANCH_WIDTH = 512
D_FF = 4 * D_MODEL
IN_SIZES = (RET_WIDTH, RET_WIDTH, RET_WIDTH, RET_WIDTH, LRU_WIDTH, LRU_WIDTH,
            ATT_Q_HEADS * HEAD_DIM, ATT_KV_HEADS * HEAD_DIM, ATT_KV_HEADS * HEAD_DIM,
            SSM_WIDTH, N_BRANCH * D_MODEL)
IN_TOTAL = 4 * RET_WIDTH + 2 * LRU_WIDTH + (ATT_Q_HEADS + 2 * ATT_KV_HEADS) * HEAD_DIM + SSM_WIDTH + N_BRANCH * D_MODEL

kernel_name = "hybrid_diffusion_prefix_trunk_step"


def rms_norm(x, g):
    xf = x.astype(jnp.float32)
    y = xf * lax.rsqrt(jnp.mean(xf * xf, axis=-1, keepdims=True) + EPS)
    return (y * g.astype(jnp.float32)).astype(x.dtype)


def linear_scan(a, b, h0):
    b = b.at[:, 0].add(a[:, 0] * h0)
    def comb(l, r):
        return (l[0] * r[0], r[0] * l[1] + r[1])
    _, h = lax.associative_scan(comb, (a, b), axis=1)
    return h


def retention_scan(q, k, v, log_gamma, s0):
    B_, L = q.shape[:2]
    n = L // BLOCK
    qc = q.reshape(B_, n, BLOCK, RET_HEADS, RET_DK)
    kc = k.reshape(B_, n, BLOCK, RET_HEADS, RET_DK)
    vc = v.reshape(B_, n, BLOCK, RET_HEADS, RET_DV)
    idx = jnp.arange(BLOCK, dtype=jnp.float32)
    rel = idx[:, None] - idx[None, :]
    decay = jnp.where(rel >= 0, jnp.exp(jnp.maximum(rel, 0.0)[None] * log_gamma[:, None, None]), 0.0)
    scores = jnp.einsum('bnihd,bnjhd->bnhij', qc, kc) * decay
    intra = jnp.einsum('bnhij,bnjhe->bnihe', scores, vc)
    w_end = jnp.exp((BLOCK - 1 - idx)[:, None] * log_gamma[None, :])
    kv = jnp.einsum('bnjhd,bnjhe->bnhde', kc * w_end[..., None], vc)
    chunk_decay = jnp.exp(BLOCK * log_gamma)[:, None, None]
    def step(s, kv_n):
        return chunk_decay * s + kv_n, s
    s_last, s_in = lax.scan(step, s0, jnp.moveaxis(kv, 1, 0))
    w_in = jnp.exp((idx + 1)[:, None] * log_gamma[None, :])
    cross = jnp.einsum('bnihd,bnhde->bnihe', qc * w_in[..., None], jnp.moveaxis(s_in, 0, 1))
    return (intra + cross).reshape(B_, L, RET_HEADS, RET_DV), s_last


def retention(q, k, v, g, decay_logit, gn_gain, s0):
    B_, L, _ = q.shape
    f32 = jnp.float32
    q = q.astype(f32).reshape(B_, L, RET_HEADS, RET_DK) * RET_DK ** -0.5
    k = k.astype(f32).reshape(B_, L, RET_HEADS, RET_DK)
    v = v.astype(f32).reshape(B_, L, RET_HEADS, RET_DV)
    log_gamma = -jax.nn.softplus(-decay_logit.astype(f32))
    o_f, s_f = retention_scan(q, k, v, log_gamma[0], s0[:, 0])
    o_b, s_b = retention_scan(jnp.flip(q, 1), jnp.flip(k, 1), jnp.flip(v, 1), log_gamma[1], s0[:, 1])
    o = o_f + jnp.flip(o_b, 1)
    mu = jnp.mean(o, axis=-1, keepdims=True)
    var = jnp.mean(jnp.square(o - mu), axis=-1, keepdims=True)
    o = ((o - mu) * lax.rsqrt(var + EPS)).reshape(B_, L, RET_WIDTH) * gn_gain.astype(f32)
    return jax.nn.silu(g.astype(f32)) * o, jnp.stack([s_f, s_b], axis=1)


def centred_conv(x, w, b):
    L = x.shape[1]
    xp = jnp.pad(x, ((0, 0), (CONV_LEFT, CONV_W - 1 - CONV_LEFT), (0, 0)))
    return sum(xp[:, j:j + L] * w[j] for j in range(CONV_W)) + b


def rglru_scan(x, w_a, b_a, w_x, b_x, lam, h0):
    B_, L, _ = x.shape
    xb = x.reshape(B_, L, LRU_BLOCKS, LRU_BLOCK_DIM)
    r = jax.nn.sigmoid(jnp.einsum('blnd,nde->blne', xb, w_a).reshape(B_, L, LRU_WIDTH) + b_a)
    i = jax.nn.sigmoid(jnp.einsum('blnd,nde->blne', xb, w_x).reshape(B_, L, LRU_WIDTH) + b_x)
    log_a = -LRU_C * r * jax.nn.softplus(-lam.astype(jnp.float32))
    a = jnp.exp(log_a)
    b = jnp.sqrt(-jnp.expm1(2.0 * log_a)) * (i * x)
    return linear_scan(a, b, h0)


def rglru_mixer(x, gate, p, s0):
    xc = centred_conv(x, p['lru_conv_w'], p['lru_conv_b']).astype(jnp.float32)
    outs, finals = [], []
    for d in range(2):
        xd = xc if d == 0 else jnp.flip(xc, 1)
        h = rglru_scan(xd, p['lru_wa'][d], p['lru_ba'][d], p['lru_wx'][d], p['lru_bx'][d], p['lru_lambda'][d], s0[:, d])
        finals.append(h[:, -1])
        outs.append(h if d == 0 else jnp.flip(h, 1))
    y = jax.nn.gelu(gate.astype(jnp.float32)) * (outs[0] + outs[1])
    return y, jnp.stack(finals, axis=1)


def axial_rope(x):
    L = x.shape[1]
    t = jnp.arange(L)
    row = (t // GRID_W).astype(jnp.float32)
    col = (t % GRID_W).astype(jnp.float32)
    n_pairs = HEAD_DIM // 4
    freqs = ROPE_BASE ** (-jnp.arange(n_pairs, dtype=jnp.float32) / n_pairs)
    ang = jnp.concatenate([row[:, None] * freqs, col[:, None] * freqs], axis=-1)
    cos = jnp.cos(ang)[None, :, None, :].astype(x.dtype)
    sin = jnp.sin(ang)[None, :, None, :].astype(x.dtype)
    x1, x2 = x[..., :HEAD_DIM // 2], x[..., HEAD_DIM // 2:]
    return jnp.concatenate([x1 * cos - x2 * sin, x2 * cos + x1 * sin], axis=-1)


def sink_attend(q, k, v, sink, mask):
    s = jnp.einsum('bqkgd,bskd->bkgqs', q, k).astype(jnp.float32) * HEAD_DIM ** -0.5
    if mask is not None:
        s = jnp.where(mask, s, NEG_INF)
    snk = sink.astype(jnp.float32)[None, :, :, None, None]
    m = jnp.maximum(jnp.max(s, axis=-1, keepdims=True), snk)
    pr = jnp.exp(s - m)
    denom = jnp.sum(pr, axis=-1, keepdims=True) + jnp.exp(snk - m)
    return jnp.einsum('bkgqs,bskd->bqkgd', (pr / denom).astype(v.dtype), v)


def context_attention(q, k, v, sink):
    B_, L = q.shape[:2]
    n = L // BLOCK
    qb = jnp.moveaxis(q.reshape(B_, n, BLOCK, ATT_KV_HEADS, ATT_GROUP, HEAD_DIM), 1, 0)
    snk = sink.reshape(ATT_KV_HEADS, ATT_GROUP)
    out = lax.map(lambda qi: sink_attend(qi, k, v, snk, None), qb)
    return jnp.moveaxis(out, 0, 1).reshape(B_, L, ATT_Q_HEADS * HEAD_DIM)


def latent_attention(q, k, v, ck, cv, sink):
    B_, L = q.shape[:2]
    n = L // BLOCK
    P = ck.shape[1]
    qg = q.reshape(B_, L, ATT_KV_HEADS, ATT_GROUP, HEAD_DIM)
    pad = ((0, 0), (BLOCK, BLOCK), (0, 0), (0, 0))
    kp, vp = jnp.pad(k, pad), jnp.pad(v, pad)
    snk = sink.reshape(ATT_KV_HEADS, ATT_GROUP)
    off = jnp.arange(BLOCK)[:, None] - jnp.arange(3 * BLOCK)[None, :] + BLOCK
    ctx_mask = jnp.ones((BLOCK, P), dtype=bool)
    def blk(i):
        start = i * BLOCK
        qi = lax.dynamic_slice_in_dim(qg, start, BLOCK, axis=1)
        ki = lax.dynamic_slice_in_dim(kp, start, 3 * BLOCK, axis=1)
        vi = lax.dynamic_slice_in_dim(vp, start, 3 * BLOCK, axis=1)
        kpos = start - BLOCK + jnp.arange(3 * BLOCK)
        band = (jnp.abs(off) <= WINDOW) & ((kpos >= 0) & (kpos < L))[None, :]
        mask = jnp.concatenate([band, ctx_mask], axis=1)
        return sink_attend(qi, jnp.concatenate([ki, ck], axis=1), jnp.concatenate([vi, cv], axis=1), snk, mask)
    out = lax.map(blk, jnp.arange(n))
    return jnp.moveaxis(out, 0, 1).reshape(B_, L, ATT_Q_HEADS * HEAD_DIM)


def s5_scan(u, a_re, a_im, log_dt, b_re, b_im, c_re, c_im, h0):
    f32 = jnp.float32
    B_, L, _ = u.shape
    lam = lax.complex(a_re.astype(f32), a_im.astype(f32))
    dt = jnp.exp(log_dt.astype(f32))[:, None]
    a_bar = jnp.exp(lam * dt)
    b_bar = ((a_bar - 1.0) / lam)[..., None] * lax.complex(b_re.astype(f32), b_im.astype(f32))
    ug = u.reshape(B_, L, SSM_GROUPS, SSM_GROUP).astype(jnp.complex64)
    bu = jnp.einsum('blgc,gpc->blgp', ug, b_bar)
    a_seq = jnp.broadcast_to(a_bar, (1, L, SSM_GROUPS, SSM_STATE))
    h = linear_scan(a_seq, bu, h0)
    y = jnp.einsum('blgp,gcp->blgc', h, lax.complex(c_re.astype(f32), c_im.astype(f32))).real
    return y.reshape(B_, L, SSM_WIDTH), h[:, -1]


def s5_mixer(u, p, s0):
    f32 = jnp.float32
    uf = u.astype(f32)
    outs, finals = [], []
    for d in range(2):
        ud = uf if d == 0 else jnp.flip(uf, 1)
        y, hl = s5_scan(ud, p['ssm_a_re'][d], p['ssm_a_im'][d], p['ssm_log_dt'][d], p['ssm_b_re'][d],
                        p['ssm_b_im'][d], p['ssm_c_re'][d], p['ssm_c_im'][d], s0[:, d])
        finals.append(hl)
        outs.append(y if d == 0 else jnp.flip(y, 1))
    y = jax.nn.gelu(outs[0] + outs[1] + p['ssm_d'].astype(f32) * uf)
    out = y * jax.nn.sigmoid(y @ p['ssm_w_glu'].astype(f32) + p['ssm_b_glu'].astype(f32))
    return out, jnp.stack(finals, axis=1)


def token_mixers(h, p, ctx_cache):
    f32 = jnp.float32
    B_, L, _ = h.shape
    is_ctx = ctx_cache is None
    offs = [int(o) for o in np.cumsum(IN_SIZES)[:-1]]
    rq, rk, rv, rg, lx, lgate, aq, ak, av, su, gates = jnp.split(h @ p['w_in'], offs, axis=-1)
    if is_ctx:
        s_ret0 = jnp.zeros((B_, 2, RET_HEADS, RET_DK, RET_DV), f32)
        s_lru0 = jnp.zeros((B_, 2, LRU_WIDTH), f32)
        s_ssm0 = jnp.zeros((B_, 2, SSM_GROUPS, SSM_STATE), jnp.complex64)
    else:
        ck, cv, s_ret0, s_lru0, s_re0, s_im0 = ctx_cache
        s_ret0 = s_ret0.astype(f32)
        s_lru0 = s_lru0.astype(f32)
        s_ssm0 = lax.complex(s_re0.astype(f32), s_im0.astype(f32))
    o_ret, s_ret = retention(rq, rk, rv, rg, p['ret_decay_logit'], p['ret_gn'], s_ret0)
    o_lru, s_lru = rglru_mixer(lx, lgate, p, s_lru0)
    q = rms_norm(aq.reshape(B_, L, ATT_Q_HEADS, HEAD_DIM), p['att_q_norm'])
    k = rms_norm(ak.reshape(B_, L, ATT_KV_HEADS, HEAD_DIM), p['att_k_norm'])
    v = av.reshape(B_, L, ATT_KV_HEADS, HEAD_DIM)
    if is_ctx:
        o_att = context_attention(q, k, v, p['att_sink'])
    else:
        o_att = latent_attention(axial_rope(q), axial_rope(k), v, ck.astype(k.dtype), cv.astype(v.dtype), p['att_sink'])
    o_ssm, s_ssm = s5_mixer(su, p, s_ssm0)
    g = gates.reshape(B_, L, N_BRANCH, D_MODEL)
    merged = 0.0
    for n_b, o_b in enumerate((o_ret, o_lru, o_att, o_ssm)):
        merged = merged + jax.nn.sigmoid(g[:, :, n_b]) * (o_b.astype(h.dtype) @ p['w_br'][n_b])
    out = merged @ p['w_out']
    if is_ctx:
        return out, (k, v, s_ret, s_lru, s_ssm.real, s_ssm.imag)
    return out, None


def trunk_layer(x, cond, p, ctx_cache):
    mod = jax.nn.silu(cond) @ p['w_mod'] + p['b_mod']
    sh1, sc1, g1, sh2, sc2, g2 = jnp.split(mod[:, None, :], 6, axis=-1)
    h = rms_norm(x, p['norm1']) * (1.0 + sc1) + sh1
    out, st = token_mixers(h, p, ctx_cache)
    x = x + g1 * out
    h = rms_norm(x, p['norm2']) * (1.0 + sc2) + sh2
    f = jnp.square(jax.nn.relu(h @ p['w_ff1'])) @ p['w_ff2']
    return x + g2 * f, st


def setup_inputs(seed: int = 0) -> dict:
    key = jax.random.key(seed)
    keys = iter(jax.random.split(key, 64))
    f32 = jnp.float32
    def nrm(shape, scale):
        return jax.random.normal(next(keys), shape, f32) * scale
    def unif(shape, lo, hi):
        return jax.random.uniform(next(keys), shape, f32, lo, hi)
    gam = 1.0 - 2.0 ** (-5.0 - jnp.arange(RET_HEADS, dtype=f32))
    a0 = unif((DEPTH, 2, LRU_WIDTH), 0.9, 0.999)
    return {
        'x_prompt': nrm((BATCH, SEQ, D_MODEL), 1.0),
        'x_sample': nrm((DEC_BATCH, DEC_SEQ, D_MODEL), 1.0),
        'cache_attn_k': nrm((DEC_BATCH, DEPTH, PAST_LEN, ATT_KV_HEADS, HEAD_DIM), 1.0),
        'cache_attn_v': nrm((DEC_BATCH, DEPTH, PAST_LEN, ATT_KV_HEADS, HEAD_DIM), 1.0),
        'state_ret': nrm((DEC_BATCH, DEPTH, 2, RET_HEADS, RET_DK, RET_DV), 1.0),
        'state_lru': nrm((DEC_BATCH, DEPTH, 2, LRU_WIDTH), 0.5),
        'state_ssm_re': nrm((DEC_BATCH, DEPTH, 2, SSM_GROUPS, SSM_STATE), 0.1),
        'state_ssm_im': nrm((DEC_BATCH, DEPTH, 2, SSM_GROUPS, SSM_STATE), 0.1),
        'c': nrm((DEC_BATCH, D_MODEL), 1.0),
        'c_ctx': nrm((D_MODEL,), 1.0),
        'w_mod': nrm((DEPTH, D_MODEL, 6 * D_MODEL), 0.5 * D_MODEL ** -0.5),
        'b_mod': nrm((DEPTH, 6 * D_MODEL), 0.02),
        'norm1': 1.0 + nrm((DEPTH, D_MODEL), 0.02),
        'w_in': nrm((DEPTH, D_MODEL, IN_TOTAL), D_MODEL ** -0.5),
        'ret_decay_logit': jnp.log(gam / (1.0 - gam)) + nrm((DEPTH, 2, RET_HEADS), 0.1),
        'ret_gn': 1.0 + nrm((DEPTH, RET_WIDTH), 0.02),
        'lru_conv_w': nrm((DEPTH, CONV_W, LRU_WIDTH), CONV_W ** -0.5),
        'lru_conv_b': nrm((DEPTH, LRU_WIDTH), 0.02),
        'lru_wa': nrm((DEPTH, 2, LRU_BLOCKS, LRU_BLOCK_DIM, LRU_BLOCK_DIM), LRU_BLOCK_DIM ** -0.5),
        'lru_ba': nrm((DEPTH, 2, LRU_WIDTH), 0.02),
        'lru_wx': nrm((DEPTH, 2, LRU_BLOCKS, LRU_BLOCK_DIM, LRU_BLOCK_DIM), LRU_BLOCK_DIM ** -0.5),
        'lru_bx': nrm((DEPTH, 2, LRU_WIDTH), 0.02),
        'lru_lambda': jnp.log(a0 / (1.0 - a0)),
        'att_q_norm': 1.0 + nrm((DEPTH, HEAD_DIM), 0.02),
        'att_k_norm': 1.0 + nrm((DEPTH, HEAD_DIM), 0.02),
        'att_sink': nrm((DEPTH, ATT_Q_HEADS), 0.5),
        'ssm_a_re': -0.5 + nrm((DEPTH, 2, SSM_GROUPS, SSM_STATE), 0.01),
        'ssm_a_im': math.pi * jnp.arange(SSM_STATE, dtype=f32) + nrm((DEPTH, 2, SSM_GROUPS, SSM_STATE), 0.01),
        'ssm_log_dt': unif((DEPTH, 2, SSM_GROUPS), math.log(1e-3), math.log(1e-1)),
        'ssm_b_re': nrm((DEPTH, 2, SSM_GROUPS, SSM_STATE, SSM_GROUP), (2 * SSM_GROUP) ** -0.5),
        'ssm_b_im': nrm((DEPTH, 2, SSM_GROUPS, SSM_STATE, SSM_GROUP), (2 * SSM_GROUP) ** -0.5),
        'ssm_c_re': nrm((DEPTH, 2, SSM_GROUPS, SSM_GROUP, SSM_STATE), SSM_STATE ** -0.5),
        'ssm_c_im': nrm((DEPTH, 2, SSM_GROUPS, SSM_GROUP, SSM_STATE), SSM_STATE ** -0.5),
        'ssm_d': nrm((DEPTH, SSM_WIDTH), 1.0),
        'ssm_w_glu': nrm((DEPTH, SSM_WIDTH, SSM_WIDTH), SSM_WIDTH ** -0.5),
        'ssm_b_glu': nrm((DEPTH, SSM_WIDTH), 0.02),
        'w_br': nrm((DEPTH, N_BRANCH, BRANCH_WIDTH, D_MODEL), BRANCH_WIDTH ** -0.5),
        'w_out': nrm((DEPTH, D_MODEL, D_MODEL), D_MODEL ** -0.5),
        'norm2': 1.0 + nrm((DEPTH, D_MODEL), 0.02),
        'w_ff1': nrm((DEPTH, D_MODEL, D_FF), D_MODEL ** -0.5),
        'w_ff2': nrm((DEPTH, D_FF, D_MODEL), D_FF ** -0.5),
    }


def reference(x_prompt, x_sample, cache_attn_k, cache_attn_v, state_ret, state_lru, state_ssm_re, state_ssm_im,
              c, c_ctx, w_mod, b_mod, norm1, w_in, ret_decay_logit, ret_gn, lru_conv_w, lru_conv_b,
              lru_wa, lru_ba, lru_wx, lru_bx, lru_lambda, att_q_norm, att_k_norm, att_sink,
              ssm_a_re, ssm_a_im, ssm_log_dt, ssm_b_re, ssm_b_im, ssm_c_re, ssm_c_im, ssm_d,
              ssm_w_glu, ssm_b_glu, w_br, w_out, norm2, w_ff1, w_ff2):
    y_prompt, y_sample = x_prompt, x_sample
    cond_ctx = c_ctx[None, :]
    ks, vs, rets, lrus, sres, sims = [], [], [], [], [], []
    for l in range(DEPTH):
        p = {
            'w_mod': w_mod[l], 'b_mod': b_mod[l], 'norm1': norm1[l], 'w_in': w_in[l],
            'ret_decay_logit': ret_decay_logit[l], 'ret_gn': ret_gn[l],
            'lru_conv_w': lru_conv_w[l], 'lru_conv_b': lru_conv_b[l], 'lru_wa': lru_wa[l], 'lru_ba': lru_ba[l],
            'lru_wx': lru_wx[l], 'lru_bx': lru_bx[l], 'lru_lambda': lru_lambda[l],
            'att_q_norm': att_q_norm[l], 'att_k_norm': att_k_norm[l], 'att_sink': att_sink[l],
            'ssm_a_re': ssm_a_re[l], 'ssm_a_im': ssm_a_im[l], 'ssm_log_dt': ssm_log_dt[l],
            'ssm_b_re': ssm_b_re[l], 'ssm_b_im': ssm_b_im[l], 'ssm_c_re': ssm_c_re[l], 'ssm_c_im': ssm_c_im[l],
            'ssm_d': ssm_d[l], 'ssm_w_glu': ssm_w_glu[l], 'ssm_b_glu': ssm_b_glu[l],
            'w_br': w_br[l], 'w_out': w_out[l], 'norm2': norm2[l], 'w_ff1': w_ff1[l], 'w_ff2': w_ff2[l],
        }
        y_prompt, st = trunk_layer(y_prompt, cond_ctx, p, None)
        ks.append(st[0]); vs.append(st[1]); rets.append(st[2]); lrus.append(st[3]); sres.append(st[4]); sims.append(st[5])
        cache_l = (cache_attn_k[:, l], cache_attn_v[:, l], state_ret[:, l], state_lru[:, l],
                   state_ssm_re[:, l], state_ssm_im[:, l])
        y_sample, _ = trunk_layer(y_sample, c, p, cache_l)
    new_attn_k = jnp.stack(ks, axis=1)
    new_attn_v = jnp.stack(vs, axis=1)
    new_ret = jnp.stack(rets, axis=1)
    new_lru = jnp.stack(lrus, axis=1)
    new_ssm_re = jnp.stack(sres, axis=1)
    new_ssm_im = jnp.stack(sims, axis=1)
    return (y_prompt, y_sample, new_attn_k, new_attn_v, new_ret, new_lru, new_ssm_re, new_ssm_im)
```

```python
import contextlib
import math
import numpy as np
import concourse.bass as bass
import concourse.mybir as mybir
from concourse.bass_utils import run_bass_kernel_spmd

F32 = mybir.dt.float32
BF16 = mybir.dt.bfloat16
ALU = mybir.AluOpType
AF = mybir.ActivationFunctionType

SEM_CAP = 30000
N_DMA_SEMS = 150

DEPTH = 4
D = 2048
KC = 16
NPT = 1024
NST = 2048
NTOK = NPT + NST
TT = 512
NT = NTOK // TT
MIXC = 36
GATE0 = 4608
EPS = 1e-6
MAGIC = 12582912.0
TWO_PI = 2.0 * math.pi
SEQS = [(0, 256, True, 0), (256, 256, True, 1), (512, 256, True, 2), (768, 256, True, 3), (1024, 2048, False, 0)]


class Dep:
    __slots__ = ("w", "r", "dsem")

    def __init__(self):
        self.w = {}
        self.r = {}
        self.dsem = None


class T:
    __slots__ = ("ap", "dep")

    def __init__(self, ap, dep):
        self.ap = ap
        self.dep = dep

    def __getitem__(self, k):
        return T(self.ap[k], self.dep)

    def rr(self, s, **kw):
        return T(self.ap.rearrange(s, **kw), self.dep)

    def bc(self, shape):
        return T(self.ap.to_broadcast(list(shape)), self.dep)

    def rev(self):
        nd = len(self.ap.shape)
        idx = tuple([slice(None)] * (nd - 1) + [slice(None, None, -1)])
        return T(self.ap[idx], self.dep)


class KB:
    def __init__(self, nc, stack):
        self.nc = nc
        self.stack = stack
        self.eng = {"pe": nc.tensor, "act": nc.scalar, "dve": nc.vector, "pool": nc.gpsimd, "sp": nc.sync}
        self.sems = {}
        self.nsem = 0
        self.cur = {}
        self.epoch = {e: 0 for e in self.eng}
        self.seen = {e: {} for e in self.eng}
        self.deps = []
        self.ninst = {e: 0 for e in self.eng}
        self.dma_pool = [("d", i) for i in range(N_DMA_SEMS)]
        self.dma_cnt = {k: 0 for k in self.dma_pool}
        self.dma_free = list(self.dma_pool)
        for e in self.eng:
            self._new_engine_sem(e)

    def _sem(self, key):
        if key not in self.sems:
            self.sems[key] = self.stack.enter_context(self.nc.semaphore(f"s{self.nsem}"))
            self.nsem += 1
        return self.sems[key]

    def _new_engine_sem(self, e):
        key = ("e", e, self.epoch[e])
        self._sem(key)
        self.cur[e] = [key, 0]

    def newdep(self):
        d = Dep()
        self.deps.append(d)
        return d

    def sb(self, stack, name, shape, dtype):
        self.nalloc = getattr(self, "nalloc", 0) + 1
        name = f"{name}_{self.nalloc}"
        t = stack.enter_context(self.nc.sbuf_tensor(name, list(shape), dtype))
        return T(t.ap(), self.newdep())

    def ps(self, stack, name, shape, dtype):
        t = stack.enter_context(self.nc.psum_tensor(name, list(shape), dtype))
        return T(t.ap(), self.newdep())

    def dram(self, name, shape, dtype, kind="Internal"):
        t = self.nc.dram_tensor(name, list(shape), dtype, kind=kind)
        return T(t.ap(), self.newdep())

    def sub(self, t):
        return T(t.ap, self.newdep())

    def _wait(self, e, evs):
        seen = self.seen[e]
        need = {}
        for k, v in evs:
            if seen.get(k, 0) < v and need.get(k, 0) < v:
                need[k] = v
        for k, v in need.items():
            seen[k] = v
            if k[0] == "e" and k[1] == e and e == "pe":
                continue
            self.eng[e].wait_ge(self._sem(k), v)

    def _collect(self, e, reads, writes, compute):
        evs = []
        for t in reads:
            evs.extend(t.dep.w.items())
        for t in writes:
            d = t.dep
            for k, v in d.w.items():
                if compute and k[0] == "e" and k[1] == e:
                    continue
                evs.append((k, v))
            for k, v in d.r.items():
                if compute and k[0] == "e" and k[1] == e:
                    continue
                evs.append((k, v))
        return evs

    def _record(self, ev, reads, writes):
        k, v = ev
        for t in reads:
            t.dep.r[k] = v
        for t in writes:
            t.dep.w[k] = v

    def op(self, e, fn, reads=(), writes=()):
        self._wait(e, self._collect(e, reads, writes, True))
        ins = fn()
        if self.cur[e][1] >= SEM_CAP:
            self.epoch[e] += 1
            self._new_engine_sem(e)
        key = self.cur[e][0]
        self.cur[e][1] += 1
        ins.then_inc(self._sem(key), 1)
        self._record((key, self.cur[e][1]), reads, writes)
        self.ninst[e] += 1
        return ins

    def dma(self, e, out, in_, sem_t=None, **kw):
        st = (sem_t or out).dep
        if st.dsem is None or self.dma_cnt[st.dsem] + 16 > SEM_CAP:
            st.dsem = self.dma_free.pop()
        self._wait(e, self._collect(e, [in_], [out], False))
        ins = self.eng[e].dma_start(out=out.ap, in_=in_.ap, **kw)
        self.dma_cnt[st.dsem] += 16
        ins.then_inc(self._sem(st.dsem), 16)
        self._record((st.dsem, self.dma_cnt[st.dsem]), [in_], [out])
        self.ninst[e] += 1
        return ins

    def barrier(self):
        evs = {}
        for d in self.deps:
            for dd in (d.w, d.r):
                for k, v in dd.items():
                    if evs.get(k, 0) < v:
                        evs[k] = v
        for e in self.eng:
            self._wait(e, list(evs.items()))
        keep = []
        for d in self.deps:
            d.w.clear()
            d.r.clear()
            d.dsem = None
        self.dma_free = [k for k in self.dma_pool if self.dma_cnt[k] + 64 <= SEM_CAP]

    def _e(self, e):
        return self.eng[e]

    def tt(self, e, out, a, b, op):
        return self.op(e, lambda: self._e(e).tensor_tensor(out=out.ap, in0=a.ap, in1=b.ap, op=op), [a, b], [out])

    def ts(self, e, out, a, s1, op0, s2=None, op1=None):
        rd = [a] + [s for s in (s1, s2) if isinstance(s, T)]
        v1 = s1.ap if isinstance(s1, T) else s1
        v2 = s2.ap if isinstance(s2, T) else s2
        if op1 is None:
            return self.op(e, lambda: self._e(e).tensor_scalar(out=out.ap, in0=a.ap, scalar1=v1, scalar2=None, op0=op0), rd, [out])
        return self.op(e, lambda: self._e(e).tensor_scalar(out=out.ap, in0=a.ap, scalar1=v1, scalar2=v2, op0=op0, op1=op1), rd, [out])

    def stt(self, e, out, a, s, b, op0, op1):
        rd = [a, b] + ([s] if isinstance(s, T) else [])
        sv = s.ap if isinstance(s, T) else s
        return self.op(e, lambda: self._e(e).scalar_tensor_tensor(out=out.ap, in0=a.ap, scalar=sv, in1=b.ap, op0=op0, op1=op1), rd, [out])

    def act(self, out, a, func, scale=None, bias=None):
        rd = [a] + [s for s in (scale, bias) if isinstance(s, T)]
        kw = {}
        if scale is not None:
            kw["scale"] = scale.ap if isinstance(scale, T) else scale
        if bias is not None:
            kw["bias"] = bias.ap if isinstance(bias, T) else bias
        return self.op("act", lambda: self.nc.scalar.activation(out=out.ap, in_=a.ap, func=func, **kw), rd, [out])

    def copy(self, e, out, a):
        if e == "act":
            return self.op("act", lambda: self.nc.scalar.copy(out=out.ap, in_=a.ap), [a], [out])
        return self.op(e, lambda: self._e(e).tensor_copy(out=out.ap, in_=a.ap), [a], [out])

    def memset(self, e, out, val):
        return self.op(e, lambda: self._e(e).memset(out.ap, val), [], [out])

    def recip(self, out, a):
        return self.op("dve", lambda: self.nc.vector.reciprocal(out=out.ap, in_=a.ap), [a], [out])

    def scan(self, out, a, b, init):
        rd = [a, b] + ([init] if isinstance(init, T) else [])
        iv = init.ap if isinstance(init, T) else init
        return self.op("dve", lambda: self.nc.vector.tensor_tensor_scan(out=out.ap, data0=a.ap, data1=b.ap, initial=iv,
                                                                         op0=ALU.mult, op1=ALU.add), rd, [out])

    def mm(self, out, lhsT, rhs, start=True, stop=True):
        return self.op("pe", lambda: self.nc.tensor.matmul(out.ap, lhsT=lhsT.ap, rhs=rhs.ap, start=start, stop=stop),
                       [lhsT, rhs], [out])

    def tr(self, out, a, ident):
        return self.op("pe", lambda: self.nc.tensor.transpose(out.ap, a.ap, ident.ap), [a, ident], [out])


class WStream:
    def __init__(self, kb, stack, plan, srcs, nslots=4):
        self.kb = kb
        self.n = nslots
        self.slots = [kb.sb(stack, f"wring{i}", [128, 16, 512], BF16) for i in range(nslots)]
        self.plan = plan
        self.srcs = srcs
        self.issued = 0
        self.used = 0

    def _issue(self, idx):
        name, l, r0, nk, c0 = self.plan[idx]
        slot = self.slots[idx % self.n]
        src = self.srcs[name]
        ap = src.ap[l, r0:r0 + nk * 128, c0:c0 + 512].rearrange("(k p) c -> p k c", p=128)
        self.kb.dma("pool", slot[:, 0:nk, :], T(ap, src.dep))

    def next(self, desc):
        assert self.plan[self.used] == desc, (self.used, self.plan[self.used], desc)
        while self.issued < min(len(self.plan), self.used + self.n - 1):
            self._issue(self.issued)
            self.issued += 1
        t = self.slots[self.used % self.n]
        self.used += 1
        return t


def weight_plan(depth):
    plan = []
    for l in range(depth):
        for cb in range(24):
            plan.append(("w_mod", l, 0, 16, cb * 512))
        for ti in range(NT // 2):
            for cb in range(9):
                plan.append(("w_in", l, 0, 16, cb * 512))
        for ti in range(NT):
            for mg in range(4):
                for b in range(4):
                    plan.append(("w_in", l, 0, 16, GATE0 + b * 2048 + mg * 512))
                    plan.append(("w_br", l * 4 + b, 0, 4, mg * 512))
            for mg in range(4):
                plan.append(("w_out", l, 0, 16, mg * 512))
            for s in range(4):
                for fb in range(4):
                    plan.append(("w_ff1", l, 0, 16, s * 2048 + fb * 512))
                for mg in range(4):
                    plan.append(("w_ff2", l, s * 2048, 16, mg * 512))
    return plan


IN_SPECS = [
    ("xp", [NPT, D]), ("xs", [NST, D]), ("ck", [DEPTH, 256, 2, 128]), ("cv", [DEPTH, 256, 2, 128]),
    ("sret", [DEPTH, 2, 4, 128, 128]), ("slru", [DEPTH, 2, 512]), ("sre", [DEPTH, 2, 32, 64]), ("sim", [DEPTH, 2, 32, 64]),
    ("cvec", [2, D]),
    ("w_mod", [DEPTH, D, 6 * D]), ("b_mod", [DEPTH, 6 * D]), ("norm1", [DEPTH, D]), ("w_in", [DEPTH, D, 12800]),
    ("ret_decay_logit", [DEPTH, 2, 4]), ("ret_gn", [DEPTH, 512]), ("lru_conv_w", [DEPTH, 4, 512]), ("lru_conv_b", [DEPTH, 512]),
    ("lru_wa", [DEPTH, 2, 4, 128, 128]), ("lru_ba", [DEPTH, 2, 512]), ("lru_wx", [DEPTH, 2, 4, 128, 128]), ("lru_bx", [DEPTH, 2, 512]),
    ("lru_lambda", [DEPTH, 2, 512]), ("att_q_norm", [DEPTH, 128]), ("att_k_norm", [DEPTH, 128]), ("att_sink", [DEPTH, 4]),
    ("ssm_a_re", [DEPTH, 2, 32, 64]), ("ssm_a_im", [DEPTH, 2, 32, 64]), ("ssm_log_dt", [DEPTH, 2, 32]),
    ("ssm_b_re", [DEPTH, 2, 32, 64, 16]), ("ssm_b_im", [DEPTH, 2, 32, 64, 16]),
    ("ssm_c_re", [DEPTH, 2, 32, 16, 64]), ("ssm_c_im", [DEPTH, 2, 32, 16, 64]),
    ("ssm_d", [DEPTH, 512]), ("ssm_w_glu", [DEPTH, 512, 512]), ("ssm_b_glu", [DEPTH, 512]),
    ("w_br", [DEPTH * 4, 512, D]), ("w_out", [DEPTH, D, D]), ("norm2", [DEPTH, D]),
    ("w_ff1", [DEPTH, D, 4 * D]), ("w_ff2", [DEPTH, 4 * D, D]),
    ("k_ident", [128, 128]), ("k_tpos", [128, 2048]), ("k_ropec", [128, 2048]), ("k_ropes", [128, 2048]),
    ("k_perm", [128, 128]), ("k_pf", [128, 128]), ("k_pb", [128, 128]), ("k_mf", [128, 128]), ("k_mb", [128, 128]),
    ("k_tp1", [128, 2048]), ("k_tm", [128, 2048]), ("k_cols", [128, 4]),
]
OUT_SPECS = [
    ("yp", [NPT, D]), ("ys", [NST, D]), ("nk", [4, DEPTH, 256, 2, 128]), ("nv", [4, DEPTH, 256, 2, 128]),
    ("nret", [4, DEPTH, 2, 4, 128, 128]), ("nlru", [4, DEPTH, 2, 512]), ("nre", [4, DEPTH, 2, 32, 64]), ("nim", [4, DEPTH, 2, 32, 64]),
]


def host_consts():
    i = np.arange(128, dtype=np.float32)
    t = np.arange(2048, dtype=np.float32)
    c = {}
    c["k_ident"] = np.eye(128, dtype=np.float32)
    c["k_tpos"] = np.broadcast_to(t[None, :], (128, 2048)).copy()
    row = np.floor(t / 64.0)
    col = t - 64.0 * row
    freqs = (10000.0 ** (-np.arange(32, dtype=np.float32) / 32.0)).astype(np.float32)
    ang = np.concatenate([row[None, :].astype(np.float32) * freqs[:, None], col[None, :].astype(np.float32) * freqs[:, None]], axis=0)
    ang = ang.astype(np.float32)
    cos = np.cos(ang).astype(np.float32)
    sin = np.sin(ang).astype(np.float32)
    c["k_ropec"] = np.concatenate([cos, cos], axis=0)
    c["k_ropes"] = np.concatenate([-sin, sin], axis=0)
    perm = np.zeros((128, 128), np.float32)
    for m in range(128):
        perm[(m + 64) % 128, m] = 1.0
    c["k_perm"] = perm
    jj = i[:, None]
    ii = i[None, :]
    c["k_pf"] = np.maximum(ii - jj, 0.0).astype(np.float32)
    c["k_pb"] = np.maximum(jj - ii, 0.0).astype(np.float32)
    c["k_mf"] = (ii >= jj).astype(np.float32)
    c["k_mb"] = (jj >= ii).astype(np.float32)
    c["k_tp1"] = np.broadcast_to(((t % 128) + 1.0)[None, :], (128, 2048)).astype(np.float32).copy()
    c["k_tm"] = np.broadcast_to((128.0 - (t % 128))[None, :], (128, 2048)).astype(np.float32).copy()
    cols = np.zeros((128, 4), np.float32)
    cols[:, 0] = 127.0 - i
    cols[:, 1] = i
    c["k_cols"] = cols
    return c


class Prog:
    def __init__(self, kb, stack, depth, debug=False):
        self.kb = kb
        self.depth = depth
        nc = kb.nc
        self.I = {n: kb.dram(n, s, F32, kind="ExternalInput") for n, s in IN_SPECS}
        self.O = {n: kb.dram(n, s, F32, kind="ExternalOutput") for n, s in OUT_SPECS}
        sk = "ExternalOutput" if debug else "Internal"
        self.xT = kb.dram("s_xT", [KC, 128, NTOK], F32)
        self.hT = kb.dram("s_hT", [KC, 128, NTOK], BF16, kind=sk)
        self.debug = debug
        if debug:
            self.dmg = kb.dram("s_mg", [KC, 128, NTOK], BF16, kind=sk)
            self.dsg = kb.dram("s_sg", [128, NTOK], F32, kind=sk)
        self.mix = kb.dram("s_mix", [MIXC, 128, NTOK], F32, kind=sk)
        self.yb = kb.dram("s_yb", [4, 128, NTOK], BF16)
        self.oT = kb.dram("s_oT", [KC, 128, NTOK], BF16 if not debug else BF16, kind=sk)
        sb = lambda n, s, d: kb.sb(stack, n, s, d)
        self.ident = sb("ident", [128, 128], F32)
        self.ones_bf = sb("ones_bf", [128, 128], BF16)
        self.invf = sb("invf", [128, 128], F32)
        self.scond = sb("scond", [128, KC, 2], BF16)
        self.modT = sb("modT", [128, 96, 2], F32)
        self.ptab1 = sb("ptab1", [128, 128], F32)
        self.A1 = sb("A1", [128, KC, 2], F32)
        self.A2 = sb("A2", [128, KC, 2], F32)
        self.banks = [kb.ps(stack, f"bank{i}", [128, 512], F32) for i in range(7)]
        self.pbf = kb.ps(stack, "pbf", [128, 1024], BF16)
        self.bi = 0
        self.ws = WStream(kb, stack, weight_plan(depth), self.I, nslots=4)

    def bank(self):
        b = self.banks[self.bi % 7]
        self.bi += 1
        return b

    def dsl(self, t, *idx):
        return T(t.ap[idx], t.dep)

    def setup(self):
        kb = self.kb
        with contextlib.ExitStack() as st:
            kb.dma("sp", self.ident, self.I["k_ident"])
            kb.memset("dve", self.ones_bf, 1.0)
            kb.memset("dve", self.invf, 1.0 / 128.0)
            stg = kb.sb(st, "su_stg", [32, 128], F32)
            kb.dma("sp", stg, self.I["cvec"].rr("j (k p) -> (j k) p", p=128))
            pb = self.bank()
            kb.tr(pb[:, 0:32], stg, self.ident[0:32, 0:32])
            cs = kb.sb(st, "su_c", [128, 32], F32)
            kb.act(cs, pb[:, 0:32], AF.Silu)
            kb.copy("dve", self.scond.rr("p k j -> p j k"), cs.rr("p (j k) -> p j k", j=2))
            kb.barrier()

    def phase0(self):
        kb = self.kb
        with contextlib.ExitStack() as st:
            xin = kb.sb(st, "p0_x", [128, 4, D], F32)
            xo = kb.sb(st, "p0_o", [128, KC, TT], F32)
            for ti in range(NT):
                if ti < 2:
                    src = self.I["xp"][ti * 512:(ti + 1) * 512, :]
                else:
                    src = self.I["xs"][(ti - 2) * 512:(ti - 1) * 512, :]
                kb.dma("sp", xin, src.rr("(g p) f -> p g f", p=128))
                for k in range(KC):
                    pb = self.bank()
                    for g in range(4):
                        kb.tr(pb[:, g * 128:(g + 1) * 128], xin[:, g, k * 128:(k + 1) * 128], self.ident)
                    kb.copy("act" if k % 2 else "dve", xo[:, k, :], pb)
                kb.dma("sp", self.xT[:, :, ti * 512:(ti + 1) * 512].rr("k p t -> p k t"), xo)
            kb.barrier()

    def phaseM(self, l):
        kb = self.kb
        with contextlib.ExitStack() as st:
            stg = kb.sb(st, "m_stg", [128, 128], F32)
            kb.dma("sp", stg[0:16, :], self.I["norm1"][l].rr("(k p) -> k p", p=128))
            kb.dma("sp", stg[16:32, :], self.I["norm2"][l].rr("(k p) -> k p", p=128))
            kb.dma("sp", stg[32:128, :], self.I["b_mod"][l].rr("(k p) -> k p", p=128))
            pb = self.bank()
            kb.tr(pb[:, 0:128], stg, self.ident)
            kb.copy("dve", self.ptab1, pb[:, 0:128])
            for cb in range(24):
                wt = self.ws.next(("w_mod", l, 0, 16, cb * 512))
                for mm in range(4):
                    m = cb * 4 + mm
                    pb = self.bank()
                    for k in range(KC):
                        kb.mm(pb[:, 0:2], wt[:, k, mm * 128:(mm + 1) * 128], self.scond[:, k, :], start=(k == 0), stop=(k == KC - 1))
                    kb.ts("dve", self.modT[:, m, :], pb[:, 0:2], self.ptab1[:, 32 + m:33 + m], ALU.add)
            for (A, nb, scb) in ((self.A1, 0, 16), (self.A2, 16, 64)):
                for j in range(2):
                    kb.ts("dve", A[:, :, j], self.modT[:, scb:scb + 16, j], 1.0, ALU.add)
                    kb.tt("dve", A[:, :, j], A[:, :, j], self.ptab1[:, nb:nb + 16], ALU.mult)
            kb.barrier()

    def norm_mod(self, xt, hT, A, shb, j, sq, tmp, rstd):
        kb = self.kb
        pb = self.bank()
        for k in range(KC):
            kb.act(sq[k % 2], xt[:, k, :], AF.Square)
            kb.mm(pb, self.ones_bf, sq[k % 2], start=(k == 0), stop=(k == KC - 1))
        kb.ts("dve", rstd, pb, 1.0 / D, ALU.mult, EPS, ALU.add)
        kb.act(rstd, rstd, AF.Sqrt)
        kb.recip(rstd, rstd)
        for k in range(KC):
            kb.stt("dve", tmp[k % 2], xt[:, k, :], A[:, k, j:j + 1], rstd, ALU.mult, ALU.mult)
            kb.act(hT[:, k, :], tmp[k % 2], AF.Identity, bias=self.modT[:, shb + k, j:j + 1])

    def phaseA(self, l):
        kb = self.kb
        TA = 2 * TT
        with contextlib.ExitStack() as st:
            xt = kb.sb(st, "a_xt", [128, KC, TA], F32)
            hT = kb.sb(st, "a_hT", [128, KC, TA], BF16)
            sq = [kb.sb(st, f"a_sq{i}", [128, TT], BF16) for i in range(2)]
            tmp = [kb.sb(st, f"a_tmp{i}", [128, TT], F32) for i in range(2)]
            rstd = kb.sb(st, "a_rstd", [128, TT], F32)
            stage = [kb.sb(st, f"a_stg{i}", [128, TT], F32) for i in range(6)]
            for ti in range(NTOK // TA):
                j = 0 if ti < 1 else 1
                tok = slice(ti * TA, (ti + 1) * TA)
                kb.dma("sp", xt, self.xT[:, :, tok].rr("k p t -> p k t"))
                for hf in range(2):
                    hs = slice(hf * TT, (hf + 1) * TT)
                    self.norm_mod(xt[:, :, hs], hT[:, :, hs], self.A1, 0, j, sq, tmp, rstd)
                kb.dma("sp", self.hT[:, :, tok].rr("k p t -> p k t"), hT)
                cnt = 0
                for cb in range(9):
                    wt = self.ws.next(("w_in", l, 0, 16, cb * 512))
                    for mm in range(4):
                        m = cb * 4 + mm
                        for hf in range(2):
                            hs = slice(hf * TT, (hf + 1) * TT)
                            pb = self.bank()
                            for k in range(KC):
                                kb.mm(pb, wt[:, k, mm * 128:(mm + 1) * 128], hT[:, k, hs], start=(k == 0), stop=(k == KC - 1))
                            sg = stage[cnt % 6]
                            kb.copy("act" if cnt % 2 else "dve", sg, pb)
                            kb.dma("sp", self.mix[m, :, ti * TA + hf * TT: ti * TA + (hf + 1) * TT], sg)
                            cnt += 1
            kb.barrier()

    def phaseC(self, l, last):
        kb = self.kb
        ws = self.ws
        with contextlib.ExitStack() as st:
            xt = kb.sb(st, "c_xt", [128, KC, TT], F32)
            hT = kb.sb(st, "c_hT", [128, KC, TT], BF16)
            oT = kb.sb(st, "c_oT", [128, KC, TT], BF16)
            mg_t = kb.sb(st, "c_mg", [128, KC, TT], BF16)
            acc = [kb.sb(st, f"c_acc{i}", [128, TT], F32) for i in range(4)]
            sig = [kb.sb(st, f"c_sig{i}", [128, TT], F32) for i in range(2)]
            tmpm = [kb.sb(st, f"c_tm{i}", [128, TT], F32) for i in range(2)]
            rl = [kb.sb(st, f"c_rl{i}", [128, TT], BF16) for i in range(2)]
            sq = [kb.sb(st, f"c_sq{i}", [128, TT], BF16) for i in range(2)]
            tmp = [kb.sb(st, f"c_tmp{i}", [128, TT], F32) for i in range(2)]
            rstd = kb.sb(st, "c_rstd", [128, TT], F32)
            actbuf = [oT, mg_t]
            for ti in range(NT):
                j = 0 if ti < 2 else 1
                tok = slice(ti * TT, (ti + 1) * TT)
                kb.dma("sp", hT, self.hT[:, :, tok].rr("k p t -> p k t"))
                kb.dma("sp", oT, self.oT[:, :, tok].rr("k p t -> p k t"))
                kb.dma("sp", xt, self.xT[:, :, tok].rr("k p t -> p k t"))
                cnt = 0
                for mg in range(4):
                    for b in range(4):
                        wg = ws.next(("w_in", l, 0, 16, GATE0 + b * 2048 + mg * 512))
                        wb = ws.next(("w_br", l * 4 + b, 0, 4, mg * 512))
                        for mm in range(4):
                            pg = self.bank()
                            for k in range(KC):
                                kb.mm(pg, wg[:, k, mm * 128:(mm + 1) * 128], hT[:, k, :], start=(k == 0), stop=(k == KC - 1))
                            sg = sig[cnt % 2]
                            kb.act(sg, pg, AF.Sigmoid)
                            pbk = self.bank()
                            for kk in range(4):
                                kb.mm(pbk, wb[:, kk, mm * 128:(mm + 1) * 128], oT[:, 4 * b + kk, :], start=(kk == 0), stop=(kk == 3))
                            if b == 0:
                                kb.tt("dve", acc[mm], sg, pbk, ALU.mult)
                            else:
                                tm = tmpm[cnt % 2]
                                kb.tt("dve", tm, sg, pbk, ALU.mult)
                                kb.tt("dve", acc[mm], acc[mm], tm, ALU.add)
                            cnt += 1
                    for mm in range(4):
                        kb.copy("act", mg_t[:, mg * 4 + mm, :], acc[mm])
                if self.debug:
                    kb.dma("sp", self.dmg[:, :, tok].rr("k p t -> p k t"), mg_t)
                    kb.dma("sp", self.dsg[:, tok], sig[(cnt - 1) % 2])
                for mg in range(4):
                    wo = ws.next(("w_out", l, 0, 16, mg * 512))
                    for mm in range(4):
                        m = mg * 4 + mm
                        pb = self.bank()
                        for k in range(KC):
                            kb.mm(pb, wo[:, k, mm * 128:(mm + 1) * 128], mg_t[:, k, :], start=(k == 0), stop=(k == KC - 1))
                        kb.stt("dve", xt[:, m, :], pb, self.modT[:, 32 + m, j:j + 1], xt[:, m, :], ALU.mult, ALU.add)
                self.norm_mod(xt, hT, self.A2, 48, j, sq, tmp, rstd)
                for s in range(4):
                    ab = actbuf[s % 2]
                    for fb in range(4):
                        w1 = ws.next(("w_ff1", l, 0, 16, s * 2048 + fb * 512))
                        for mm in range(4):
                            f = fb * 4 + mm
                            pb = self.bank()
                            for k in range(KC):
                                kb.mm(pb, w1[:, k, mm * 128:(mm + 1) * 128], hT[:, k, :], start=(k == 0), stop=(k == KC - 1))
                            r = rl[f % 2]
                            kb.act(r, pb, AF.Relu)
                            kb.tt("dve", ab[:, f, :], r, r, ALU.mult)
                    for mg in range(4):
                        w2 = ws.next(("w_ff2", l, s * 2048, 16, mg * 512))
                        for mm in range(4):
                            m = mg * 4 + mm
                            pb = self.bank()
                            for f in range(KC):
                                kb.mm(pb, w2[:, f, mm * 128:(mm + 1) * 128], ab[:, f, :], start=(f == 0), stop=(f == KC - 1))
                            kb.stt("dve", xt[:, m, :], pb, self.modT[:, 80 + m, j:j + 1], xt[:, m, :], ALU.mult, ALU.add)
                if not last:
                    kb.dma("sp", self.xT[:, :, tok].rr("k p t -> p k t"), xt)
                else:
                    for g in range(4):
                        for kq in range(4):
                            pb = self.bank()
                            for kk in range(4):
                                kb.tr(pb[:, kk * 128:(kk + 1) * 128], xt[:, kq * 4 + kk, g * 128:(g + 1) * 128], self.ident)
                            kb.copy("act" if kq % 2 else "dve", acc[kq], pb)
                            if ti < 2:
                                dst = self.O["yp"][ti * 512 + g * 128: ti * 512 + (g + 1) * 128, kq * 512:(kq + 1) * 512]
                            else:
                                dst = self.O["ys"][(ti - 2) * 512 + g * 128:(ti - 2) * 512 + (g + 1) * 128, kq * 512:(kq + 1) * 512]
                            kb.dma("sp", dst, acc[kq])
            kb.barrier()

    def phaseB(self, l):
        kb = self.kb
        with contextlib.ExitStack() as st:
            stg = kb.sb(st, "b_stg", [128, 128], F32)
            I = self.I
            kb.memset("dve", stg, 0.0)
            kb.dma("sp", stg[0:4, :], I["ret_gn"][l].rr("(k p) -> k p", p=128))
            kb.dma("sp", stg[4:20, :], I["lru_conv_w"][l].rr("j (n p) -> (j n) p", p=128))
            kb.dma("sp", stg[20:24, :], I["lru_conv_b"][l].rr("(k p) -> k p", p=128))
            kb.dma("sp", stg[24:32, :], I["lru_ba"][l].rr("d (n p) -> (d n) p", p=128))
            kb.dma("sp", stg[32:40, :], I["lru_bx"][l].rr("d (n p) -> (d n) p", p=128))
            kb.dma("sp", stg[40:48, :], I["lru_lambda"][l].rr("d (n p) -> (d n) p", p=128))
            kb.dma("sp", stg[48:52, :], I["ssm_d"][l].rr("(k p) -> k p", p=128))
            kb.dma("sp", stg[52:56, :], I["ssm_b_glu"][l].rr("(k p) -> k p", p=128))
            kb.dma("sp", stg[56:57, :], I["att_q_norm"][l:l + 1, :])
            kb.dma("sp", stg[57:58, :], I["att_k_norm"][l:l + 1, :])
            kb.dma("sp", stg[58:66, :], I["slru"][l].rr("d (n p) -> (d n) p", p=128))
            pt2 = kb.sb(st, "b_pt2", [128, 128], F32)
            pb = self.bank()
            kb.tr(pb[:, 0:128], stg, self.ident)
            kb.copy("dve", pt2, pb[:, 0:128])
            bc = kb.sb(st, "b_bc", [128, 12], F32)
            kb.dma("sp", bc[:, 0:8], T(I["ret_decay_logit"].ap[l].rearrange("d h -> (d h)").partition_broadcast(128), I["ret_decay_logit"].dep))
            kb.dma("sp", bc[:, 8:12], T(I["att_sink"].ap[l].partition_broadcast(128), I["att_sink"].dep))
            identb = kb.sb(st, "b_identb", [128, 128], BF16)
            kb.copy("dve", identb, self.ident)
            kb.barrier()
            self.B_ret(l, pt2, bc, identb)
            self.B_lru(l, pt2)
            self.B_att(l, pt2, bc, identb)
            self.B_ssm(l, pt2, identb)

    def B_ret(self, l, pt2, bc, identb):
        kb = self.kb
        I = self.I
        LM = NST
        with contextlib.ExitStack() as st:
            sbf = lambda n, s, d: kb.sb(st, "r_" + n, s, d)
            pf, pbk_, mf, mb = [sbf(n, [128, 128], F32) for n in ("pf", "pb", "mf", "mb")]
            for t_, n_ in ((pf, "k_pf"), (pbk_, "k_pb"), (mf, "k_mf"), (mb, "k_mb")):
                kb.dma("sp", t_, I[n_])
            tp1 = sbf("tp1", [128, LM], F32)
            tm = sbf("tm", [128, LM], F32)
            cols = sbf("cols", [128, 4], F32)
            kb.dma("sp", tp1, I["k_tp1"])
            kb.dma("sp", tm, I["k_tm"])
            kb.dma("sp", cols, I["k_cols"])
            lg = sbf("lg", [128, 8], F32)
            kb.act(lg, bc[:, 0:8], AF.Exp, scale=-1.0)
            kb.ts("dve", lg, lg, 1.0, ALU.add)
            kb.act(lg, lg, AF.Ln)
            kb.ts("dve", lg, lg, -1.0, ALU.mult)
            lnsc = sbf("lnsc", [128, 1], F32)
            kb.memset("dve", lnsc, float(math.log(128.0 ** -0.5)))
            Dm = sbf("Dm", [128, 128], F32)
            e2 = sbf("e2", [128, 128], F32)
            wcol = sbf("wcol", [128, 4], F32)
            wfT = sbf("wfT", [128, LM], F32)
            wbT = sbf("wbT", [128, LM], F32)
            S0 = sbf("S0", [128, 2, 128], F32)
            S = sbf("S", [128, 128], F32)
            Hq, Hk, Hv, Hqf, Hqb, Hout = [sbf(n, [128, LM], BF16) for n in ("Hq", "Hk", "Hv", "Hqf", "Hqb", "Hout")]
            Ktok, Vtok, Kwf, Kwb, Sfb, Sbb = [sbf(n, [128, LM // 128, 128], BF16) for n in ("Kt", "Vt", "Kwf", "Kwb", "Sfb", "Sbb")]
            F0, F1, F2, F3 = [sbf(n, [128, LM], F32) for n in ("F0", "F1", "F2", "F3")]
            Pt = [sbf(f"P{i}", [128, 128], BF16) for i in range(2)]
            for h in range(4):
                lgf = lg[:, h:h + 1]
                lgb = lg[:, 4 + h:5 + h]
                kb.act(Dm, pf, AF.Exp, scale=lgf, bias=lnsc)
                kb.tt("dve", Dm, Dm, mf, ALU.mult)
                kb.act(e2, pbk_, AF.Exp, scale=lgb, bias=lnsc)
                kb.tt("dve", e2, e2, mb, ALU.mult)
                kb.tt("dve", Dm, Dm, e2, ALU.add)
                kb.act(wcol[:, 0:1], cols[:, 0:1], AF.Exp, scale=lgf)
                kb.act(wcol[:, 1:2], cols[:, 1:2], AF.Exp, scale=lgb)
                kb.act(wcol[:, 2:3], lgf, AF.Exp, scale=128.0)
                kb.act(wcol[:, 3:4], lgb, AF.Exp, scale=128.0)
                kb.act(wfT, tp1, AF.Exp, scale=lgf, bias=lnsc)
                kb.act(wbT, tm, AF.Exp, scale=lgb, bias=lnsc)
                kb.dma("sp", S0, I["sret"][l, :, h].rr("a d e -> d a e"))
                for (t0, L, is_ctx, si) in SEQS:
                    n = L // 128
                    tk = slice(t0, t0 + L)
                    kb.dma("pool", Hq[:, 0:L], self.mix[h, :, tk])
                    kb.dma("pool", Hk[:, 0:L], self.mix[4 + h, :, tk])
                    kb.dma("pool", Hv[:, 0:L], self.mix[8 + h, :, tk])
                    kb.dma("sp", F0[:, 0:L], self.mix[12 + h, :, tk])
                    for c in range(n):
                        cs_ = slice(c * 128, (c + 1) * 128)
                        kb.tr(self.pbf[:, 0:128], Hk[:, cs_], identb)
                        kb.copy("act", Ktok[:, c, :], self.pbf[:, 0:128])
                        kb.tr(self.pbf[:, 128:256], Hv[:, cs_], identb)
                        kb.copy("dve", Vtok[:, c, :], self.pbf[:, 128:256])
                    kb.ts("pool", Kwf[:, 0:n, :], Ktok[:, 0:n, :], wcol[:, 0:1], ALU.mult)
                    kb.ts("pool", Kwb[:, 0:n, :], Ktok[:, 0:n, :], wcol[:, 1:2], ALU.mult)
                    for d in range(2):
                        if is_ctx:
                            kb.memset("dve", S, 0.0)
                        else:
                            kb.copy("dve", S, S0[:, d, :])
                        Sx = Sfb if d == 0 else Sbb
                        Kw = Kwf if d == 0 else Kwb
                        order = range(n) if d == 0 else range(n - 1, -1, -1)
                        for c in order:
                            kb.copy("act", Sx[:, c, :], S)
                            pk = self.bank()
                            kb.mm(pk[:, 0:128], Kw[:, c, :], Vtok[:, c, :])
                            kb.stt("dve", S, S, wcol[:, 2 + d:3 + d], pk[:, 0:128], ALU.mult, ALU.add)
                        if is_ctx:
                            kb.dma("sp", self.O["nret"][si, l, d, h], S)
                    kb.tt("pool", Hqf[:, 0:L], Hq[:, 0:L], wfT[:, 0:L], ALU.mult)
                    kb.tt("pool", Hqb[:, 0:L], Hq[:, 0:L], wbT[:, 0:L], ALU.mult)
                    cb_ = min(512, L)
                    cnt = 0
                    for blk in range(L // cb_):
                        po = self.bank()
                        for cc in range(cb_ // 128):
                            c = blk * (cb_ // 128) + cc
                            cs_ = slice(c * 128, (c + 1) * 128)
                            psc = self.bank()
                            kb.mm(psc[:, 0:128], Hk[:, cs_], Hq[:, cs_])
                            P = Pt[cnt % 2]
                            cnt += 1
                            kb.tt("dve", P, psc[:, 0:128], Dm, ALU.mult)
                            oc = po[:, cc * 128:(cc + 1) * 128]
                            kb.mm(oc, Vtok[:, c, :], P, start=True, stop=False)
                            kb.mm(oc, Sfb[:, c, :], Hqf[:, cs_], start=False, stop=False)
                            kb.mm(oc, Sbb[:, c, :], Hqb[:, cs_], start=False, stop=True)
                        bs = slice(blk * cb_, (blk + 1) * cb_)
                        kb.copy("act", F1[:, bs], po[:, 0:cb_])
                        pm_ = self.bank()
                        kb.mm(pm_[:, 0:cb_], self.invf, F1[:, bs])
                        kb.tt("dve", F2[:, bs], F1[:, bs], pm_[:, 0:cb_], ALU.subtract)
                        kb.act(F3[:, bs], F2[:, bs], AF.Square)
                        pv = self.bank()
                        kb.mm(pv[:, 0:cb_], self.invf, F3[:, bs])
                        kb.ts("dve", F3[:, bs], pv[:, 0:cb_], EPS, ALU.add)
                        kb.act(F3[:, bs], F3[:, bs], AF.Sqrt)
                        kb.recip(F3[:, bs], F3[:, bs])
                        kb.tt("dve", F2[:, bs], F2[:, bs], F3[:, bs], ALU.mult)
                        kb.act(F0[:, bs], F0[:, bs], AF.Silu)
                        kb.stt("dve", Hout[:, bs], F2[:, bs], pt2[:, h:h + 1], F0[:, bs], ALU.mult, ALU.mult)
                    kb.dma("sp", self.oT[h, :, tk], Hout[:, 0:L])
            kb.barrier()

    def gelu_tanh(self, out, x, t1, e="dve"):
        kb = self.kb
        kb.act(t1, x, AF.Square)
        kb.ts(e, t1, t1, 0.044715, ALU.mult, 1.0, ALU.add)
        kb.tt(e, t1, t1, x, ALU.mult)
        kb.act(t1, t1, AF.Sigmoid, scale=1.5957691216057308)
        kb.tt(e, out, x, t1, ALU.mult)

    def B_lru(self, l, pt2):
        kb = self.kb
        I = self.I
        LM = NST
        with contextlib.ExitStack() as st:
            sbf = lambda n, s, d: kb.sb(st, "l_" + n, s, d)
            nls = sbf("nls", [128, 8], F32)
            kb.act(nls, pt2[:, 40:48], AF.Exp, scale=-1.0)
            kb.ts("dve", nls, nls, 1.0, ALU.add)
            kb.act(nls, nls, AF.Ln)
            kb.ts("dve", nls, nls, -8.0, ALU.mult)
            wa = sbf("wa", [128, 2, 128], BF16)
            wx = sbf("wx", [128, 2, 128], BF16)
            F = [sbf(f"F{i}", [128, LM], F32) for i in range(9)]
            H0 = sbf("H0", [128, LM], BF16)
            H1 = sbf("H1", [128, LM], BF16)
            fin = sbf("fin", [128, 2], F32)
            for nb in range(4):
                kb.dma("pool", wa, I["lru_wa"][l, :, nb].rr("a d e -> d a e"))
                kb.dma("pool", wx, I["lru_wx"][l, :, nb].rr("a d e -> d a e"))
                cw = lambda j: pt2[:, 4 + j * 4 + nb: 5 + j * 4 + nb]
                for (t0, L, is_ctx, si) in SEQS:
                    tk = slice(t0, t0 + L)
                    x, gate, xc = F[0][:, 0:L], F[1][:, 0:L], F[2][:, 0:L]
                    kb.dma("sp", x, self.mix[16 + nb, :, tk])
                    kb.dma("sp", gate, self.mix[20 + nb, :, tk])
                    kb.ts("dve", xc, x, cw(2), ALU.mult, pt2[:, 20 + nb:21 + nb], ALU.add)
                    kb.stt("dve", xc[:, 2:L], x[:, 0:L - 2], cw(0), xc[:, 2:L], ALU.mult, ALU.add)
                    kb.stt("dve", xc[:, 1:L], x[:, 0:L - 1], cw(1), xc[:, 1:L], ALU.mult, ALU.add)
                    kb.stt("dve", xc[:, 0:L - 1], x[:, 1:L], cw(3), xc[:, 0:L - 1], ALU.mult, ALU.add)
                    kb.copy("act", H0[:, 0:L], xc)
                    cb_ = min(512, L)
                    hs = [F[7][:, 0:L], F[8][:, 0:L]]
                    for d in range(2):
                        r, ig, a, sq_ = F[3][:, 0:L], F[4][:, 0:L], F[5][:, 0:L], F[6][:, 0:L]
                        for blk in range(L // cb_):
                            bs = slice(blk * cb_, (blk + 1) * cb_)
                            p1 = self.bank()
                            kb.mm(p1[:, 0:cb_], wa[:, d, :], H0[:, bs])
                            kb.act(r[:, bs], p1[:, 0:cb_], AF.Sigmoid, bias=pt2[:, 24 + d * 4 + nb:25 + d * 4 + nb])
                            p2 = self.bank()
                            kb.mm(p2[:, 0:cb_], wx[:, d, :], H0[:, bs])
                            kb.act(ig[:, bs], p2[:, 0:cb_], AF.Sigmoid, bias=pt2[:, 32 + d * 4 + nb:33 + d * 4 + nb])
                        kb.act(a, r, AF.Exp, scale=nls[:, d * 4 + nb:d * 4 + nb + 1])
                        kb.tt("dve", sq_, a, a, ALU.mult)
                        kb.ts("dve", sq_, sq_, -1.0, ALU.mult, 1.0, ALU.add)
                        kb.act(sq_, sq_, AF.Relu)
                        kb.act(sq_, sq_, AF.Sqrt)
                        kb.tt("dve", ig, ig, xc, ALU.mult)
                        kb.tt("dve", ig, ig, sq_, ALU.mult)
                        if is_ctx:
                            init = 0.0
                        else:
                            init = pt2[:, 58 + d * 4 + nb:59 + d * 4 + nb]
                        hd = hs[d]
                        if d == 0:
                            kb.scan(hd, a, ig, init)
                            kb.copy("act", fin[:, 0:1], hd[:, L - 1:L])
                        else:
                            kb.scan(hd.rev(), a.rev(), ig.rev(), init)
                            kb.copy("act", fin[:, 1:2], hd[:, 0:1])
                    if is_ctx:
                        kb.dma("sp", self.O["nlru"][si, l, :, nb * 128:(nb + 1) * 128].rr("d p -> p d"), fin, allow_slow_non_contiguous=True)
                    kb.tt("dve", hs[0], hs[0], hs[1], ALU.add)
                    self.gelu_tanh(F[4][:, 0:L], gate, F[3][:, 0:L], e="pool")
                    kb.tt("dve", H1[:, 0:L], F[4][:, 0:L], hs[0], ALU.mult)
                    kb.dma("sp", self.oT[4 + nb, :, tk], H1[:, 0:L])
            kb.barrier()

    def rms_feat(self, x, L, gain, w1):
        kb = self.kb
        cb_ = min(512, L)
        for blk in range(L // cb_):
            bs = slice(blk * cb_, (blk + 1) * cb_)
            kb.act(w1[:, bs], x[:, bs], AF.Square)
            pm_ = self.bank()
            kb.mm(pm_[:, 0:cb_], self.invf, w1[:, bs])
            kb.ts("dve", w1[:, bs], pm_[:, 0:cb_], EPS, ALU.add)
            kb.act(w1[:, bs], w1[:, bs], AF.Sqrt)
            kb.recip(w1[:, bs], w1[:, bs])
            kb.stt("dve", x[:, bs], x[:, bs], gain, w1[:, bs], ALU.mult, ALU.mult)

    def B_att(self, l, pt2, bc, identb):
        kb = self.kb
        I = self.I
        LM = NST
        SC = 128.0 ** -0.5
        with contextlib.ExitStack() as st:
            sbf = lambda n, s, d: kb.sb(st, "t_" + n, s, d)
            ropec = sbf("ropec", [128, LM], F32)
            ropes = sbf("ropes", [128, LM], F32)
            perm = sbf("perm", [128, 128], F32)
            mfb = sbf("mfb", [128, 128], BF16)
            mbb = sbf("mbb", [128, 128], BF16)
            kb.dma("sp", ropec, I["k_ropec"])
            kb.dma("sp", ropes, I["k_ropes"])
            kb.dma("sp", perm, I["k_perm"])
            kb.dma("pool", mfb, I["k_mf"])
            kb.dma("pool", mbb, I["k_mb"])
            esink = sbf("esink", [128, 4], F32)
            kb.act(esink, bc[:, 8:12], AF.Exp)
            Q = [sbf(f"Q{i}", [128, LM], F32) for i in range(2)]
            K = sbf("K", [128, LM], F32)
            V = sbf("V", [128, LM], F32)
            W1 = sbf("W1", [128, LM], F32)
            W2 = sbf("W2", [128, LM], F32)
            Hq = [sbf(f"Hq{i}", [128, LM], BF16) for i in range(2)]
            Hk = sbf("Hk", [128, LM], BF16)
            Ho = sbf("Ho", [128, LM], BF16)
            vtok = sbf("vtok", [128, LM // 128, 128], BF16)
            kst = sbf("kst", [128, 2, 128], F32)
            vst = sbf("vst", [128, 2, 128], F32)
            cks = sbf("cks", [128, 2, 128], F32)
            ckT = sbf("ckT", [128, 256], BF16)
            cvb = sbf("cvb", [128, 2, 128], BF16)
            PL = sbf("PL", [128, LM // 128, 384], BF16)
            PC = sbf("PC", [128, 2, LM], BF16)
            den = sbf("den", [128, 512], F32)
            for kvh in range(2):
                kb.dma("sp", cks, I["ck"][l, :, kvh, :].rr("(c t) d -> t c d", t=128))
                kb.dma("pool", cvb, I["cv"][l, :, kvh, :].rr("(c t) d -> t c d", t=128))
                for c in range(2):
                    pb = self.bank()
                    kb.tr(pb[:, 0:128], cks[:, c, :], self.ident)
                    kb.copy("act", ckT[:, c * 128:(c + 1) * 128], pb[:, 0:128])
                for (t0, L, is_ctx, si) in SEQS:
                    n = L // 128
                    tk = slice(t0, t0 + L)
                    cb_ = min(512, L)
                    nblk = L // cb_
                    kb.dma("sp", Q[0][:, 0:L], self.mix[24 + 2 * kvh, :, tk])
                    kb.dma("sp", Q[1][:, 0:L], self.mix[25 + 2 * kvh, :, tk])
                    kb.dma("sp", K[:, 0:L], self.mix[28 + kvh, :, tk])
                    kb.dma("sp", V[:, 0:L], self.mix[30 + kvh, :, tk])
                    self.rms_feat(Q[0], L, pt2[:, 56:57], W1)
                    self.rms_feat(Q[1], L, pt2[:, 56:57], W1)
                    self.rms_feat(K, L, pt2[:, 57:58], W1)
                    if is_ctx:
                        for c in range(n):
                            cs_ = slice(c * 128, (c + 1) * 128)
                            pb = self.bank()
                            kb.tr(pb[:, 0:128], K[:, cs_], self.ident)
                            kb.copy("act", kst[:, c, :], pb[:, 0:128])
                            pb2 = self.bank()
                            kb.tr(pb2[:, 0:128], V[:, cs_], self.ident)
                            kb.copy("dve", vst[:, c, :], pb2[:, 0:128])
                        kb.copy("act", vtok[:, 0:n, :], vst[:, 0:n, :])
                        kb.dma("sp", self.O["nk"][si, l, :, kvh, :].rr("(c t) d -> t c d", t=128), kst)
                        kb.dma("sp", self.O["nv"][si, l, :, kvh, :].rr("(c t) d -> t c d", t=128), vst)
                    else:
                        for X in (Q[0], Q[1], K):
                            for blk in range(nblk):
                                bs = slice(blk * cb_, (blk + 1) * cb_)
                                pp = self.bank()
                                kb.mm(pp[:, 0:cb_], perm, X[:, bs])
                                kb.tt("pool", W1[:, bs], X[:, bs], ropec[:, bs], ALU.mult)
                                kb.tt("dve", W2[:, bs], pp[:, 0:cb_], ropes[:, bs], ALU.mult)
                                kb.tt("pool", X[:, bs], W1[:, bs], W2[:, bs], ALU.add)
                        for c in range(n):
                            pb = self.bank()
                            kb.tr(pb[:, 0:128], V[:, c * 128:(c + 1) * 128], self.ident)
                            kb.copy("act" if c % 2 else "dve", vtok[:, c, :], pb[:, 0:128])
                    kb.copy("act", Hq[0][:, 0:L], Q[0][:, 0:L])
                    kb.copy("act", Hq[1][:, 0:L], Q[1][:, 0:L])
                    kb.copy("dve", Hk[:, 0:L], K[:, 0:L])
                    for hq in range(2):
                        qh = 2 * kvh + hq
                        q = Hq[hq]
                        es = esink[:, qh:qh + 1]
                        if is_ctx:
                            for c in range(2):
                                ps_ = self.bank()
                                kb.mm(ps_[:, 0:L], Hk[:, c * 128:(c + 1) * 128], q[:, 0:L])
                                kb.act(PC[:, c, 0:L], ps_[:, 0:L], AF.Exp, scale=SC)
                            po = self.bank()
                            pd = self.bank()
                            for c in range(2):
                                kb.mm(po[:, 0:L], vtok[:, c, :], PC[:, c, 0:L], start=(c == 0), stop=(c == 1))
                                kb.mm(pd[:, 0:L], self.ones_bf, PC[:, c, 0:L], start=(c == 0), stop=(c == 1))
                            kb.ts("dve", den[:, 0:L], pd[:, 0:L], es, ALU.add)
                            kb.recip(den[:, 0:L], den[:, 0:L])
                            kb.tt("dve", Ho[:, 0:L], po[:, 0:L], den[:, 0:L], ALU.mult)
                        else:
                            for jc in range(n):
                                b0 = max(jc - 1, 0)
                                b1 = min(jc + 1, n - 1)
                                ncol = (b1 - b0 + 1) * 128
                                s0 = (b0 - (jc - 1)) * 128
                                ps_ = self.bank()
                                kb.mm(ps_[:, 0:ncol], Hk[:, jc * 128:(jc + 1) * 128], q[:, b0 * 128:(b1 + 1) * 128])
                                kb.act(PL[:, jc, s0:s0 + ncol], ps_[:, 0:ncol], AF.Exp, scale=SC)
                                if jc >= 1:
                                    kb.tt("dve", PL[:, jc, 0:128], PL[:, jc, 0:128], mfb, ALU.mult)
                                if jc <= n - 2:
                                    kb.tt("dve", PL[:, jc, 256:384], PL[:, jc, 256:384], mbb, ALU.mult)
                            for c in range(2):
                                for blk in range(nblk):
                                    bs = slice(blk * cb_, (blk + 1) * cb_)
                                    ps_ = self.bank()
                                    kb.mm(ps_, ckT[:, c * 128:(c + 1) * 128], q[:, bs])
                                    kb.act(PC[:, c, bs], ps_, AF.Exp, scale=SC)
                            for blk in range(nblk):
                                bs = slice(blk * cb_, (blk + 1) * cb_)
                                po = self.bank()
                                pd = self.bank()
                                for ii in range(4):
                                    i = blk * 4 + ii
                                    terms = []
                                    for jc in (i - 1, i, i + 1):
                                        if 0 <= jc < n:
                                            sl = (i - (jc - 1)) * 128
                                            terms.append((vtok[:, jc, :], PL[:, jc, sl:sl + 128]))
                                    for c in range(2):
                                        terms.append((cvb[:, c, :], PC[:, c, i * 128:(i + 1) * 128]))
                                    cs_ = slice(ii * 128, (ii + 1) * 128)
                                    for idx, (lh, rh) in enumerate(terms):
                                        kb.mm(po[:, cs_], lh, rh, start=(idx == 0), stop=(idx == len(terms) - 1))
                                        kb.mm(pd[:, cs_], self.ones_bf, rh, start=(idx == 0), stop=(idx == len(terms) - 1))
                                kb.ts("dve", den, pd, es, ALU.add)
                                kb.recip(den, den)
                                kb.tt("dve", Ho[:, bs], po, den, ALU.mult)
                        kb.dma("sp", self.oT[8 + qh, :, tk], Ho[:, 0:L])
            kb.barrier()

    def B_ssm(self, l, pt2, identb):
        kb = self.kb
        I = self.I
        LM = NST
        with contextlib.ExitStack() as st:
            sbf = lambda n, s, d: kb.sb(st, "s_" + n, s, d)
            stg = sbf("stg", [128, 128], F32)
            kb.dma("sp", stg[0:32, :], I["ssm_a_re"][l].rr("d (q g) p -> (d q) (g p)", g=2))
            kb.dma("sp", stg[32:64, :], I["ssm_a_im"][l].rr("d (q g) p -> (d q) (g p)", g=2))
            kb.dma("sp", stg[64:96, :], I["sre"][l].rr("d (q g) p -> (d q) (g p)", g=2))
            kb.dma("sp", stg[96:128, :], I["sim"][l].rr("d (q g) p -> (d q) (g p)", g=2))
            ct = sbf("ct", [128, 128], F32)
            pb = self.bank()
            kb.tr(pb[:, 0:128], stg, self.ident)
            kb.copy("dve", ct, pb[:, 0:128])
            dt = sbf("dt", [128, 32], F32)
            ld = I["ssm_log_dt"]
            for d in range(2):
                for g in range(2):
                    kb.dma("sp", dt[g * 64:(g + 1) * 64, d * 16:(d + 1) * 16],
                           T(ld.ap[l, d, g::2].partition_broadcast(64), ld.dep), allow_slow_non_contiguous=True)
            kb.act(dt, dt, AF.Exp)
            are, aim, h0r, h0i = ct[:, 0:32], ct[:, 32:64], ct[:, 64:96], ct[:, 96:128]
            tb = lambda n: sbf(n, [128, 32], F32)
            mag, fr, sn, cs, t1, t2, t3, wre, wim, nwim, gir, gii = [tb(n) for n in
                ("mag", "fr", "sn", "cs", "t1", "t2", "t3", "wre", "wim", "nwim", "gir", "gii")]
            hpi = sbf("hpi", [128, 1], F32)
            kb.memset("dve", hpi, float(math.pi / 2))
            kb.tt("dve", mag, are, dt, ALU.mult)
            kb.act(mag, mag, AF.Exp)
            kb.tt("dve", fr, aim, dt, ALU.mult)
            kb.ts("dve", fr, fr, float(1.0 / TWO_PI), ALU.mult)
            kb.ts("dve", t1, fr, MAGIC, ALU.add, MAGIC, ALU.subtract)
            kb.tt("dve", fr, fr, t1, ALU.subtract)
            kb.act(sn, fr, AF.Sin, scale=float(TWO_PI))
            kb.act(t1, fr, AF.Abs)
            kb.act(cs, t1, AF.Sin, scale=float(-TWO_PI), bias=hpi)
            kb.tt("dve", t1, mag, cs, ALU.mult)
            kb.ts("dve", t1, t1, -1.0, ALU.add)
            kb.tt("dve", t2, mag, sn, ALU.mult)
            kb.tt("dve", t3, are, are, ALU.mult)
            kb.tt("dve", wre, aim, aim, ALU.mult)
            kb.tt("dve", t3, t3, wre, ALU.add)
            kb.recip(t3, t3)
            kb.tt("dve", wre, t1, are, ALU.mult)
            kb.tt("dve", wim, t2, aim, ALU.mult)
            kb.tt("dve", wre, wre, wim, ALU.add)
            kb.tt("dve", wre, wre, t3, ALU.mult)
            kb.tt("dve", wim, t2, are, ALU.mult)
            kb.tt("dve", nwim, t1, aim, ALU.mult)
            kb.tt("dve", wim, wim, nwim, ALU.subtract)
            kb.tt("dve", wim, wim, t3, ALU.mult)
            kb.ts("dve", nwim, wim, -1.0, ALU.mult)
            kb.tt("dve", gir, cs, h0r, ALU.mult)
            kb.tt("dve", t1, sn, h0i, ALU.mult)
            kb.tt("dve", gir, gir, t1, ALU.subtract)
            kb.tt("dve", gii, sn, h0r, ALU.mult)
            kb.tt("dve", t1, cs, h0i, ALU.mult)
            kb.tt("dve", gii, gii, t1, ALU.add)
            Bre = sbf("Bre", [128, 32, 16], F32)
            Bim = sbf("Bim", [128, 32, 16], F32)
            for d in range(2):
                kb.dma("sp", Bre[:, d * 16:(d + 1) * 16, :], I["ssm_b_re"][l, d].rr("(q g) p c -> (g p) q c", g=2))
                kb.dma("sp", Bim[:, d * 16:(d + 1) * 16, :], I["ssm_b_im"][l, d].rr("(q g) p c -> (g p) q c", g=2))
            Zre = [sbf(f"Zre{r}", [128, 128], F32) for r in range(4)]
            Zim = [sbf(f"Zim{r}", [128, 128], F32) for r in range(4)]
            LCre = [sbf(f"LCre{r}", [128, 128], BF16) for r in range(4)]
            LCim = [sbf(f"LCim{r}", [128, 128], BF16) for r in range(4)]
            for r in range(4):
                kb.memset("dve", Zre[r], 0.0)
                kb.memset("dve", Zim[r], 0.0)
                kb.memset("dve", LCre[r], 0.0)
                kb.memset("dve", LCim[r], 0.0)
            Zc_re = sbf("Zc_re", [32, 128], F32)
            Zc_im = sbf("Zc_im", [32, 128], F32)
            kb.memset("dve", Zc_re, 0.0)
            kb.memset("dve", Zc_im, 0.0)
            LBre = sbf("LBre", [128, 128], BF16)
            LBim = sbf("LBim", [128, 128], BF16)
            tpos = sbf("tpos", [128, LM], F32)
            kb.dma("sp", tpos, I["k_tpos"])
            CS = sbf("CS", [128, LM], F32)
            SN = sbf("SN", [128, LM], F32)
            G1, G2, HR, HI, PR, PI = [sbf(n, [128, LM], F32) for n in ("G1", "G2", "HR", "HI", "PR", "PI")]
            T1, T2 = PR, PI
            hrb = sbf("hrb", [128, LM], BF16)
            hib = sbf("hib", [128, LM], BF16)
            SU = sbf("SU", [128, NTOK], BF16)
            Y = sbf("Y", [128, NTOK], F32)
            ybt = [sbf(f"ybt{i}", [128, 4, TT], BF16) for i in range(2)]
            ybs = sbf("ybs", [128, LM], BF16)
            FIN = sbf("FIN", [128, 4, 2, 32], F32)
            fcol = sbf("fcol", [128, 2], F32)
            wglu = sbf("wglu", [128, 4, 512], BF16)
            kb.dma("pool", wglu, I["ssm_w_glu"][l].rr("(k p) c -> p k c", p=128))
            for j in range(4):
                kb.dma("pool", SU, self.mix[32 + j, :, :])
                first = True
                for d in range(2):
                    for r in range(4):
                        q = 4 * j + r
                        dq = d * 16 + q
                        for g in range(2):
                            rows = slice(g * 64, (g + 1) * 64)
                            cols = slice(r * 32 + g * 16, r * 32 + g * 16 + 16)
                            kb.ts("dve", Zre[r][rows, cols], Bre[rows, dq, :], wre[rows, dq:dq + 1], ALU.mult)
                            kb.stt("dve", Zre[r][rows, cols], Bim[rows, dq, :], nwim[rows, dq:dq + 1], Zre[r][rows, cols], ALU.mult, ALU.add)
                            kb.ts("dve", Zim[r][rows, cols], Bim[rows, dq, :], wre[rows, dq:dq + 1], ALU.mult)
                            kb.stt("dve", Zim[r][rows, cols], Bre[rows, dq, :], wim[rows, dq:dq + 1], Zim[r][rows, cols], ALU.mult, ALU.add)
                        pz = self.bank()
                        kb.tr(pz[:, 0:128], Zre[r], self.ident)
                        kb.copy("act", LBre, pz[:, 0:128])
                        pz = self.bank()
                        kb.tr(pz[:, 0:128], Zim[r], self.ident)
                        kb.copy("act", LBim, pz[:, 0:128])
                        for g in range(2):
                            kb.dma("sp", Zc_re[g * 16:(g + 1) * 16, g * 64:(g + 1) * 64], I["ssm_c_re"][l, d, 2 * q + g])
                            kb.dma("sp", Zc_im[g * 16:(g + 1) * 16, g * 64:(g + 1) * 64], I["ssm_c_im"][l, d, 2 * q + g])
                        pz = self.bank()
                        kb.tr(pz[:, 0:32], Zc_re, self.ident[0:32, 0:32])
                        kb.copy("act", LCre[r][:, r * 32:(r + 1) * 32], pz[:, 0:32])
                        pz = self.bank()
                        kb.tr(pz[:, 0:32], Zc_im, self.ident[0:32, 0:32])
                        kb.ts("dve", LCim[r][:, r * 32:(r + 1) * 32], pz[:, 0:32], -1.0, ALU.mult)
                        kb.ts("dve", G1, tpos, fr[:, dq:dq + 1], ALU.mult)
                        kb.ts("dve", G2, G1, MAGIC, ALU.add, MAGIC, ALU.subtract)
                        kb.tt("dve", G1, G1, G2, ALU.subtract)
                        kb.act(SN, G1, AF.Sin, scale=float(TWO_PI))
                        kb.act(G2, G1, AF.Abs)
                        kb.act(CS, G2, AF.Sin, scale=float(-TWO_PI), bias=hpi)
                        for (t0, L, is_ctx, si) in SEQS:
                            cb_ = min(512, L)
                            nblk = L // cb_
                            if d == 0:
                                CSf, SNf = CS[:, 0:L], SN[:, 0:L]
                            else:
                                CSf, SNf = CS[:, 0:L].rev(), SN[:, 0:L].rev()
                            for blk in range(nblk):
                                bs = slice(blk * cb_, (blk + 1) * cb_)
                                ts_ = slice(t0 + blk * cb_, t0 + (blk + 1) * cb_)
                                pr = self.bank()
                                pi_ = self.bank()
                                kb.mm(pr[:, 0:cb_], LBre, SU[:, ts_])
                                kb.mm(pi_[:, 0:cb_], LBim, SU[:, ts_])
                                kb.copy("act", PR[:, bs], pr[:, 0:cb_])
                                kb.copy("act", PI[:, bs], pi_[:, 0:cb_])
                            Lx = slice(0, L)
                            kb.tt("dve", G1[:, Lx], PR[:, Lx], CSf, ALU.mult)
                            kb.tt("dve", HR[:, Lx], PI[:, Lx], SNf, ALU.mult)
                            kb.tt("dve", G1[:, Lx], G1[:, Lx], HR[:, Lx], ALU.add)
                            kb.tt("pool", G2[:, Lx], PI[:, Lx], CSf, ALU.mult)
                            kb.tt("pool", HI[:, Lx], PR[:, Lx], SNf, ALU.mult)
                            kb.tt("pool", G2[:, Lx], G2[:, Lx], HI[:, Lx], ALU.subtract)
                            magb = mag[:, dq:dq + 1].bc([128, L])
                            ir = 0.0 if is_ctx else gir[:, dq:dq + 1]
                            ii_ = 0.0 if is_ctx else gii[:, dq:dq + 1]
                            if d == 0:
                                kb.scan(HR[:, 0:L], magb, G1[:, 0:L], ir)
                                kb.scan(HI[:, 0:L], magb, G2[:, 0:L], ii_)
                            else:
                                kb.scan(HR[:, 0:L].rev(), magb, G1[:, 0:L].rev(), ir)
                                kb.scan(HI[:, 0:L].rev(), magb, G2[:, 0:L].rev(), ii_)
                            if is_ctx:
                                lc = L - 1 if d == 0 else 0
                                hr1, hi1 = HR[:, lc:lc + 1], HI[:, lc:lc + 1]
                                c1, s1 = CS[:, L - 1:L], SN[:, L - 1:L]
                                kb.tt("dve", fcol[:, 0:1], hr1, c1, ALU.mult)
                                kb.tt("dve", fcol[:, 1:2], hi1, s1, ALU.mult)
                                kb.tt("dve", FIN[:, si, 0, dq:dq + 1], fcol[:, 0:1], fcol[:, 1:2], ALU.subtract)
                                kb.tt("dve", fcol[:, 0:1], hi1, c1, ALU.mult)
                                kb.tt("dve", fcol[:, 1:2], hr1, s1, ALU.mult)
                                kb.tt("dve", FIN[:, si, 1, dq:dq + 1], fcol[:, 0:1], fcol[:, 1:2], ALU.add)
                            kb.tt("dve", G1[:, 0:L], HR[:, 0:L], CSf, ALU.mult)
                            kb.tt("dve", T1[:, 0:L], HI[:, 0:L], SNf, ALU.mult)
                            kb.tt("dve", hrb[:, 0:L], G1[:, 0:L], T1[:, 0:L], ALU.subtract)
                            kb.tt("pool", G2[:, 0:L], HI[:, 0:L], CSf, ALU.mult)
                            kb.tt("pool", T2[:, 0:L], HR[:, 0:L], SNf, ALU.mult)
                            kb.tt("pool", hib[:, 0:L], G2[:, 0:L], T2[:, 0:L], ALU.add)
                            for blk in range(nblk):
                                bs = slice(blk * cb_, (blk + 1) * cb_)
                                ts_ = slice(t0 + blk * cb_, t0 + (blk + 1) * cb_)
                                py = self.bank()
                                kb.mm(py[:, 0:cb_], LCre[r], hrb[:, bs], start=True, stop=False)
                                kb.mm(py[:, 0:cb_], LCim[r], hib[:, bs], start=False, stop=True)
                                if first:
                                    kb.copy("act", Y[:, ts_], py[:, 0:cb_])
                                else:
                                    kb.tt("dve", Y[:, ts_], Y[:, ts_], py[:, 0:cb_], ALU.add)
                        first = False
                for (t0, L, is_ctx, si) in SEQS:
                    tk = slice(t0, t0 + L)
                    kb.dma("sp", HR[:, 0:L], self.mix[32 + j, :, tk])
                    kb.stt("dve", Y[:, tk], HR[:, 0:L], pt2[:, 48 + j:49 + j], Y[:, tk], ALU.mult, ALU.add)
                    self.gelu_tanh(ybs[:, 0:L], Y[:, tk], HI[:, 0:L])
                    kb.dma("sp", self.yb[j, :, tk], ybs[:, 0:L])
            for blk in range(NT):
                bs = slice(blk * TT, (blk + 1) * TT)
                yt = ybt[blk % 2]
                kb.dma("sp", yt, self.yb[:, :, bs].rr("k p t -> p k t"))
                for m in range(4):
                    pg = self.bank()
                    for kk in range(4):
                        kb.mm(pg, wglu[:, kk, m * 128:(m + 1) * 128], yt[:, kk, :], start=(kk == 0), stop=(kk == 3))
                    gsl = G1[:, m * TT:(m + 1) * TT]
                    kb.act(gsl, pg, AF.Sigmoid, bias=pt2[:, 52 + m:53 + m])
                    osl = hrb[:, m * TT:(m + 1) * TT]
                    kb.tt("dve", osl, yt[:, m, :], gsl, ALU.mult)
                    kb.dma("sp", self.oT[12 + m, :, bs], osl)
            for si in range(4):
                for c in range(2):
                    pz = self.bank()
                    kb.tr(pz[0:32, 0:128], FIN[:, si, c, :], self.ident)
                    kb.copy("act", stg[0:32, :], pz[0:32, 0:128])
                    dst = self.O["nre" if c == 0 else "nim"][si, l].rr("d (q g) p -> (d q) (g p)", g=2)
                    kb.dma("sp", dst, stg[0:32, :])
            kb.barrier()

    def run(self):
        self.setup()
        self.phase0()
        for l in range(self.depth):
            self.phaseM(l)
            self.phaseA(l)
            self.phaseB(l)
            self.phaseC(l, last=(l == self.depth - 1))
        outs = list(self.O.values())
        self.kb.barrier()


def build_program(depth=DEPTH, debug=False):
    nc = bass.Bass("TRN2", target_bir_lowering=False)
    with contextlib.ExitStack() as stack:
        kb = KB(nc, stack)
        prog = Prog(kb, stack, depth, debug)
        prog.run()
    return nc, kb


_CACHE = {}


def make_in_maps(inp):
    consts = host_consts()
    f = lambda a: np.ascontiguousarray(np.asarray(a, dtype=np.float32))
    shared = {}
    for n in ("w_mod", "b_mod", "norm1", "w_in", "ret_decay_logit", "ret_gn", "lru_conv_w", "lru_conv_b", "lru_wa", "lru_ba",
              "lru_wx", "lru_bx", "lru_lambda", "att_q_norm", "att_k_norm", "att_sink", "ssm_a_re", "ssm_a_im", "ssm_log_dt",
              "ssm_b_re", "ssm_b_im", "ssm_c_re", "ssm_c_im", "ssm_d", "ssm_w_glu", "ssm_b_glu", "w_out", "norm2", "w_ff1", "w_ff2"):
        shared[n] = f(inp[n])
    shared["w_br"] = f(inp["w_br"]).reshape(DEPTH * 4, 512, D)
    shared.update(consts)
    xp = f(inp["x_prompt"])
    xs = f(inp["x_sample"])
    maps = []
    for c in range(8):
        m = dict(shared)
        m["xp"] = xp[4 * c:4 * c + 4].reshape(NPT, D)
        m["xs"] = xs[c]
        m["ck"] = f(inp["cache_attn_k"][c])
        m["cv"] = f(inp["cache_attn_v"][c])
        m["sret"] = f(inp["state_ret"][c])
        m["slru"] = f(inp["state_lru"][c])
        m["sre"] = f(inp["state_ssm_re"][c])
        m["sim"] = f(inp["state_ssm_im"][c])
        m["cvec"] = np.stack([f(inp["c_ctx"]), f(inp["c"])[c]], axis=0)
        maps.append(m)
    return maps


def kernel(**inputs):
    if "nc" not in _CACHE:
        _CACHE["nc"] = build_program()[0]
    nc = _CACHE["nc"]
    maps = make_in_maps(inputs)
    res = run_bass_kernel_spmd(nc, maps, core_ids=list(range(8)))
    R = res.results
    y_prompt = np.concatenate([r["yp"].reshape(4, 256, D) for r in R], axis=0)
    y_sample = np.stack([r["ys"] for r in R], axis=0)
    cat = lambda n: np.concatenate([r[n] for r in R], axis=0)
    return (y_prompt, y_sample, cat("nk"), cat("nv"), cat("nret"), cat("nlru"), cat("nre"), cat("nim"))
```

```python
import contextlib
import math
import numpy as np
import concourse.bass as bass
import concourse.mybir as mybir
from concourse.bass_utils import run_bass_kernel_spmd

F32 = mybir.dt.float32
BF16 = mybir.dt.bfloat16
ALU = mybir.AluOpType
AF = mybir.ActivationFunctionType

SEM_CAP = 30000
N_DMA_SEMS = 150

DEPTH = 4
D = 2048
KC = 16
NPT = 1024
NST = 2048
NTOK = NPT + NST
TT = 512
NT = NTOK // TT
MIXC = 36
GATE0 = 4608
EPS = 1e-6
MAGIC = 12582912.0
TWO_PI = 2.0 * math.pi
SEQS = [(0, 256, True, 0), (256, 256, True, 1), (512, 256, True, 2), (768, 256, True, 3), (1024, 2048, False, 0)]


class Dep:
    __slots__ = ("w", "r", "dsem")

    def __init__(self):
        self.w = {}
        self.r = {}
        self.dsem = None


class T:
    __slots__ = ("ap", "dep")

    def __init__(self, ap, dep):
        self.ap = ap
        self.dep = dep

    def __getitem__(self, k):
        return T(self.ap[k], self.dep)

    def rr(self, s, **kw):
        return T(self.ap.rearrange(s, **kw), self.dep)

    def bc(self, shape):
        return T(self.ap.to_broadcast(list(shape)), self.dep)

    def rev(self):
        nd = len(self.ap.shape)
        idx = tuple([slice(None)] * (nd - 1) + [slice(None, None, -1)])
        return T(self.ap[idx], self.dep)


class KB:
    def __init__(self, nc, stack):
        self.nc = nc
        self.stack = stack
        self.eng = {"pe": nc.tensor, "act": nc.scalar, "dve": nc.vector, "pool": nc.gpsimd, "sp": nc.sync}
        self.sems = {}
        self.nsem = 0
        self.cur = {}
        self.epoch = {e: 0 for e in self.eng}
        self.seen = {e: {} for e in self.eng}
        self.deps = []
        self.ninst = {e: 0 for e in self.eng}
        self.dma_pool = [("d", i) for i in range(N_DMA_SEMS)]
        self.dma_cnt = {k: 0 for k in self.dma_pool}
        self.dma_free = list(self.dma_pool)
        for e in self.eng:
            self._new_engine_sem(e)

    def _sem(self, key):
        if key not in self.sems:
            self.sems[key] = self.stack.enter_context(self.nc.semaphore(f"s{self.nsem}"))
            self.nsem += 1
        return self.sems[key]

    def _new_engine_sem(self, e):
        key = ("e", e, self.epoch[e])
        self._sem(key)
        self.cur[e] = [key, 0]

    def newdep(self):
        d = Dep()
        self.deps.append(d)
        return d

    def sb(self, stack, name, shape, dtype):
        self.nalloc = getattr(self, "nalloc", 0) + 1
        name = f"{name}_{self.nalloc}"
        t = stack.enter_context(self.nc.sbuf_tensor(name, list(shape), dtype))
        return T(t.ap(), self.newdep())

    def ps(self, stack, name, shape, dtype):
        t = stack.enter_context(self.nc.psum_tensor(name, list(shape), dtype))
        return T(t.ap(), self.newdep())

    def dram(self, name, shape, dtype, kind="Internal"):
        t = self.nc.dram_tensor(name, list(shape), dtype, kind=kind)
        return T(t.ap(), self.newdep())

    def sub(self, t):
        return T(t.ap, self.newdep())

    def _wait(self, e, evs):
        seen = self.seen[e]
        need = {}
        for k, v in evs:
            if seen.get(k, 0) < v and need.get(k, 0) < v:
                need[k] = v
        for k, v in need.items():
            seen[k] = v
            if k[0] == "e" and k[1] == e and e == "pe":
                continue
            self.eng[e].wait_ge(self._sem(k), v)

    def _collect(self, e, reads, writes, compute):
        evs = []
        for t in reads:
            evs.extend(t.dep.w.items())
        for t in writes:
            d = t.dep
            for k, v in d.w.items():
                if compute and k[0] == "e" and k[1] == e:
                    continue
                evs.append((k, v))
            for k, v in d.r.items():
                if compute and k[0] == "e" and k[1] == e:
                    continue
                evs.append((k, v))
        return evs

    def _record(self, ev, reads, writes):
        k, v = ev
        for t in reads:
            t.dep.r[k] = v
        for t in writes:
            t.dep.w[k] = v

    def op(self, e, fn, reads=(), writes=()):
        self._wait(e, self._collect(e, reads, writes, True))
        ins = fn()
        if self.cur[e][1] >= SEM_CAP:
            self.epoch[e] += 1
            self._new_engine_sem(e)
        key = self.cur[e][0]
        self.cur[e][1] += 1
        ins.then_inc(self._sem(key), 1)
        self._record((key, self.cur[e][1]), reads, writes)
        self.ninst[e] += 1
        return ins

    def dma(self, e, out, in_, sem_t=None, **kw):
        st = (sem_t or out).dep
        if st.dsem is None or self.dma_cnt[st.dsem] + 16 > SEM_CAP:
            st.dsem = self.dma_free.pop()
        self._wait(e, self._collect(e, [in_], [out], False))
        ins = self.eng[e].dma_start(out=out.ap, in_=in_.ap, **kw)
        self.dma_cnt[st.dsem] += 16
        ins.then_inc(self._sem(st.dsem), 16)
        self._record((st.dsem, self.dma_cnt[st.dsem]), [in_], [out])
        self.ninst[e] += 1
        return ins

    def barrier(self):
        evs = {}
        for d in self.deps:
            for dd in (d.w, d.r):
                for k, v in dd.items():
                    if evs.get(k, 0) < v:
                        evs[k] = v
        for e in self.eng:
            self._wait(e, list(evs.items()))
        keep = []
        for d in self.deps:
            d.w.clear()
            d.r.clear()
            d.dsem = None
        self.dma_free = [k for k in self.dma_pool if self.dma_cnt[k] + 64 <= SEM_CAP]

    def _e(self, e):
        return self.eng[e]

    def tt(self, e, out, a, b, op):
        return self.op(e, lambda: self._e(e).tensor_tensor(out=out.ap, in0=a.ap, in1=b.ap, op=op), [a, b], [out])

    def ts(self, e, out, a, s1, op0, s2=None, op1=None):
        rd = [a] + [s for s in (s1, s2) if isinstance(s, T)]
        v1 = s1.ap if isinstance(s1, T) else s1
        v2 = s2.ap if isinstance(s2, T) else s2
        if op1 is None:
            return self.op(e, lambda: self._e(e).tensor_scalar(out=out.ap, in0=a.ap, scalar1=v1, scalar2=None, op0=op0), rd, [out])
        return self.op(e, lambda: self._e(e).tensor_scalar(out=out.ap, in0=a.ap, scalar1=v1, scalar2=v2, op0=op0, op1=op1), rd, [out])

    def stt(self, e, out, a, s, b, op0, op1):
        rd = [a, b] + ([s] if isinstance(s, T) else [])
        sv = s.ap if isinstance(s, T) else s
        return self.op(e, lambda: self._e(e).scalar_tensor_tensor(out=out.ap, in0=a.ap, scalar=sv, in1=b.ap, op0=op0, op1=op1), rd, [out])

    def act(self, out, a, func, scale=None, bias=None):
        rd = [a] + [s for s in (scale, bias) if isinstance(s, T)]
        kw = {}
        if scale is not None:
            kw["scale"] = scale.ap if isinstance(scale, T) else scale
        if bias is not None:
            kw["bias"] = bias.ap if isinstance(bias, T) else bias
        return self.op("act", lambda: self.nc.scalar.activation(out=out.ap, in_=a.ap, func=func, **kw), rd, [out])

    def copy(self, e, out, a):
        if e == "act":
            return self.op("act", lambda: self.nc.scalar.copy(out=out.ap, in_=a.ap), [a], [out])
        return self.op(e, lambda: self._e(e).tensor_copy(out=out.ap, in_=a.ap), [a], [out])

    def memset(self, e, out, val):
        return self.op(e, lambda: self._e(e).memset(out.ap, val), [], [out])

    def recip(self, out, a):
        return self.op("dve", lambda: self.nc.vector.reciprocal(out=out.ap, in_=a.ap), [a], [out])

    def scan(self, out, a, b, init):
        rd = [a, b] + ([init] if isinstance(init, T) else [])
        iv = init.ap if isinstance(init, T) else init
        return self.op("dve", lambda: self.nc.vector.tensor_tensor_scan(out=out.ap, data0=a.ap, data1=b.ap, initial=iv,
                                                                         op0=ALU.mult, op1=ALU.add), rd, [out])

    def mm(self, out, lhsT, rhs, start=True, stop=True):
        return self.op("pe", lambda: self.nc.tensor.matmul(out.ap, lhsT=lhsT.ap, rhs=rhs.ap, start=start, stop=stop),
                       [lhsT, rhs], [out])

    def tr(self, out, a, ident):
        return self.op("pe", lambda: self.nc.tensor.transpose(out.ap, a.ap, ident.ap), [a, ident], [out])


class WStream:
    def __init__(self, kb, stack, plan, srcs, nslots=4):
        self.kb = kb
        self.n = nslots
        self.slots = [kb.sb(stack, f"wring{i}", [128, 16, 512], BF16) for i in range(nslots)]
        self.plan = plan
        self.srcs = srcs
        self.issued = 0
        self.used = 0

    def _issue(self, idx):
        name, l, r0, nk, c0 = self.plan[idx]
        slot = self.slots[idx % self.n]
        src = self.srcs[name]
        ap = src.ap[l, r0:r0 + nk * 128, c0:c0 + 512].rearrange("(k p) c -> p k c", p=128)
        self.kb.dma("pool", slot[:, 0:nk, :], T(ap, src.dep))

    def next(self, desc):
        assert self.plan[self.used] == desc, (self.used, self.plan[self.used], desc)
        while self.issued < min(len(self.plan), self.used + self.n - 1):
            self._issue(self.issued)
            self.issued += 1
        t = self.slots[self.used % self.n]
        self.used += 1
        return t


def weight_plan(depth):
    plan = []
    for l in range(depth):
        for cb in range(24):
            plan.append(("w_mod", l, 0, 16, cb * 512))
        for ti in range(NT // 2):
            for cb in range(9):
                plan.append(("w_in", l, 0, 16, cb * 512))
        for ti in range(NT):
            for mg in range(4):
                for b in range(4):
                    plan.append(("w_in", l, 0, 16, GATE0 + b * 2048 + mg * 512))
                    plan.append(("w_br", l * 4 + b, 0, 4, mg * 512))
            for mg in range(4):
                plan.append(("w_out", l, 0, 16, mg * 512))
            for s in range(4):
                for fb in range(4):
                    plan.append(("w_ff1", l, 0, 16, s * 2048 + fb * 512))
                for mg in range(4):
                    plan.append(("w_ff2", l, s * 2048, 16, mg * 512))
    return plan


IN_SPECS = [
    ("xp", [NPT, D]), ("xs", [NST, D]), ("ck", [DEPTH, 256, 2, 128]), ("cv", [DEPTH, 256, 2, 128]),
    ("sret", [DEPTH, 2, 4, 128, 128]), ("slru", [DEPTH, 2, 512]), ("sre", [DEPTH, 2, 32, 64]), ("sim", [DEPTH, 2, 32, 64]),
    ("cvec", [2, D]),
    ("w_mod", [DEPTH, D, 6 * D]), ("b_mod", [DEPTH, 6 * D]), ("norm1", [DEPTH, D]), ("w_in", [DEPTH, D, 12800]),
    ("ret_decay_logit", [DEPTH, 2, 4]), ("ret_gn", [DEPTH, 512]), ("lru_conv_w", [DEPTH, 4, 512]), ("lru_conv_b", [DEPTH, 512]),
    ("lru_wa", [DEPTH, 2, 4, 128, 128]), ("lru_ba", [DEPTH, 2, 512]), ("lru_wx", [DEPTH, 2, 4, 128, 128]), ("lru_bx", [DEPTH, 2, 512]),
    ("lru_lambda", [DEPTH, 2, 512]), ("att_q_norm", [DEPTH, 128]), ("att_k_norm", [DEPTH, 128]), ("att_sink", [DEPTH, 4]),
    ("ssm_a_re", [DEPTH, 2, 32, 64]), ("ssm_a_im", [DEPTH, 2, 32, 64]), ("ssm_log_dt", [DEPTH, 2, 32]),
    ("ssm_b_re", [DEPTH, 2, 32, 64, 16]), ("ssm_b_im", [DEPTH, 2, 32, 64, 16]),
    ("ssm_c_re", [DEPTH, 2, 32, 16, 64]), ("ssm_c_im", [DEPTH, 2, 32, 16, 64]),
    ("ssm_d", [DEPTH, 512]), ("ssm_w_glu", [DEPTH, 512, 512]), ("ssm_b_glu", [DEPTH, 512]),
    ("w_br", [DEPTH * 4, 512, D]), ("w_out", [DEPTH, D, D]), ("norm2", [DEPTH, D]),
    ("w_ff1", [DEPTH, D, 4 * D]), ("w_ff2", [DEPTH, 4 * D, D]),
    ("k_ident", [128, 128]), ("k_tpos", [128, 2048]), ("k_ropec", [128, 2048]), ("k_ropes", [128, 2048]),
    ("k_perm", [128, 128]), ("k_pf", [128, 128]), ("k_pb", [128, 128]), ("k_mf", [128, 128]), ("k_mb", [128, 128]),
    ("k_tp1", [128, 2048]), ("k_tm", [128, 2048]), ("k_cols", [128, 4]),
]
OUT_SPECS = [
    ("yp", [NPT, D]), ("ys", [NST, D]), ("nk", [4, DEPTH, 256, 2, 128]), ("nv", [4, DEPTH, 256, 2, 128]),
    ("nret", [4, DEPTH, 2, 4, 128, 128]), ("nlru", [4, DEPTH, 2, 512]), ("nre", [4, DEPTH, 2, 32, 64]), ("nim", [4, DEPTH, 2, 32, 64]),
]


def host_consts():
    i = np.arange(128, dtype=np.float32)
    t = np.arange(2048, dtype=np.float32)
    c = {}
    c["k_ident"] = np.eye(128, dtype=np.float32)
    c["k_tpos"] = np.broadcast_to(t[None, :], (128, 2048)).copy()
    row = np.floor(t / 64.0)
    col = t - 64.0 * row
    freqs = (10000.0 ** (-np.arange(32, dtype=np.float32) / 32.0)).astype(np.float32)
    ang = np.concatenate([row[None, :].astype(np.float32) * freqs[:, None], col[None, :].astype(np.float32) * freqs[:, None]], axis=0)
    ang = ang.astype(np.float32)
    cos = np.cos(ang).astype(np.float32)
    sin = np.sin(ang).astype(np.float32)
    c["k_ropec"] = np.concatenate([cos, cos], axis=0)
    c["k_ropes"] = np.concatenate([-sin, sin], axis=0)
    perm = np.zeros((128, 128), np.float32)
    for m in range(128):
        perm[(m + 64) % 128, m] = 1.0
    c["k_perm"] = perm
    jj = i[:, None]
    ii = i[None, :]
    c["k_pf"] = np.maximum(ii - jj, 0.0).astype(np.float32)
    c["k_pb"] = np.maximum(jj - ii, 0.0).astype(np.float32)
    c["k_mf"] = (ii >= jj).astype(np.float32)
    c["k_mb"] = (jj >= ii).astype(np.float32)
    c["k_tp1"] = np.broadcast_to(((t % 128) + 1.0)[None, :], (128, 2048)).astype(np.float32).copy()
    c["k_tm"] = np.broadcast_to((128.0 - (t % 128))[None, :], (128, 2048)).astype(np.float32).copy()
    cols = np.zeros((128, 4), np.float32)
    cols[:, 0] = 127.0 - i
    cols[:, 1] = i
    c["k_cols"] = cols
    return c


class Prog:
    def __init__(self, kb, stack, depth, debug=False):
        self.kb = kb
        self.depth = depth
        nc = kb.nc
        self.I = {n: kb.dram(n, s, F32, kind="ExternalInput") for n, s in IN_SPECS}
        self.O = {n: kb.dram(n, s, F32, kind="ExternalOutput") for n, s in OUT_SPECS}
        sk = "ExternalOutput" if debug else "Internal"
        self.xT = kb.dram("s_xT", [KC, 128, NTOK], F32)
        self.hT = kb.dram("s_hT", [KC, 128, NTOK], BF16, kind=sk)
        self.debug = debug
        if debug:
            self.dmg = kb.dram("s_mg", [KC, 128, NTOK], BF16, kind=sk)
            self.dsg = kb.dram("s_sg", [128, NTOK], F32, kind=sk)
        self.mix = kb.dram("s_mix", [MIXC, 128, NTOK], F32, kind=sk)
        self.yb = kb.dram("s_yb", [4, 128, NTOK], BF16)
        self.oT = kb.dram("s_oT", [KC, 128, NTOK], BF16 if not debug else BF16, kind=sk)
        sb = lambda n, s, d: kb.sb(stack, n, s, d)
        self.ident = sb("ident", [128, 128], F32)
        self.ones_bf = sb("ones_bf", [128, 128], BF16)
        self.invf = sb("invf", [128, 128], F32)
        self.scond = sb("scond", [128, KC, 2], BF16)
        self.modT = sb("modT", [128, 96, 2], F32)
        self.ptab1 = sb("ptab1", [128, 128], F32)
        self.A1 = sb("A1", [128, KC, 2], F32)
        self.A2 = sb("A2", [128, KC, 2], F32)
        self.banks = [kb.ps(stack, f"bank{i}", [128, 512], F32) for i in range(7)]
        self.pbf = kb.ps(stack, "pbf", [128, 1024], BF16)
        self.bi = 0
        self.ws = WStream(kb, stack, weight_plan(depth), self.I, nslots=4)

    def bank(self):
        b = self.banks[self.bi % 7]
        self.bi += 1
        return b

    def dsl(self, t, *idx):
        return T(t.ap[idx], t.dep)

    def setup(self):
        kb = self.kb
        with contextlib.ExitStack() as st:
            kb.dma("sp", self.ident, self.I["k_ident"])
            kb.memset("dve", self.ones_bf, 1.0)
            kb.memset("dve", self.invf, 1.0 / 128.0)
            stg = kb.sb(st, "su_stg", [32, 128], F32)
            kb.dma("sp", stg, self.I["cvec"].rr("j (k p) -> (j k) p", p=128))
            pb = self.bank()
            kb.tr(pb[:, 0:32], stg, self.ident[0:32, 0:32])
            cs = kb.sb(st, "su_c", [128, 32], F32)
            kb.act(cs, pb[:, 0:32], AF.Silu)
            kb.copy("dve", self.scond.rr("p k j -> p j k"), cs.rr("p (j k) -> p j k", j=2))
            kb.barrier()

    def phase0(self):
        kb = self.kb
        with contextlib.ExitStack() as st:
            xin = kb.sb(st, "p0_x", [128, 4, D], F32)
            xo = kb.sb(st, "p0_o", [128, KC, TT], F32)
            for ti in range(NT):
                if ti < 2:
                    src = self.I["xp"][ti * 512:(ti + 1) * 512, :]
                else:
                    src = self.I["xs"][(ti - 2) * 512:(ti - 1) * 512, :]
                kb.dma("sp", xin, src.rr("(g p) f -> p g f", p=128))
                for k in range(KC):
                    pb = self.bank()
                    for g in range(4):
                        kb.tr(pb[:, g * 128:(g + 1) * 128], xin[:, g, k * 128:(k + 1) * 128], self.ident)
                    kb.copy("act" if k % 2 else "dve", xo[:, k, :], pb)
                kb.dma("sp", self.xT[:, :, ti * 512:(ti + 1) * 512].rr("k p t -> p k t"), xo)
            kb.barrier()

    def phaseM(self, l):
        kb = self.kb
        with contextlib.ExitStack() as st:
            stg = kb.sb(st, "m_stg", [128, 128], F32)
            kb.dma("sp", stg[0:16, :], self.I["norm1"][l].rr("(k p) -> k p", p=128))
            kb.dma("sp", stg[16:32, :], self.I["norm2"][l].rr("(k p) -> k p", p=128))
            kb.dma("sp", stg[32:128, :], self.I["b_mod"][l].rr("(k p) -> k p", p=128))
            pb = self.bank()
            kb.tr(pb[:, 0:128], stg, self.ident)
            kb.copy("dve", self.ptab1, pb[:, 0:128])
            for cb in range(24):
                wt = self.ws.next(("w_mod", l, 0, 16, cb * 512))
                for mm in range(4):
                    m = cb * 4 + mm
                    pb = self.bank()
                    for k in range(KC):
                        kb.mm(pb[:, 0:2], wt[:, k, mm * 128:(mm + 1) * 128], self.scond[:, k, :], start=(k == 0), stop=(k == KC - 1))
                    kb.ts("dve", self.modT[:, m, :], pb[:, 0:2], self.ptab1[:, 32 + m:33 + m], ALU.add)
            for (A, nb, scb) in ((self.A1, 0, 16), (self.A2, 16, 64)):
                for j in range(2):
                    kb.ts("dve", A[:, :, j], self.modT[:, scb:scb + 16, j], 1.0, ALU.add)
                    kb.tt("dve", A[:, :, j], A[:, :, j], self.ptab1[:, nb:nb + 16], ALU.mult)
            kb.barrier()

    def norm_mod(self, xt, hT, A, shb, j, sq, tmp, rstd):
        kb = self.kb
        pb = self.bank()
        for k in range(KC):
            kb.act(sq[k % 2], xt[:, k, :], AF.Square)
            kb.mm(pb, self.ones_bf, sq[k % 2], start=(k == 0), stop=(k == KC - 1))
        kb.ts("dve", rstd, pb, 1.0 / D, ALU.mult, EPS, ALU.add)
        kb.act(rstd, rstd, AF.Sqrt)
        kb.recip(rstd, rstd)
        for k in range(KC):
            kb.stt("dve", tmp[k % 2], xt[:, k, :], A[:, k, j:j + 1], rstd, ALU.mult, ALU.mult)
            kb.act(hT[:, k, :], tmp[k % 2], AF.Identity, bias=self.modT[:, shb + k, j:j + 1])

    def phaseA(self, l):
        kb = self.kb
        TA = 2 * TT
        with contextlib.ExitStack() as st:
            xt = kb.sb(st, "a_xt", [128, KC, TA], F32)
            hT = kb.sb(st, "a_hT", [128, KC, TA], BF16)
            sq = [kb.sb(st, f"a_sq{i}", [128, TT], BF16) for i in range(2)]
            tmp = [kb.sb(st, f"a_tmp{i}", [128, TT], F32) for i in range(2)]
            rstd = kb.sb(st, "a_rstd", [128, TT], F32)
            stage = [kb.sb(st, f"a_stg{i}", [128, TT], F32) for i in range(6)]
            for ti in range(NTOK // TA):
                j = 0 if ti < 1 else 1
                tok = slice(ti * TA, (ti + 1) * TA)
                kb.dma("sp", xt, self.xT[:, :, tok].rr("k p t -> p k t"))
                for hf in range(2):
                    hs = slice(hf * TT, (hf + 1) * TT)
                    self.norm_mod(xt[:, :, hs], hT[:, :, hs], self.A1, 0, j, sq, tmp, rstd)
                kb.dma("sp", self.hT[:, :, tok].rr("k p t -> p k t"), hT)
                cnt = 0
                for cb in range(9):
                    wt = self.ws.next(("w_in", l, 0, 16, cb * 512))
                    for mm in range(4):
                        m = cb * 4 + mm
                        for hf in range(2):
                            hs = slice(hf * TT, (hf + 1) * TT)
                            pb = self.bank()
                            for k in range(KC):
                                kb.mm(pb, wt[:, k, mm * 128:(mm + 1) * 128], hT[:, k, hs], start=(k == 0), stop=(k == KC - 1))
                            sg = stage[cnt % 6]
                            kb.copy("act" if cnt % 2 else "dve", sg, pb)
                            kb.dma("sp", self.mix[m, :, ti * TA + hf * TT: ti * TA + (hf + 1) * TT], sg)
                            cnt += 1
            kb.barrier()

    def phaseC(self, l, last):
        kb = self.kb
        ws = self.ws
        with contextlib.ExitStack() as st:
            xt = kb.sb(st, "c_xt", [128, KC, TT], F32)
            hT = kb.sb(st, "c_hT", [128, KC, TT], BF16)
            oT = kb.sb(st, "c_oT", [128, KC, TT], BF16)
            mg_t = kb.sb(st, "c_mg", [128, KC, TT], BF16)
            acc = [kb.sb(st, f"c_acc{i}", [128, TT], F32) for i in range(4)]
            sig = [kb.sb(st, f"c_sig{i}", [128, TT], F32) for i in range(2)]
            tmpm = [kb.sb(st, f"c_tm{i}", [128, TT], F32) for i in range(2)]
            rl = [kb.sb(st, f"c_rl{i}", [128, TT], BF16) for i in range(2)]
            sq = [kb.sb(st, f"c_sq{i}", [128, TT], BF16) for i in range(2)]
            tmp = [kb.sb(st, f"c_tmp{i}", [128, TT], F32) for i in range(2)]
            rstd = kb.sb(st, "c_rstd", [128, TT], F32)
            actbuf = [oT, mg_t]
            for ti in range(NT):
                j = 0 if ti < 2 else 1
                tok = slice(ti * TT, (ti + 1) * TT)
                kb.dma("sp", hT, self.hT[:, :, tok].rr("k p t -> p k t"))
                kb.dma("sp", oT, self.oT[:, :, tok].rr("k p t -> p k t"))
                kb.dma("sp", xt, self.xT[:, :, tok].rr("k p t -> p k t"))
                cnt = 0
                for mg in range(4):
                    for b in range(4):
                        wg = ws.next(("w_in", l, 0, 16, GATE0 + b * 2048 + mg * 512))
                        wb = ws.next(("w_br", l * 4 + b, 0, 4, mg * 512))
                        for mm in range(4):
                            pg = self.bank()
                            for k in range(KC):
                                kb.mm(pg, wg[:, k, mm * 128:(mm + 1) * 128], hT[:, k, :], start=(k == 0), stop=(k == KC - 1))
                            sg = sig[cnt % 2]
                            kb.act(sg, pg, AF.Sigmoid)
                            pbk = self.bank()
                            for kk in range(4):
                                kb.mm(pbk, wb[:, kk, mm * 128:(mm + 1) * 128], oT[:, 4 * b + kk, :], start=(kk == 0), stop=(kk == 3))
                            if b == 0:
                                kb.tt("dve", acc[mm], sg, pbk, ALU.mult)
                            else:
                                tm = tmpm[cnt % 2]
                                kb.tt("dve", tm, sg, pbk, ALU.mult)
                                kb.tt("dve", acc[mm], acc[mm], tm, ALU.add)
                            cnt += 1
                    for mm in range(4):
                        kb.copy("act", mg_t[:, mg * 4 + mm, :], acc[mm])
                if self.debug:
                    kb.dma("sp", self.dmg[:, :, tok].rr("k p t -> p k t"), mg_t)
                    kb.dma("sp", self.dsg[:, tok], sig[(cnt - 1) % 2])
                for mg in range(4):
                    wo = ws.next(("w_out", l, 0, 16, mg * 512))
                    for mm in range(4):
                        m = mg * 4 + mm
                        pb = self.bank()
                        for k in range(KC):
                            kb.mm(pb, wo[:, k, mm * 128:(mm + 1) * 128], mg_t[:, k, :], start=(k == 0), stop=(k == KC - 1))
                        kb.stt("dve", xt[:, m, :], pb, self.modT[:, 32 + m, j:j + 1], xt[:, m, :], ALU.mult, ALU.add)
                self.norm_mod(xt, hT, self.A2, 48, j, sq, tmp, rstd)
                for s in range(4):
                    ab = actbuf[s % 2]
                    for fb in range(4):
                        w1 = ws.next(("w_ff1", l, 0, 16, s * 2048 + fb * 512))
                        for mm in range(4):
                            f = fb * 4 + mm
                            pb = self.bank()
                            for k in range(KC):
                                kb.mm(pb, w1[:, k, mm * 128:(mm + 1) * 128], hT[:, k, :], start=(k == 0), stop=(k == KC - 1))
                            r = rl[f % 2]
                            kb.act(r, pb, AF.Relu)
                            kb.tt("dve", ab[:, f, :], r, r, ALU.mult)
                    for mg in range(4):
                        w2 = ws.next(("w_ff2", l, s * 2048, 16, mg * 512))
                        for mm in range(4):
                            m = mg * 4 + mm
                            pb = self.bank()
                            for f in range(KC):
                                kb.mm(pb, w2[:, f, mm * 128:(mm + 1) * 128], ab[:, f, :], start=(f == 0), stop=(f == KC - 1))
                            kb.stt("dve", xt[:, m, :], pb, self.modT[:, 80 + m, j:j + 1], xt[:, m, :], ALU.mult, ALU.add)
                if not last:
                    kb.dma("sp", self.xT[:, :, tok].rr("k p t -> p k t"), xt)
                else:
                    for g in range(4):
                        for kq in range(4):
                            pb = self.bank()
                            for kk in range(4):
                                kb.tr(pb[:, kk * 128:(kk + 1) * 128], xt[:, kq * 4 + kk, g * 128:(g + 1) * 128], self.ident)
                            kb.copy("act" if kq % 2 else "dve", acc[kq], pb)
                            if ti < 2:
                                dst = self.O["yp"][ti * 512 + g * 128: ti * 512 + (g + 1) * 128, kq * 512:(kq + 1) * 512]
                            else:
                                dst = self.O["ys"][(ti - 2) * 512 + g * 128:(ti - 2) * 512 + (g + 1) * 128, kq * 512:(kq + 1) * 512]
                            kb.dma("sp", dst, acc[kq])
            kb.barrier()

    def phaseB(self, l):
        kb = self.kb
        with contextlib.ExitStack() as st:
            stg = kb.sb(st, "b_stg", [128, 128], F32)
            I = self.I
            kb.memset("dve", stg, 0.0)
            kb.dma("sp", stg[0:4, :], I["ret_gn"][l].rr("(k p) -> k p", p=128))
            kb.dma("sp", stg[4:20, :], I["lru_conv_w"][l].rr("j (n p) -> (j n) p", p=128))
            kb.dma("sp", stg[20:24, :], I["lru_conv_b"][l].rr("(k p) -> k p", p=128))
            kb.dma("sp", stg[24:32, :], I["lru_ba"][l].rr("d (n p) -> (d n) p", p=128))
            kb.dma("sp", stg[32:40, :], I["lru_bx"][l].rr("d (n p) -> (d n) p", p=128))
            kb.dma("sp", stg[40:48, :], I["lru_lambda"][l].rr("d (n p) -> (d n) p", p=128))
            kb.dma("sp", stg[48:52, :], I["ssm_d"][l].rr("(k p) -> k p", p=128))
            kb.dma("sp", stg[52:56, :], I["ssm_b_glu"][l].rr("(k p) -> k p", p=128))
            kb.dma("sp", stg[56:57, :], I["att_q_norm"][l:l + 1, :])
            kb.dma("sp", stg[57:58, :], I["att_k_norm"][l:l + 1, :])
            kb.dma("sp", stg[58:66, :], I["slru"][l].rr("d (n p) -> (d n) p", p=128))
            pt2 = kb.sb(st, "b_pt2", [128, 128], F32)
            pb = self.bank()
            kb.tr(pb[:, 0:128], stg, self.ident)
            kb.copy("dve", pt2, pb[:, 0:128])
            bc = kb.sb(st, "b_bc", [128, 12], F32)
            kb.dma("sp", bc[:, 0:8], T(I["ret_decay_logit"].ap[l].rearrange("d h -> (d h)").partition_broadcast(128), I["ret_decay_logit"].dep))
            kb.dma("sp", bc[:, 8:12], T(I["att_sink"].ap[l].partition_broadcast(128), I["att_sink"].dep))
            identb = kb.sb(st, "b_identb", [128, 128], BF16)
            kb.copy("dve", identb, self.ident)
            kb.barrier()
            self.B_ret(l, pt2, bc, identb)
            self.B_lru(l, pt2)
            self.B_att(l, pt2, bc, identb)
            self.B_ssm(l, pt2, identb)

    def B_ret(self, l, pt2, bc, identb):
        kb = self.kb
        I = self.I
        LM = NST
        with contextlib.ExitStack() as st:
            sbf = lambda n, s, d: kb.sb(st, "r_" + n, s, d)
            pf, pbk_, mf, mb = [sbf(n, [128, 128], F32) for n in ("pf", "pb", "mf", "mb")]
            for t_, n_ in ((pf, "k_pf"), (pbk_, "k_pb"), (mf, "k_mf"), (mb, "k_mb")):
                kb.dma("sp", t_, I[n_])
            tp1 = sbf("tp1", [128, LM], F32)
            tm = sbf("tm", [128, LM], F32)
            cols = sbf("cols", [128, 4], F32)
            kb.dma("sp", tp1, I["k_tp1"])
            kb.dma("sp", tm, I["k_tm"])
            kb.dma("sp", cols, I["k_cols"])
            lg = sbf("lg", [128, 8], F32)
            kb.act(lg, bc[:, 0:8], AF.Exp, scale=-1.0)
            kb.ts("dve", lg, lg, 1.0, ALU.add)
            kb.act(lg, lg, AF.Ln)
            kb.ts("dve", lg, lg, -1.0, ALU.mult)
            lnsc = sbf("lnsc", [128, 1], F32)
            kb.memset("dve", lnsc, float(math.log(128.0 ** -0.5)))
            Dm = sbf("Dm", [128, 128], F32)
            e2 = sbf("e2", [128, 128], F32)
            wcol = sbf("wcol", [128, 4], F32)
            wfT = sbf("wfT", [128, LM], F32)
            wbT = sbf("wbT", [128, LM], F32)
            S0 = sbf("S0", [128, 2, 128], F32)
            S = sbf("S", [128, 128], F32)
            Hq, Hk, Hv, Hqf, Hqb, Hout = [sbf(n, [128, LM], BF16) for n in ("Hq", "Hk", "Hv", "Hqf", "Hqb", "Hout")]
            Ktok, Vtok, Kwf, Kwb, Sfb, Sbb = [sbf(n, [128, LM // 128, 128], BF16) for n in ("Kt", "Vt", "Kwf", "Kwb", "Sfb", "Sbb")]
            F0, F1, F2, F3 = [sbf(n, [128, LM], F32) for n in ("F0", "F1", "F2", "F3")]
            Pt = [sbf(f"P{i}", [128, 128], BF16) for i in range(2)]
            for h in range(4):
                lgf = lg[:, h:h + 1]
                lgb = lg[:, 4 + h:5 + h]
                kb.act(Dm, pf, AF.Exp, scale=lgf, bias=lnsc)
                kb.tt("dve", Dm, Dm, mf, ALU.mult)
                kb.act(e2, pbk_, AF.Exp, scale=lgb, bias=lnsc)
                kb.tt("dve", e2, e2, mb, ALU.mult)
                kb.tt("dve", Dm, Dm, e2, ALU.add)
                kb.act(wcol[:, 0:1], cols[:, 0:1], AF.Exp, scale=lgf)
                kb.act(wcol[:, 1:2], cols[:, 1:2], AF.Exp, scale=lgb)
                kb.act(wcol[:, 2:3], lgf, AF.Exp, scale=128.0)
                kb.act(wcol[:, 3:4], lgb, AF.Exp, scale=128.0)
                kb.act(wfT, tp1, AF.Exp, scale=lgf, bias=lnsc)
                kb.act(wbT, tm, AF.Exp, scale=lgb, bias=lnsc)
                kb.dma("sp", S0, I["sret"][l, :, h].rr("a d e -> d a e"))
                for (t0, L, is_ctx, si) in SEQS:
                    n = L // 128
                    tk = slice(t0, t0 + L)
                    kb.dma("pool", Hq[:, 0:L], self.mix[h, :, tk])
                    kb.dma("pool", Hk[:, 0:L], self.mix[4 + h, :, tk])
                    kb.dma("pool", Hv[:, 0:L], self.mix[8 + h, :, tk])
                    kb.dma("sp", F0[:, 0:L], self.mix[12 + h, :, tk])
                    for c in range(n):
                        cs_ = slice(c * 128, (c + 1) * 128)
                        kb.tr(self.pbf[:, 0:128], Hk[:, cs_], identb)
                        kb.copy("act", Ktok[:, c, :], self.pbf[:, 0:128])
                        kb.tr(self.pbf[:, 128:256], Hv[:, cs_], identb)
                        kb.copy("dve", Vtok[:, c, :], self.pbf[:, 128:256])
                    kb.ts("pool", Kwf[:, 0:n, :], Ktok[:, 0:n, :], wcol[:, 0:1], ALU.mult)
                    kb.ts("pool", Kwb[:, 0:n, :], Ktok[:, 0:n, :], wcol[:, 1:2], ALU.mult)
                    for d in range(2):
                        if is_ctx:
                            kb.memset("dve", S, 0.0)
                        else:
                            kb.copy("dve", S, S0[:, d, :])
                        Sx = Sfb if d == 0 else Sbb
                        Kw = Kwf if d == 0 else Kwb
                        order = range(n) if d == 0 else range(n - 1, -1, -1)
                        for c in order:
                            kb.copy("act", Sx[:, c, :], S)
                            pk = self.bank()
                            kb.mm(pk[:, 0:128], Kw[:, c, :], Vtok[:, c, :])
                            kb.stt("dve", S, S, wcol[:, 2 + d:3 + d], pk[:, 0:128], ALU.mult, ALU.add)
                        if is_ctx:
                            kb.dma("sp", self.O["nret"][si, l, d, h], S)
                    kb.tt("pool", Hqf[:, 0:L], Hq[:, 0:L], wfT[:, 0:L], ALU.mult)
                    kb.tt("pool", Hqb[:, 0:L], Hq[:, 0:L], wbT[:, 0:L], ALU.mult)
                    cb_ = min(512, L)
                    cnt = 0
                    for blk in range(L // cb_):
                        po = self.bank()
                        for cc in range(cb_ // 128):
                            c = blk * (cb_ // 128) + cc
                            cs_ = slice(c * 128, (c + 1) * 128)
                            psc = self.bank()
                            kb.mm(psc[:, 0:128], Hk[:, cs_], Hq[:, cs_])
                            P = Pt[cnt % 2]
                            cnt += 1
                            kb.tt("dve", P, psc[:, 0:128], Dm, ALU.mult)
                            oc = po[:, cc * 128:(cc + 1) * 128]
                            kb.mm(oc, Vtok[:, c, :], P, start=True, stop=False)
                            kb.mm(oc, Sfb[:, c, :], Hqf[:, cs_], start=False, stop=False)
                            kb.mm(oc, Sbb[:, c, :], Hqb[:, cs_], start=False, stop=True)
                        bs = slice(blk * cb_, (blk + 1) * cb_)
                        kb.copy("act", F1[:, bs], po[:, 0:cb_])
                        pm_ = self.bank()
                        kb.mm(pm_[:, 0:cb_], self.invf, F1[:, bs])
                        kb.tt("dve", F2[:, bs], F1[:, bs], pm_[:, 0:cb_], ALU.subtract)
                        kb.act(F3[:, bs], F2[:, bs], AF.Square)
                        pv = self.bank()
                        kb.mm(pv[:, 0:cb_], self.invf, F3[:, bs])
                        kb.ts("dve", F3[:, bs], pv[:, 0:cb_], EPS, ALU.add)
                        kb.act(F3[:, bs], F3[:, bs], AF.Sqrt)
                        kb.recip(F3[:, bs], F3[:, bs])
                        kb.tt("dve", F2[:, bs], F2[:, bs], F3[:, bs], ALU.mult)
                        kb.act(F0[:, bs], F0[:, bs], AF.Silu)
                        kb.stt("dve", Hout[:, bs], F2[:, bs], pt2[:, h:h + 1], F0[:, bs], ALU.mult, ALU.mult)
                    kb.dma("sp", self.oT[h, :, tk], Hout[:, 0:L])
            kb.barrier()

    def gelu_tanh(self, out, x, t1, e="dve"):
        kb = self.kb
        kb.act(t1, x, AF.Square)
        kb.ts(e, t1, t1, 0.044715, ALU.mult, 1.0, ALU.add)
        kb.tt(e, t1, t1, x, ALU.mult)
        kb.act(t1, t1, AF.Sigmoid, scale=1.5957691216057308)
        kb.tt(e, out, x, t1, ALU.mult)

    def B_lru(self, l, pt2):
        kb = self.kb
        I = self.I
        LM = NST
        with contextlib.ExitStack() as st:
            sbf = lambda n, s, d: kb.sb(st, "l_" + n, s, d)
            nls = sbf("nls", [128, 8], F32)
            kb.act(nls, pt2[:, 40:48], AF.Exp, scale=-1.0)
            kb.ts("dve", nls, nls, 1.0, ALU.add)
            kb.act(nls, nls, AF.Ln)
            kb.ts("dve", nls, nls, -8.0, ALU.mult)
            wa = sbf("wa", [128, 2, 128], BF16)
            wx = sbf("wx", [128, 2, 128], BF16)
            F = [sbf(f"F{i}", [128, LM], F32) for i in range(9)]
            H0 = sbf("H0", [128, LM], BF16)
            H1 = sbf("H1", [128, LM], BF16)
            fin = sbf("fin", [128, 2], F32)
            for nb in range(4):
                kb.dma("pool", wa, I["lru_wa"][l, :, nb].rr("a d e -> d a e"))
                kb.dma("pool", wx, I["lru_wx"][l, :, nb].rr("a d e -> d a e"))
                cw = lambda j: pt2[:, 4 + j * 4 + nb: 5 + j * 4 + nb]
                for (t0, L, is_ctx, si) in SEQS:
                    tk = slice(t0, t0 + L)
                    x, gate, xc = F[0][:, 0:L], F[1][:, 0:L], F[2][:, 0:L]
                    kb.dma("sp", x, self.mix[16 + nb, :, tk])
                    kb.dma("sp", gate, self.mix[20 + nb, :, tk])
                    kb.ts("dve", xc, x, cw(2), ALU.mult, pt2[:, 20 + nb:21 + nb], ALU.add)
                    kb.stt("dve", xc[:, 2:L], x[:, 0:L - 2], cw(0), xc[:, 2:L], ALU.mult, ALU.add)
                    kb.stt("dve", xc[:, 1:L], x[:, 0:L - 1], cw(1), xc[:, 1:L], ALU.mult, ALU.add)
                    kb.stt("dve", xc[:, 0:L - 1], x[:, 1:L], cw(3), xc[:, 0:L - 1], ALU.mult, ALU.add)
                    kb.copy("act", H0[:, 0:L], xc)
                    cb_ = min(512, L)
                    hs = [F[7][:, 0:L], F[8][:, 0:L]]
                    for d in range(2):
                        r, ig, a, sq_ = F[3][:, 0:L], F[4][:, 0:L], F[5][:, 0:L], F[6][:, 0:L]
                        for blk in range(L // cb_):
                            bs = slice(blk * cb_, (blk + 1) * cb_)
                            p1 = self.bank()
                            kb.mm(p1[:, 0:cb_], wa[:, d, :], H0[:, bs])
                            kb.act(r[:, bs], p1[:, 0:cb_], AF.Sigmoid, bias=pt2[:, 24 + d * 4 + nb:25 + d * 4 + nb])
                            p2 = self.bank()
                            kb.mm(p2[:, 0:cb_], wx[:, d, :], H0[:, bs])
                            kb.act(ig[:, bs], p2[:, 0:cb_], AF.Sigmoid, bias=pt2[:, 32 + d * 4 + nb:33 + d * 4 + nb])
                        kb.act(a, r, AF.Exp, scale=nls[:, d * 4 + nb:d * 4 + nb + 1])
                        kb.tt("dve", sq_, a, a, ALU.mult)
                        kb.ts("dve", sq_, sq_, -1.0, ALU.mult, 1.0, ALU.add)
                        kb.act(sq_, sq_, AF.Relu)
                        kb.act(sq_, sq_, AF.Sqrt)
                        kb.tt("dve", ig, ig, xc, ALU.mult)
                        kb.tt("dve", ig, ig, sq_, ALU.mult)
                        if is_ctx:
                            init = 0.0
                        else:
                            init = pt2[:, 58 + d * 4 + nb:59 + d * 4 + nb]
                        hd = hs[d]
                        if d == 0:
                            kb.scan(hd, a, ig, init)
                            kb.copy("act", fin[:, 0:1], hd[:, L - 1:L])
                        else:
                            kb.scan(hd.rev(), a.rev(), ig.rev(), init)
                            kb.copy("act", fin[:, 1:2], hd[:, 0:1])
                    if is_ctx:
                        kb.dma("sp", self.O["nlru"][si, l, :, nb * 128:(nb + 1) * 128].rr("d p -> p d"), fin, allow_slow_non_contiguous=True)
                    kb.tt("dve", hs[0], hs[0], hs[1], ALU.add)
                    self.gelu_tanh(F[4][:, 0:L], gate, F[3][:, 0:L], e="pool")
                    kb.tt("dve", H1[:, 0:L], F[4][:, 0:L], hs[0], ALU.mult)
                    kb.dma("sp", self.oT[4 + nb, :, tk], H1[:, 0:L])
            kb.barrier()

    def rms_feat(self, x, L, gain, w1):
        kb = self.kb
        cb_ = min(512, L)
        for blk in range(L // cb_):
            bs = slice(blk * cb_, (blk + 1) * cb_)
            kb.act(w1[:, bs], x[:, bs], AF.Square)
            pm_ = self.bank()
            kb.mm(pm_[:, 0:cb_], self.invf, w1[:, bs])
            kb.ts("dve", w1[:, bs], pm_[:, 0:cb_], EPS, ALU.add)
            kb.act(w1[:, bs], w1[:, bs], AF.Sqrt)
            kb.recip(w1[:, bs], w1[:, bs])
            kb.stt("dve", x[:, bs], x[:, bs], gain, w1[:, bs], ALU.mult, ALU.mult)

    def B_att(self, l, pt2, bc, identb):
        kb = self.kb
        I = self.I
        LM = NST
        SC = 128.0 ** -0.5
        with contextlib.ExitStack() as st:
            sbf = lambda n, s, d: kb.sb(st, "t_" + n, s, d)
            ropec = sbf("ropec", [128, LM], F32)
            ropes = sbf("ropes", [128, LM], F32)
            perm = sbf("perm", [128, 128], F32)
            mfb = sbf("mfb", [128, 128], BF16)
            mbb = sbf("mbb", [128, 128], BF16)
            kb.dma("sp", ropec, I["k_ropec"])
            kb.dma("sp", ropes, I["k_ropes"])
            kb.dma("sp", perm, I["k_perm"])
            kb.dma("pool", mfb, I["k_mf"])
            kb.dma("pool", mbb, I["k_mb"])
            esink = sbf("esink", [128, 4], F32)
            kb.act(esink, bc[:, 8:12], AF.Exp)
            Q = [sbf(f"Q{i}", [128, LM], F32) for i in range(2)]
            K = sbf("K", [128, LM], F32)
            V = sbf("V", [128, LM], F32)
            W1 = sbf("W1", [128, LM], F32)
            W2 = sbf("W2", [128, LM], F32)
            Hq = [sbf(f"Hq{i}", [128, LM], BF16) for i in range(2)]
            Hk = sbf("Hk", [128, LM], BF16)
            Ho = sbf("Ho", [128, LM], BF16)
            vtok = sbf("vtok", [128, LM // 128, 128], BF16)
            kst = sbf("kst", [128, 2, 128], F32)
            vst = sbf("vst", [128, 2, 128], F32)
            cks = sbf("cks", [128, 2, 128], F32)
            ckT = sbf("ckT", [128, 256], BF16)
            cvb = sbf("cvb", [128, 2, 128], BF16)
            PL = sbf("PL", [128, LM // 128, 384], BF16)
            PC = sbf("PC", [128, 2, LM], BF16)
            den = sbf("den", [128, 512], F32)
            for kvh in range(2):
                kb.dma("sp", cks, I["ck"][l, :, kvh, :].rr("(c t) d -> t c d", t=128))
                kb.dma("pool", cvb, I["cv"][l, :, kvh, :].rr("(c t) d -> t c d", t=128))
                for c in range(2):
                    pb = self.bank()
                    kb.tr(pb[:, 0:128], cks[:, c, :], self.ident)
                    kb.copy("act", ckT[:, c * 128:(c + 1) * 128], pb[:, 0:128])
                for (t0, L, is_ctx, si) in SEQS:
                    n = L // 128
                    tk = slice(t0, t0 + L)
                    cb_ = min(512, L)
                    nblk = L // cb_
                    kb.dma("sp", Q[0][:, 0:L], self.mix[24 + 2 * kvh, :, tk])
                    kb.dma("sp", Q[1][:, 0:L], self.mix[25 + 2 * kvh, :, tk])
                    kb.dma("sp", K[:, 0:L], self.mix[28 + kvh, :, tk])
                    kb.dma("sp", V[:, 0:L], self.mix[30 + kvh, :, tk])
                    self.rms_feat(Q[0], L, pt2[:, 56:57], W1)
                    self.rms_feat(Q[1], L, pt2[:, 56:57], W1)
                    self.rms_feat(K, L, pt2[:, 57:58], W1)
                    if is_ctx:
                        for c in range(n):
                            cs_ = slice(c * 128, (c + 1) * 128)
                            pb = self.bank()
                            kb.tr(pb[:, 0:128], K[:, cs_], self.ident)
                            kb.copy("act", kst[:, c, :], pb[:, 0:128])
                            pb2 = self.bank()
                            kb.tr(pb2[:, 0:128], V[:, cs_], self.ident)
                            kb.copy("dve", vst[:, c, :], pb2[:, 0:128])
                        kb.copy("act", vtok[:, 0:n, :], vst[:, 0:n, :])
                        kb.dma("sp", self.O["nk"][si, l, :, kvh, :].rr("(c t) d -> t c d", t=128), kst)
                        kb.dma("sp", self.O["nv"][si, l, :, kvh, :].rr("(c t) d -> t c d", t=128), vst)
                    else:
                        for X in (Q[0], Q[1], K):
                            for blk in range(nblk):
                                bs = slice(blk * cb_, (blk + 1) * cb_)
                                pp = self.bank()
                                kb.mm(pp[:, 0:cb_], perm, X[:, bs])
                                kb.tt("pool", W1[:, bs], X[:, bs], ropec[:, bs], ALU.mult)
                                kb.tt("dve", W2[:, bs], pp[:, 0:cb_], ropes[:, bs], ALU.mult)
                                kb.tt("pool", X[:, bs], W1[:, bs], W2[:, bs], ALU.add)
                        for c in range(n):
                            pb = self.bank()
                            kb.tr(pb[:, 0:128], V[:, c * 128:(c + 1) * 128], self.ident)
                            kb.copy("act" if c % 2 else "dve", vtok[:, c, :], pb[:, 0:128])
                    kb.copy("act", Hq[0][:, 0:L], Q[0][:, 0:L])
                    kb.copy("act", Hq[1][:, 0:L], Q[1][:, 0:L])
                    kb.copy("dve", Hk[:, 0:L], K[:, 0:L])
                    for hq in range(2):
                        qh = 2 * kvh + hq
                        q = Hq[hq]
                        es = esink[:, qh:qh + 1]
                        if is_ctx:
                            for c in range(2):
                                ps_ = self.bank()
                                kb.mm(ps_[:, 0:L], Hk[:, c * 128:(c + 1) * 128], q[:, 0:L])
                                kb.act(PC[:, c, 0:L], ps_[:, 0:L], AF.Exp, scale=SC)
                            po = self.bank()
                            pd = self.bank()
                            for c in range(2):
                                kb.mm(po[:, 0:L], vtok[:, c, :], PC[:, c, 0:L], start=(c == 0), stop=(c == 1))
                                kb.mm(pd[:, 0:L], self.ones_bf, PC[:, c, 0:L], start=(c == 0), stop=(c == 1))
                            kb.ts("dve", den[:, 0:L], pd[:, 0:L], es, ALU.add)
                            kb.recip(den[:, 0:L], den[:, 0:L])
                            kb.tt("dve", Ho[:, 0:L], po[:, 0:L], den[:, 0:L], ALU.mult)
                        else:
                            for jc in range(n):
                                b0 = max(jc - 1, 0)
                                b1 = min(jc + 1, n - 1)
                                ncol = (b1 - b0 + 1) * 128
                                s0 = (b0 - (jc - 1)) * 128
                                ps_ = self.bank()
                                kb.mm(ps_[:, 0:ncol], Hk[:, jc * 128:(jc + 1) * 128], q[:, b0 * 128:(b1 + 1) * 128])
                                kb.act(PL[:, jc, s0:s0 + ncol], ps_[:, 0:ncol], AF.Exp, scale=SC)
                                if jc >= 1:
                                    kb.tt("dve", PL[:, jc, 0:128], PL[:, jc, 0:128], mfb, ALU.mult)
                                if jc <= n - 2:
                                    kb.tt("dve", PL[:, jc, 256:384], PL[:, jc, 256:384], mbb, ALU.mult)
                            for c in range(2):
                                for blk in range(nblk):
                                    bs = slice(blk * cb_, (blk + 1) * cb_)
                                    ps_ = self.bank()
                                    kb.mm(ps_, ckT[:, c * 128:(c + 1) * 128], q[:, bs])
                                    kb.act(PC[:, c, bs], ps_, AF.Exp, scale=SC)
                            for blk in range(nblk):
                                bs = slice(blk * cb_, (blk + 1) * cb_)
                                po = self.bank()
                                pd = self.bank()
                                for ii in range(4):
                                    i = blk * 4 + ii
                                    terms = []
                                    for jc in (i - 1, i, i + 1):
                                        if 0 <= jc < n:
                                            sl = (i - (jc - 1)) * 128
                                            terms.append((vtok[:, jc, :], PL[:, jc, sl:sl + 128]))
                                    for c in range(2):
                                        terms.append((cvb[:, c, :], PC[:, c, i * 128:(i + 1) * 128]))
                                    cs_ = slice(ii * 128, (ii + 1) * 128)
                                    for idx, (lh, rh) in enumerate(terms):
                                        kb.mm(po[:, cs_], lh, rh, start=(idx == 0), stop=(idx == len(terms) - 1))
                                        kb.mm(pd[:, cs_], self.ones_bf, rh, start=(idx == 0), stop=(idx == len(terms) - 1))
                                kb.ts("dve", den, pd, es, ALU.add)
                                kb.recip(den, den)
                                kb.tt("dve", Ho[:, bs], po, den, ALU.mult)
                        kb.dma("sp", self.oT[8 + qh, :, tk], Ho[:, 0:L])
            kb.barrier()

    def B_ssm(self, l, pt2, identb):
        kb = self.kb
        I = self.I
        LM = NST
        with contextlib.ExitStack() as st:
            sbf = lambda n, s, d: kb.sb(st, "s_" + n, s, d)
            stg = sbf("stg", [128, 128], F32)
            kb.dma("sp", stg[0:32, :], I["ssm_a_re"][l].rr("d (q g) p -> (d q) (g p)", g=2))
            kb.dma("sp", stg[32:64, :], I["ssm_a_im"][l].rr("d (q g) p -> (d q) (g p)", g=2))
            kb.dma("sp", stg[64:96, :], I["sre"][l].rr("d (q g) p -> (d q) (g p)", g=2))
            kb.dma("sp", stg[96:128, :], I["sim"][l].rr("d (q g) p -> (d q) (g p)", g=2))
            ct = sbf("ct", [128, 128], F32)
            pb = self.bank()
            kb.tr(pb[:, 0:128], stg, self.ident)
            kb.copy("dve", ct, pb[:, 0:128])
            dt = sbf("dt", [128, 32], F32)
            ld = I["ssm_log_dt"]
            for d in range(2):
                for g in range(2):
                    kb.dma("sp", dt[g * 64:(g + 1) * 64, d * 16:(d + 1) * 16],
                           T(ld.ap[l, d, g::2].partition_broadcast(64), ld.dep), allow_slow_non_contiguous=True)
            kb.act(dt, dt, AF.Exp)
            are, aim, h0r, h0i = ct[:, 0:32], ct[:, 32:64], ct[:, 64:96], ct[:, 96:128]
            tb = lambda n: sbf(n, [128, 32], F32)
            mag, fr, sn, cs, t1, t2, t3, wre, wim, nwim, gir, gii = [tb(n) for n in
                ("mag", "fr", "sn", "cs", "t1", "t2", "t3", "wre", "wim", "nwim", "gir", "gii")]
            hpi = sbf("hpi", [128, 1], F32)
            kb.memset("dve", hpi, float(math.pi / 2))
            kb.tt("dve", mag, are, dt, ALU.mult)
            kb.act(mag, mag, AF.Exp)
            kb.tt("dve", fr, aim, dt, ALU.mult)
            kb.ts("dve", fr, fr, float(1.0 / TWO_PI), ALU.mult)
            kb.ts("dve", t1, fr, MAGIC, ALU.add, MAGIC, ALU.subtract)
            kb.tt("dve", fr, fr, t1, ALU.subtract)
            kb.act(sn, fr, AF.Sin, scale=float(TWO_PI))
            kb.act(t1, fr, AF.Abs)
            kb.act(cs, t1, AF.Sin, scale=float(-TWO_PI), bias=hpi)
            kb.tt("dve", t1, mag, cs, ALU.mult)
            kb.ts("dve", t1, t1, -1.0, ALU.add)
            kb.tt("dve", t2, mag, sn, ALU.mult)
            kb.tt("dve", t3, are, are, ALU.mult)
            kb.tt("dve", wre, aim, aim, ALU.mult)
            kb.tt("dve", t3, t3, wre, ALU.add)
            kb.recip(t3, t3)
            kb.tt("dve", wre, t1, are, ALU.mult)
            kb.tt("dve", wim, t2, aim, ALU.mult)
            kb.tt("dve", wre, wre, wim, ALU.add)
            kb.tt("dve", wre, wre, t3, ALU.mult)
            kb.tt("dve", wim, t2, are, ALU.mult)
            kb.tt("dve", nwim, t1, aim, ALU.mult)
            kb.tt("dve", wim, wim, nwim, ALU.subtract)
            kb.tt("dve", wim, wim, t3, ALU.mult)
            kb.ts("dve", nwim, wim, -1.0, ALU.mult)
            kb.tt("dve", gir, cs, h0r, ALU.mult)
            kb.tt("dve", t1, sn, h0i, ALU.mult)
            kb.tt("dve", gir, gir, t1, ALU.subtract)
            kb.tt("dve", gii, sn, h0r, ALU.mult)
            kb.tt("dve", t1, cs, h0i, ALU.mult)
            kb.tt("dve", gii, gii, t1, ALU.add)
            Bre = sbf("Bre", [128, 32, 16], F32)
            Bim = sbf("Bim", [128, 32, 16], F32)
            for d in range(2):
                kb.dma("sp", Bre[:, d * 16:(d + 1) * 16, :], I["ssm_b_re"][l, d].rr("(q g) p c -> (g p) q c", g=2))
                kb.dma("sp", Bim[:, d * 16:(d + 1) * 16, :], I["ssm_b_im"][l, d].rr("(q g) p c -> (g p) q c", g=2))
            Zre = [sbf(f"Zre{r}", [128, 128], F32) for r in range(4)]
            Zim = [sbf(f"Zim{r}", [128, 128], F32) for r in range(4)]
            LCre = [sbf(f"LCre{r}", [128, 128], BF16) for r in range(4)]
            LCim = [sbf(f"LCim{r}", [128, 128], BF16) for r in range(4)]
            for r in range(4):
                kb.memset("dve", Zre[r], 0.0)
                kb.memset("dve", Zim[r], 0.0)
                kb.memset("dve", LCre[r], 0.0)
                kb.memset("dve", LCim[r], 0.0)
            Zc_re = sbf("Zc_re", [32, 128], F32)
            Zc_im = sbf("Zc_im", [32, 128], F32)
            kb.memset("dve", Zc_re, 0.0)
            kb.memset("dve", Zc_im, 0.0)
            LBre = sbf("LBre", [128, 128], BF16)
            LBim = sbf("LBim", [128, 128], BF16)
            tpos = sbf("tpos", [128, LM], F32)
            kb.dma("sp", tpos, I["k_tpos"])
            CS = sbf("CS", [128, LM], F32)
            SN = sbf("SN", [128, LM], F32)
            X1, X2, BA, BB = [sbf(n, [128, NTOK], F32) for n in ("X1", "X2", "BA", "BB")]
            G1, HR, HI = BA, X2, X1
            hrb = sbf("hrb", [128, NTOK], BF16)
            hib = sbf("hib", [128, NTOK], BF16)
            SQ = []
            for (t0_, L_, c_, si_) in SEQS:
                tk_ = slice(t0_, t0_ + L_)
                SQ.append({n_: kb.sub(t_[:, tk_]) for n_, t_ in (("X1", X1), ("X2", X2), ("A", BA), ("B", BB), ("hrb", hrb), ("hib", hib))})
            fcols = [sbf(f"fcol{i}", [128, 2], F32) for i in range(5)]
            TG1 = SQ[4]["A"]
            TG2 = SQ[4]["B"]
            SU = sbf("SU", [128, NTOK], BF16)
            Y = sbf("Y", [128, NTOK], F32)
            ybt = [sbf(f"ybt{i}", [128, 4, TT], BF16) for i in range(2)]
            ybs = hrb
            FIN = sbf("FIN", [128, 4, 2, 32], F32)
            fcol = sbf("fcol", [128, 2], F32)
            wglu = sbf("wglu", [128, 4, 512], BF16)
            kb.dma("pool", wglu, I["ssm_w_glu"][l].rr("(k p) c -> p k c", p=128))
            for j in range(4):
                kb.dma("pool", SU, self.mix[32 + j, :, :])
                first = True
                for d in range(2):
                    for r in range(4):
                        q = 4 * j + r
                        dq = d * 16 + q
                        for g in range(2):
                            rows = slice(g * 64, (g + 1) * 64)
                            cols = slice(r * 32 + g * 16, r * 32 + g * 16 + 16)
                            kb.ts("dve", Zre[r][rows, cols], Bre[rows, dq, :], wre[rows, dq:dq + 1], ALU.mult)
                            kb.stt("dve", Zre[r][rows, cols], Bim[rows, dq, :], nwim[rows, dq:dq + 1], Zre[r][rows, cols], ALU.mult, ALU.add)
                            kb.ts("dve", Zim[r][rows, cols], Bim[rows, dq, :], wre[rows, dq:dq + 1], ALU.mult)
                            kb.stt("dve", Zim[r][rows, cols], Bre[rows, dq, :], wim[rows, dq:dq + 1], Zim[r][rows, cols], ALU.mult, ALU.add)
                        pz = self.bank()
                        kb.tr(pz[:, 0:128], Zre[r], self.ident)
                        kb.copy("act", LBre, pz[:, 0:128])
                        pz = self.bank()
                        kb.tr(pz[:, 0:128], Zim[r], self.ident)
                        kb.copy("act", LBim, pz[:, 0:128])
                        for g in range(2):
                            kb.dma("sp", Zc_re[g * 16:(g + 1) * 16, g * 64:(g + 1) * 64], I["ssm_c_re"][l, d, 2 * q + g])
                            kb.dma("sp", Zc_im[g * 16:(g + 1) * 16, g * 64:(g + 1) * 64], I["ssm_c_im"][l, d, 2 * q + g])
                        pz = self.bank()
                        kb.tr(pz[:, 0:32], Zc_re, self.ident[0:32, 0:32])
                        kb.copy("act", LCre[r][:, r * 32:(r + 1) * 32], pz[:, 0:32])
                        pz = self.bank()
                        kb.tr(pz[:, 0:32], Zc_im, self.ident[0:32, 0:32])
                        kb.ts("dve", LCim[r][:, r * 32:(r + 1) * 32], pz[:, 0:32], -1.0, ALU.mult)
                        kb.ts("dve", TG1, tpos, fr[:, dq:dq + 1], ALU.mult)
                        kb.ts("dve", TG2, TG1, MAGIC, ALU.add, MAGIC, ALU.subtract)
                        kb.tt("dve", TG1, TG1, TG2, ALU.subtract)
                        kb.act(SN, TG1, AF.Sin, scale=float(TWO_PI))
                        kb.act(TG2, TG1, AF.Abs)
                        kb.act(CS, TG2, AF.Sin, scale=float(-TWO_PI), bias=hpi)
                        chains = []
                        for (t0, L, is_ctx, si) in SEQS:
                            chains.append(self._ssm_chain(d, r, dq, t0, L, is_ctx, si, first, SQ[si if is_ctx else 4],
                                                          CS, SN, LBre, LBim, LCre, LCim, SU, Y, mag, gir, gii, FIN, fcols[si if is_ctx else 4]))
                        alive = list(chains)
                        while alive:
                            nxt = []
                            for g in alive:
                                try:
                                    next(g)
                                    nxt.append(g)
                                except StopIteration:
                                    pass
                            alive = nxt
                        first = False
                for (t0, L, is_ctx, si) in SEQS:
                    tk = slice(t0, t0 + L)
                    Bs = SQ[si if is_ctx else 4]
                    kb.dma("sp", Bs["X2"], self.mix[32 + j, :, tk])
                    kb.stt("dve", Y[:, tk], Bs["X2"], pt2[:, 48 + j:49 + j], Y[:, tk], ALU.mult, ALU.add)
                    self.gelu_tanh(Bs["hrb"], Y[:, tk], Bs["X1"])
                    kb.dma("sp", self.yb[j, :, tk], Bs["hrb"])
            for blk in range(NT):
                bs = slice(blk * TT, (blk + 1) * TT)
                yt = ybt[blk % 2]
                kb.dma("sp", yt, self.yb[:, :, bs].rr("k p t -> p k t"))
                for m in range(4):
                    pg = self.bank()
                    for kk in range(4):
                        kb.mm(pg, wglu[:, kk, m * 128:(m + 1) * 128], yt[:, kk, :], start=(kk == 0), stop=(kk == 3))
                    gsl = SQ[4]["A"][:, m * TT:(m + 1) * TT]
                    kb.act(gsl, pg, AF.Sigmoid, bias=pt2[:, 52 + m:53 + m])
                    osl = SQ[4]["hrb"][:, m * TT:(m + 1) * TT]
                    kb.tt("dve", osl, yt[:, m, :], gsl, ALU.mult)
                    kb.dma("sp", self.oT[12 + m, :, bs], osl)
            for si in range(4):
                for c in range(2):
                    pz = self.bank()
                    kb.tr(pz[0:32, 0:128], FIN[:, si, c, :], self.ident)
                    kb.copy("act", stg[0:32, :], pz[0:32, 0:128])
                    dst = self.O["nre" if c == 0 else "nim"][si, l].rr("d (q g) p -> (d q) (g p)", g=2)
                    kb.dma("sp", dst, stg[0:32, :])
            kb.barrier()


    def _ssm_chain(self, d, r, dq, t0, L, is_ctx, si, first, B, CS, SN, LBre, LBim, LCre, LCim, SU, Y, mag, gir, gii, FIN, fcol):
        kb = self.kb
        cb_ = min(512, L)
        nblk = L // cb_
        if d == 0:
            CSf, SNf = CS[:, 0:L], SN[:, 0:L]
        else:
            CSf, SNf = CS[:, 0:L].rev(), SN[:, 0:L].rev()
        X1, X2, A, Bq, hrb, hib = (B[k] for k in ("X1", "X2", "A", "B", "hrb", "hib"))
        for blk in range(nblk):
            bs = slice(blk * cb_, (blk + 1) * cb_)
            ts_ = slice(t0 + blk * cb_, t0 + (blk + 1) * cb_)
            pr = self.bank()
            pi_ = self.bank()
            kb.mm(pr[:, 0:cb_], LBre, SU[:, ts_])
            kb.mm(pi_[:, 0:cb_], LBim, SU[:, ts_])
            kb.copy("act", X1[:, bs], pr[:, 0:cb_])
            kb.copy("act", X2[:, bs], pi_[:, 0:cb_])
        yield
        kb.tt("dve", A, X1, CSf, ALU.mult)
        kb.tt("dve", Bq, X2, SNf, ALU.mult)
        yield
        kb.tt("dve", A, A, Bq, ALU.add)
        yield
        kb.tt("dve", Bq, X2, CSf, ALU.mult)
        kb.tt("dve", X1, X1, SNf, ALU.mult)
        yield
        kb.tt("dve", Bq, Bq, X1, ALU.subtract)
        yield
        magb = mag[:, dq:dq + 1].bc([128, L])
        ir = 0.0 if is_ctx else gir[:, dq:dq + 1]
        ii_ = 0.0 if is_ctx else gii[:, dq:dq + 1]
        if d == 0:
            kb.scan(X2, magb, A, ir)
            kb.scan(X1, magb, Bq, ii_)
        else:
            kb.scan(X2.rev(), magb, A.rev(), ir)
            kb.scan(X1.rev(), magb, Bq.rev(), ii_)
        yield
        kb.tt("dve", A, X2, CSf, ALU.mult)
        kb.tt("dve", Bq, X1, SNf, ALU.mult)
        yield
        kb.tt("dve", hrb, A, Bq, ALU.subtract)
        lc = L - 1 if d == 0 else 0
        if is_ctx:
            kb.tt("dve", FIN[:, si, 0, dq:dq + 1], A[:, lc:lc + 1], Bq[:, lc:lc + 1], ALU.subtract)
        yield
        kb.tt("dve", A, X1, CSf, ALU.mult)
        kb.tt("dve", Bq, X2, SNf, ALU.mult)
        yield
        kb.tt("dve", hib, A, Bq, ALU.add)
        if is_ctx:
            kb.tt("dve", FIN[:, si, 1, dq:dq + 1], A[:, lc:lc + 1], Bq[:, lc:lc + 1], ALU.add)
        yield
        for blk in range(nblk):
            bs = slice(blk * cb_, (blk + 1) * cb_)
            ts_ = slice(t0 + blk * cb_, t0 + (blk + 1) * cb_)
            py = self.bank()
            kb.mm(py[:, 0:cb_], LCre[r], hrb[:, bs], start=True, stop=False)
            kb.mm(py[:, 0:cb_], LCim[r], hib[:, bs], start=False, stop=True)
            if first:
                kb.copy("act", Y[:, ts_], py[:, 0:cb_])
            else:
                kb.tt("dve", Y[:, ts_], Y[:, ts_], py[:, 0:cb_], ALU.add)
        yield

    def run(self):
        self.setup()
        self.phase0()
        for l in range(self.depth):
            self.phaseM(l)
            self.phaseA(l)
            self.phaseB(l)
            self.phaseC(l, last=(l == self.depth - 1))
        outs = list(self.O.values())
        self.kb.barrier()


def build_program(depth=DEPTH, debug=False):
    nc = bass.Bass("TRN2", target_bir_lowering=False)
    with contextlib.ExitStack() as stack:
        kb = KB(nc, stack)
        prog = Prog(kb, stack, depth, debug)
        prog.run()
    return nc, kb


_CACHE = {}


def make_in_maps(inp):
    consts = host_consts()
    f = lambda a: np.ascontiguousarray(np.asarray(a, dtype=np.float32))
    shared = {}
    for n in ("w_mod", "b_mod", "norm1", "w_in", "ret_decay_logit", "ret_gn", "lru_conv_w", "lru_conv_b", "lru_wa", "lru_ba",
              "lru_wx", "lru_bx", "lru_lambda", "att_q_norm", "att_k_norm", "att_sink", "ssm_a_re", "ssm_a_im", "ssm_log_dt",
              "ssm_b_re", "ssm_b_im", "ssm_c_re", "ssm_c_im", "ssm_d", "ssm_w_glu", "ssm_b_glu", "w_out", "norm2", "w_ff1", "w_ff2"):
        shared[n] = f(inp[n])
    shared["w_br"] = f(inp["w_br"]).reshape(DEPTH * 4, 512, D)
    shared.update(consts)
    xp = f(inp["x_prompt"])
    xs = f(inp["x_sample"])
    maps = []
    for c in range(8):
        m = dict(shared)
        m["xp"] = xp[4 * c:4 * c + 4].reshape(NPT, D)
        m["xs"] = xs[c]
        m["ck"] = f(inp["cache_attn_k"][c])
        m["cv"] = f(inp["cache_attn_v"][c])
        m["sret"] = f(inp["state_ret"][c])
        m["slru"] = f(inp["state_lru"][c])
        m["sre"] = f(inp["state_ssm_re"][c])
        m["sim"] = f(inp["state_ssm_im"][c])
        m["cvec"] = np.stack([f(inp["c_ctx"]), f(inp["c"])[c]], axis=0)
        maps.append(m)
    return maps


def kernel(**inputs):
    if "nc" not in _CACHE:
        _CACHE["nc"] = build_program()[0]
    nc = _CACHE["nc"]
    maps = make_in_maps(inputs)
    res = run_bass_kernel_spmd(nc, maps, core_ids=list(range(8)))
    R = res.results
    y_prompt = np.concatenate([r["yp"].reshape(4, 256, D) for r in R], axis=0)
    y_sample = np.stack([r["ys"] for r in R], axis=0)
    cat = lambda n: np.concatenate([r[n] for r in R], axis=0)
    return (y_prompt, y_sample, cat("nk"), cat("nv"), cat("nret"), cat("nlru"), cat("nre"), cat("nim"))
```

```python
import contextlib
import math
import numpy as np
import concourse.bass as bass
import concourse.mybir as mybir
from concourse.bass_utils import run_bass_kernel_spmd

F32 = mybir.dt.float32
BF16 = mybir.dt.bfloat16
ALU = mybir.AluOpType
AF = mybir.ActivationFunctionType

SEM_CAP = 30000
N_DMA_SEMS = 150

DEPTH = 4
D = 2048
KC = 16
NPT = 1024
NST = 2048
NTOK = NPT + NST
TT = 512
NT = NTOK // TT
MIXC = 36
GATE0 = 4608
EPS = 1e-6
MAGIC = 12582912.0
TWO_PI = 2.0 * math.pi
SEQS = [(0, 256, True, 0), (256, 256, True, 1), (512, 256, True, 2), (768, 256, True, 3), (1024, 2048, False, 0)]


class Dep:
    __slots__ = ("w", "r", "dsem")

    def __init__(self):
        self.w = {}
        self.r = {}
        self.dsem = None


class T:
    __slots__ = ("ap", "dep")

    def __init__(self, ap, dep):
        self.ap = ap
        self.dep = dep

    def __getitem__(self, k):
        return T(self.ap[k], self.dep)

    def rr(self, s, **kw):
        return T(self.ap.rearrange(s, **kw), self.dep)

    def bc(self, shape):
        return T(self.ap.to_broadcast(list(shape)), self.dep)

    def rev(self):
        nd = len(self.ap.shape)
        idx = tuple([slice(None)] * (nd - 1) + [slice(None, None, -1)])
        return T(self.ap[idx], self.dep)


class KB:
    def __init__(self, nc, stack):
        self.nc = nc
        self.stack = stack
        self.eng = {"pe": nc.tensor, "act": nc.scalar, "dve": nc.vector, "pool": nc.gpsimd, "sp": nc.sync}
        self.sems = {}
        self.nsem = 0
        self.cur = {}
        self.epoch = {e: 0 for e in self.eng}
        self.seen = {e: {} for e in self.eng}
        self.deps = []
        self.ninst = {e: 0 for e in self.eng}
        self.dma_pool = [("d", i) for i in range(N_DMA_SEMS)]
        self.dma_cnt = {k: 0 for k in self.dma_pool}
        self.dma_free = list(self.dma_pool)
        for e in self.eng:
            self._new_engine_sem(e)

    def _sem(self, key):
        if key not in self.sems:
            self.sems[key] = self.stack.enter_context(self.nc.semaphore(f"s{self.nsem}"))
            self.nsem += 1
        return self.sems[key]

    def _new_engine_sem(self, e):
        key = ("e", e, self.epoch[e])
        self._sem(key)
        self.cur[e] = [key, 0]

    def newdep(self):
        d = Dep()
        self.deps.append(d)
        return d

    def sb(self, stack, name, shape, dtype):
        self.nalloc = getattr(self, "nalloc", 0) + 1
        name = f"{name}_{self.nalloc}"
        t = stack.enter_context(self.nc.sbuf_tensor(name, list(shape), dtype))
        return T(t.ap(), self.newdep())

    def ps(self, stack, name, shape, dtype):
        t = stack.enter_context(self.nc.psum_tensor(name, list(shape), dtype))
        return T(t.ap(), self.newdep())

    def dram(self, name, shape, dtype, kind="Internal"):
        t = self.nc.dram_tensor(name, list(shape), dtype, kind=kind)
        return T(t.ap(), self.newdep())

    def sub(self, t):
        return T(t.ap, self.newdep())

    def _wait(self, e, evs):
        seen = self.seen[e]
        need = {}
        for k, v in evs:
            if seen.get(k, 0) < v and need.get(k, 0) < v:
                need[k] = v
        for k, v in need.items():
            seen[k] = v
            if k[0] == "e" and k[1] == e and e == "pe":
                continue
            self.eng[e].wait_ge(self._sem(k), v)

    def _collect(self, e, reads, writes, compute):
        evs = []
        for t in reads:
            evs.extend(t.dep.w.items())
        for t in writes:
            d = t.dep
            for k, v in d.w.items():
                if compute and k[0] == "e" and k[1] == e:
                    continue
                evs.append((k, v))
            for k, v in d.r.items():
                if compute and k[0] == "e" and k[1] == e:
                    continue
                evs.append((k, v))
        return evs

    def _record(self, ev, reads, writes):
        k, v = ev
        for t in reads:
            t.dep.r[k] = v
        for t in writes:
            t.dep.w[k] = v

    def op(self, e, fn, reads=(), writes=()):
        self._wait(e, self._collect(e, reads, writes, True))
        ins = fn()
        if self.cur[e][1] >= SEM_CAP:
            self.epoch[e] += 1
            self._new_engine_sem(e)
        key = self.cur[e][0]
        self.cur[e][1] += 1
        ins.then_inc(self._sem(key), 1)
        self._record((key, self.cur[e][1]), reads, writes)
        self.ninst[e] += 1
        return ins

    def dma(self, e, out, in_, sem_t=None, **kw):
        st = (sem_t or out).dep
        if st.dsem is None or self.dma_cnt[st.dsem] + 16 > SEM_CAP:
            st.dsem = self.dma_free.pop()
        self._wait(e, self._collect(e, [in_], [out], False))
        ins = self.eng[e].dma_start(out=out.ap, in_=in_.ap, **kw)
        self.dma_cnt[st.dsem] += 16
        ins.then_inc(self._sem(st.dsem), 16)
        self._record((st.dsem, self.dma_cnt[st.dsem]), [in_], [out])
        self.ninst[e] += 1
        return ins

    def barrier(self):
        evs = {}
        for d in self.deps:
            for dd in (d.w, d.r):
                for k, v in dd.items():
                    if evs.get(k, 0) < v:
                        evs[k] = v
        for e in self.eng:
            self._wait(e, list(evs.items()))
        keep = []
        for d in self.deps:
            d.w.clear()
            d.r.clear()
            d.dsem = None
        self.dma_free = [k for k in self.dma_pool if self.dma_cnt[k] + 64 <= SEM_CAP]

    def _e(self, e):
        return self.eng[e]

    def tt(self, e, out, a, b, op):
        return self.op(e, lambda: self._e(e).tensor_tensor(out=out.ap, in0=a.ap, in1=b.ap, op=op), [a, b], [out])

    def ts(self, e, out, a, s1, op0, s2=None, op1=None):
        rd = [a] + [s for s in (s1, s2) if isinstance(s, T)]
        v1 = s1.ap if isinstance(s1, T) else s1
        v2 = s2.ap if isinstance(s2, T) else s2
        if op1 is None:
            return self.op(e, lambda: self._e(e).tensor_scalar(out=out.ap, in0=a.ap, scalar1=v1, scalar2=None, op0=op0), rd, [out])
        return self.op(e, lambda: self._e(e).tensor_scalar(out=out.ap, in0=a.ap, scalar1=v1, scalar2=v2, op0=op0, op1=op1), rd, [out])

    def stt(self, e, out, a, s, b, op0, op1):
        rd = [a, b] + ([s] if isinstance(s, T) else [])
        sv = s.ap if isinstance(s, T) else s
        return self.op(e, lambda: self._e(e).scalar_tensor_tensor(out=out.ap, in0=a.ap, scalar=sv, in1=b.ap, op0=op0, op1=op1), rd, [out])

    def act(self, out, a, func, scale=None, bias=None):
        rd = [a] + [s for s in (scale, bias) if isinstance(s, T)]
        kw = {}
        if scale is not None:
            kw["scale"] = scale.ap if isinstance(scale, T) else scale
        if bias is not None:
            kw["bias"] = bias.ap if isinstance(bias, T) else bias
        return self.op("act", lambda: self.nc.scalar.activation(out=out.ap, in_=a.ap, func=func, **kw), rd, [out])

    def copy(self, e, out, a):
        if e == "act":
            return self.op("act", lambda: self.nc.scalar.copy(out=out.ap, in_=a.ap), [a], [out])
        return self.op(e, lambda: self._e(e).tensor_copy(out=out.ap, in_=a.ap), [a], [out])

    def memset(self, e, out, val):
        return self.op(e, lambda: self._e(e).memset(out.ap, val), [], [out])

    def recip(self, out, a):
        return self.op("dve", lambda: self.nc.vector.reciprocal(out=out.ap, in_=a.ap), [a], [out])

    def scan(self, out, a, b, init):
        rd = [a, b] + ([init] if isinstance(init, T) else [])
        iv = init.ap if isinstance(init, T) else init
        return self.op("dve", lambda: self.nc.vector.tensor_tensor_scan(out=out.ap, data0=a.ap, data1=b.ap, initial=iv,
                                                                         op0=ALU.mult, op1=ALU.add), rd, [out])

    def mm(self, out, lhsT, rhs, start=True, stop=True):
        return self.op("pe", lambda: self.nc.tensor.matmul(out.ap, lhsT=lhsT.ap, rhs=rhs.ap, start=start, stop=stop),
                       [lhsT, rhs], [out])

    def tr(self, out, a, ident):
        return self.op("pe", lambda: self.nc.tensor.transpose(out.ap, a.ap, ident.ap), [a, ident], [out])


class WStream:
    def __init__(self, kb, stack, plan, srcs, nslots=4):
        self.kb = kb
        self.n = nslots
        self.slots = [kb.sb(stack, f"wring{i}", [128, 16, 512], BF16) for i in range(nslots)]
        self.plan = plan
        self.srcs = srcs
        self.issued = 0
        self.used = 0

    def _issue(self, idx):
        name, l, r0, nk, c0 = self.plan[idx]
        slot = self.slots[idx % self.n]
        src = self.srcs[name]
        ap = src.ap[l, r0:r0 + nk * 128, c0:c0 + 512].rearrange("(k p) c -> p k c", p=128)
        self.kb.dma("pool", slot[:, 0:nk, :], T(ap, src.dep))

    def next(self, desc):
        assert self.plan[self.used] == desc, (self.used, self.plan[self.used], desc)
        while self.issued < min(len(self.plan), self.used + self.n - 1):
            self._issue(self.issued)
            self.issued += 1
        t = self.slots[self.used % self.n]
        self.used += 1
        return t


def weight_plan(depth):
    plan = []
    for l in range(depth):
        for cb in range(24):
            plan.append(("w_mod", l, 0, 16, cb * 512))
        for ti in range(NT // 2):
            for cb in range(9):
                plan.append(("w_in", l, 0, 16, cb * 512))
        for ti in range(NT):
            for mg in range(4):
                for b in range(4):
                    plan.append(("w_in", l, 0, 16, GATE0 + b * 2048 + mg * 512))
                    plan.append(("w_br", l * 4 + b, 0, 4, mg * 512))
            for mg in range(4):
                plan.append(("w_out", l, 0, 16, mg * 512))
            for s in range(4):
                for fb in range(4):
                    plan.append(("w_ff1", l, 0, 16, s * 2048 + fb * 512))
                for mg in range(4):
                    plan.append(("w_ff2", l, s * 2048, 16, mg * 512))
    return plan


IN_SPECS = [
    ("xp", [NPT, D]), ("xs", [NST, D]), ("ck", [DEPTH, 256, 2, 128]), ("cv", [DEPTH, 256, 2, 128]),
    ("sret", [DEPTH, 2, 4, 128, 128]), ("slru", [DEPTH, 2, 512]), ("sre", [DEPTH, 2, 32, 64]), ("sim", [DEPTH, 2, 32, 64]),
    ("cvec", [2, D]),
    ("w_mod", [DEPTH, D, 6 * D]), ("b_mod", [DEPTH, 6 * D]), ("norm1", [DEPTH, D]), ("w_in", [DEPTH, D, 12800]),
    ("ret_decay_logit", [DEPTH, 2, 4]), ("ret_gn", [DEPTH, 512]), ("lru_conv_w", [DEPTH, 4, 512]), ("lru_conv_b", [DEPTH, 512]),
    ("lru_wa", [DEPTH, 2, 4, 128, 128]), ("lru_ba", [DEPTH, 2, 512]), ("lru_wx", [DEPTH, 2, 4, 128, 128]), ("lru_bx", [DEPTH, 2, 512]),
    ("lru_lambda", [DEPTH, 2, 512]), ("att_q_norm", [DEPTH, 128]), ("att_k_norm", [DEPTH, 128]), ("att_sink", [DEPTH, 4]),
    ("ssm_a_re", [DEPTH, 2, 32, 64]), ("ssm_a_im", [DEPTH, 2, 32, 64]), ("ssm_log_dt", [DEPTH, 2, 32]),
    ("ssm_b_re", [DEPTH, 2, 32, 64, 16]), ("ssm_b_im", [DEPTH, 2, 32, 64, 16]),
    ("ssm_c_re", [DEPTH, 2, 32, 16, 64]), ("ssm_c_im", [DEPTH, 2, 32, 16, 64]),
    ("ssm_d", [DEPTH, 512]), ("ssm_w_glu", [DEPTH, 512, 512]), ("ssm_b_glu", [DEPTH, 512]),
    ("w_br", [DEPTH * 4, 512, D]), ("w_out", [DEPTH, D, D]), ("norm2", [DEPTH, D]),
    ("w_ff1", [DEPTH, D, 4 * D]), ("w_ff2", [DEPTH, 4 * D, D]),
    ("k_ident", [128, 128]), ("k_tpos", [128, 2048]), ("k_ropec", [128, 2048]), ("k_ropes", [128, 2048]),
    ("k_perm", [128, 128]), ("k_pf", [128, 128]), ("k_pb", [128, 128]), ("k_mf", [128, 128]), ("k_mb", [128, 128]),
    ("k_tp1", [128, 2048]), ("k_tm", [128, 2048]), ("k_cols", [128, 4]),
]
OUT_SPECS = [
    ("yp", [NPT, D]), ("ys", [NST, D]), ("nk", [4, DEPTH, 256, 2, 128]), ("nv", [4, DEPTH, 256, 2, 128]),
    ("nret", [4, DEPTH, 2, 4, 128, 128]), ("nlru", [4, DEPTH, 2, 512]), ("nre", [4, DEPTH, 2, 32, 64]), ("nim", [4, DEPTH, 2, 32, 64]),
]


def host_consts():
    i = np.arange(128, dtype=np.float32)
    t = np.arange(2048, dtype=np.float32)
    c = {}
    c["k_ident"] = np.eye(128, dtype=np.float32)
    c["k_tpos"] = np.broadcast_to(t[None, :], (128, 2048)).copy()
    row = np.floor(t / 64.0)
    col = t - 64.0 * row
    freqs = (10000.0 ** (-np.arange(32, dtype=np.float32) / 32.0)).astype(np.float32)
    ang = np.concatenate([row[None, :].astype(np.float32) * freqs[:, None], col[None, :].astype(np.float32) * freqs[:, None]], axis=0)
    ang = ang.astype(np.float32)
    cos = np.cos(ang).astype(np.float32)
    sin = np.sin(ang).astype(np.float32)
    c["k_ropec"] = np.concatenate([cos, cos], axis=0)
    c["k_ropes"] = np.concatenate([-sin, sin], axis=0)
    perm = np.zeros((128, 128), np.float32)
    for m in range(128):
        perm[(m + 64) % 128, m] = 1.0
    c["k_perm"] = perm
    jj = i[:, None]
    ii = i[None, :]
    c["k_pf"] = np.maximum(ii - jj, 0.0).astype(np.float32)
    c["k_pb"] = np.maximum(jj - ii, 0.0).astype(np.float32)
    c["k_mf"] = (ii >= jj).astype(np.float32)
    c["k_mb"] = (jj >= ii).astype(np.float32)
    c["k_tp1"] = np.broadcast_to(((t % 128) + 1.0)[None, :], (128, 2048)).astype(np.float32).copy()
    c["k_tm"] = np.broadcast_to((128.0 - (t % 128))[None, :], (128, 2048)).astype(np.float32).copy()
    cols = np.zeros((128, 4), np.float32)
    cols[:, 0] = 127.0 - i
    cols[:, 1] = i
    c["k_cols"] = cols
    return c


class Prog:
    def __init__(self, kb, stack, depth, debug=False):
        self.kb = kb
        self.depth = depth
        nc = kb.nc
        self.I = {n: kb.dram(n, s, F32, kind="ExternalInput") for n, s in IN_SPECS}
        self.O = {n: kb.dram(n, s, F32, kind="ExternalOutput") for n, s in OUT_SPECS}
        sk = "ExternalOutput" if debug else "Internal"
        self.xT = kb.dram("s_xT", [KC, 128, NTOK], F32)
        self.hT = kb.dram("s_hT", [KC, 128, NTOK], BF16, kind=sk)
        self.debug = debug
        if debug:
            self.dmg = kb.dram("s_mg", [KC, 128, NTOK], BF16, kind=sk)
            self.dsg = kb.dram("s_sg", [128, NTOK], F32, kind=sk)
        self.mix = kb.dram("s_mix", [MIXC, 128, NTOK], F32, kind=sk)
        self.yb = kb.dram("s_yb", [4, 128, NTOK], BF16)
        self.oT = kb.dram("s_oT", [KC, 128, NTOK], BF16 if not debug else BF16, kind=sk)
        sb = lambda n, s, d: kb.sb(stack, n, s, d)
        self.ident = sb("ident", [128, 128], F32)
        self.ones_bf = sb("ones_bf", [128, 128], BF16)
        self.invf = sb("invf", [128, 128], F32)
        self.scond = sb("scond", [128, KC, 2], BF16)
        self.modT = sb("modT", [128, 96, 2], F32)
        self.ptab1 = sb("ptab1", [128, 128], F32)
        self.A1 = sb("A1", [128, KC, 2], F32)
        self.A2 = sb("A2", [128, KC, 2], F32)
        self.banks = [kb.ps(stack, f"bank{i}", [128, 512], F32) for i in range(7)]
        self.pbf = kb.ps(stack, "pbf", [128, 1024], BF16)
        self.bi = 0
        self.ws = WStream(kb, stack, weight_plan(depth), self.I, nslots=4)

    def bank(self):
        b = self.banks[self.bi % 7]
        self.bi += 1
        return b

    def dsl(self, t, *idx):
        return T(t.ap[idx], t.dep)

    def setup(self):
        kb = self.kb
        with contextlib.ExitStack() as st:
            kb.dma("sp", self.ident, self.I["k_ident"])
            kb.memset("dve", self.ones_bf, 1.0)
            kb.memset("dve", self.invf, 1.0 / 128.0)
            stg = kb.sb(st, "su_stg", [32, 128], F32)
            kb.dma("sp", stg, self.I["cvec"].rr("j (k p) -> (j k) p", p=128))
            pb = self.bank()
            kb.tr(pb[:, 0:32], stg, self.ident[0:32, 0:32])
            cs = kb.sb(st, "su_c", [128, 32], F32)
            kb.act(cs, pb[:, 0:32], AF.Silu)
            kb.copy("dve", self.scond.rr("p k j -> p j k"), cs.rr("p (j k) -> p j k", j=2))
            kb.barrier()

    def phase0(self):
        kb = self.kb
        with contextlib.ExitStack() as st:
            xin = kb.sb(st, "p0_x", [128, 4, D], F32)
            xo = kb.sb(st, "p0_o", [128, KC, TT], F32)
            for ti in range(NT):
                if ti < 2:
                    src = self.I["xp"][ti * 512:(ti + 1) * 512, :]
                else:
                    src = self.I["xs"][(ti - 2) * 512:(ti - 1) * 512, :]
                kb.dma("sp", xin, src.rr("(g p) f -> p g f", p=128))
                for k in range(KC):
                    pb = self.bank()
                    for g in range(4):
                        kb.tr(pb[:, g * 128:(g + 1) * 128], xin[:, g, k * 128:(k + 1) * 128], self.ident)
                    kb.copy("act" if k % 2 else "dve", xo[:, k, :], pb)
                kb.dma("sp", self.xT[:, :, ti * 512:(ti + 1) * 512].rr("k p t -> p k t"), xo)
            kb.barrier()

    def phaseM(self, l):
        kb = self.kb
        with contextlib.ExitStack() as st:
            stg = kb.sb(st, "m_stg", [128, 128], F32)
            kb.dma("sp", stg[0:16, :], self.I["norm1"][l].rr("(k p) -> k p", p=128))
            kb.dma("sp", stg[16:32, :], self.I["norm2"][l].rr("(k p) -> k p", p=128))
            kb.dma("sp", stg[32:128, :], self.I["b_mod"][l].rr("(k p) -> k p", p=128))
            pb = self.bank()
            kb.tr(pb[:, 0:128], stg, self.ident)
            kb.copy("dve", self.ptab1, pb[:, 0:128])
            for cb in range(24):
                wt = self.ws.next(("w_mod", l, 0, 16, cb * 512))
                for mm in range(4):
                    m = cb * 4 + mm
                    pb = self.bank()
                    for k in range(KC):
                        kb.mm(pb[:, 0:2], wt[:, k, mm * 128:(mm + 1) * 128], self.scond[:, k, :], start=(k == 0), stop=(k == KC - 1))
                    kb.ts("dve", self.modT[:, m, :], pb[:, 0:2], self.ptab1[:, 32 + m:33 + m], ALU.add)
            for (A, nb, scb) in ((self.A1, 0, 16), (self.A2, 16, 64)):
                for j in range(2):
                    kb.ts("dve", A[:, :, j], self.modT[:, scb:scb + 16, j], 1.0, ALU.add)
                    kb.tt("dve", A[:, :, j], A[:, :, j], self.ptab1[:, nb:nb + 16], ALU.mult)
            kb.barrier()

    def norm_mod(self, xt, hT, A, shb, j, sq, tmp, rstd):
        kb = self.kb
        pb = self.bank()
        for k in range(KC):
            kb.act(sq[k % 2], xt[:, k, :], AF.Square)
            kb.mm(pb, self.ones_bf, sq[k % 2], start=(k == 0), stop=(k == KC - 1))
        kb.ts("dve", rstd, pb, 1.0 / D, ALU.mult, EPS, ALU.add)
        kb.act(rstd, rstd, AF.Sqrt)
        kb.recip(rstd, rstd)
        for k in range(KC):
            kb.stt("dve", tmp[k % 2], xt[:, k, :], A[:, k, j:j + 1], rstd, ALU.mult, ALU.mult)
            kb.act(hT[:, k, :], tmp[k % 2], AF.Identity, bias=self.modT[:, shb + k, j:j + 1])

    def phaseA(self, l):
        kb = self.kb
        TA = 2 * TT
        with contextlib.ExitStack() as st:
            xt = kb.sb(st, "a_xt", [128, KC, TA], F32)
            hT = kb.sb(st, "a_hT", [128, KC, TA], BF16)
            sq = [kb.sb(st, f"a_sq{i}", [128, TT], BF16) for i in range(2)]
            tmp = [kb.sb(st, f"a_tmp{i}", [128, TT], F32) for i in range(2)]
            rstd = kb.sb(st, "a_rstd", [128, TT], F32)
            stage = [kb.sb(st, f"a_stg{i}", [128, TT], F32) for i in range(6)]
            for ti in range(NTOK // TA):
                j = 0 if ti < 1 else 1
                tok = slice(ti * TA, (ti + 1) * TA)
                kb.dma("sp", xt, self.xT[:, :, tok].rr("k p t -> p k t"))
                for hf in range(2):
                    hs = slice(hf * TT, (hf + 1) * TT)
                    self.norm_mod(xt[:, :, hs], hT[:, :, hs], self.A1, 0, j, sq, tmp, rstd)
                kb.dma("sp", self.hT[:, :, tok].rr("k p t -> p k t"), hT)
                cnt = 0
                for cb in range(9):
                    wt = self.ws.next(("w_in", l, 0, 16, cb * 512))
                    for mm in range(4):
                        m = cb * 4 + mm
                        for hf in range(2):
                            hs = slice(hf * TT, (hf + 1) * TT)
                            pb = self.bank()
                            for k in range(KC):
                                kb.mm(pb, wt[:, k, mm * 128:(mm + 1) * 128], hT[:, k, hs], start=(k == 0), stop=(k == KC - 1))
                            sg = stage[cnt % 6]
                            kb.copy("act" if cnt % 2 else "dve", sg, pb)
                            kb.dma("sp", self.mix[m, :, ti * TA + hf * TT: ti * TA + (hf + 1) * TT], sg)
                            cnt += 1
            kb.barrier()

    def phaseC(self, l, last):
        kb = self.kb
        ws = self.ws
        with contextlib.ExitStack() as st:
            xt = kb.sb(st, "c_xt", [128, KC, TT], F32)
            hT = kb.sb(st, "c_hT", [128, KC, TT], BF16)
            oT = kb.sb(st, "c_oT", [128, KC, TT], BF16)
            mg_t = kb.sb(st, "c_mg", [128, KC, TT], BF16)
            acc = [kb.sb(st, f"c_acc{i}", [128, TT], F32) for i in range(4)]
            sig = [kb.sb(st, f"c_sig{i}", [128, TT], F32) for i in range(2)]
            tmpm = [kb.sb(st, f"c_tm{i}", [128, TT], F32) for i in range(2)]
            rl = [kb.sb(st, f"c_rl{i}", [128, TT], BF16) for i in range(2)]
            sq = [kb.sb(st, f"c_sq{i}", [128, TT], BF16) for i in range(2)]
            tmp = [kb.sb(st, f"c_tmp{i}", [128, TT], F32) for i in range(2)]
            rstd = kb.sb(st, "c_rstd", [128, TT], F32)
            actbuf = [oT, mg_t]
            for ti in range(NT):
                j = 0 if ti < 2 else 1
                tok = slice(ti * TT, (ti + 1) * TT)
                kb.dma("sp", hT, self.hT[:, :, tok].rr("k p t -> p k t"))
                kb.dma("sp", oT, self.oT[:, :, tok].rr("k p t -> p k t"))
                kb.dma("sp", xt, self.xT[:, :, tok].rr("k p t -> p k t"))
                cnt = 0
                for mg in range(4):
                    for b in range(4):
                        wg = ws.next(("w_in", l, 0, 16, GATE0 + b * 2048 + mg * 512))
                        wb = ws.next(("w_br", l * 4 + b, 0, 4, mg * 512))
                        for mm in range(4):
                            pg = self.bank()
                            for k in range(KC):
                                kb.mm(pg, wg[:, k, mm * 128:(mm + 1) * 128], hT[:, k, :], start=(k == 0), stop=(k == KC - 1))
                            sg = sig[cnt % 2]
                            kb.act(sg, pg, AF.Sigmoid)
                            pbk = self.bank()
                            for kk in range(4):
                                kb.mm(pbk, wb[:, kk, mm * 128:(mm + 1) * 128], oT[:, 4 * b + kk, :], start=(kk == 0), stop=(kk == 3))
                            if b == 0:
                                kb.tt("dve", acc[mm], sg, pbk, ALU.mult)
                            else:
                                tm = tmpm[cnt % 2]
                                kb.tt("dve", tm, sg, pbk, ALU.mult)
                                kb.tt("dve", acc[mm], acc[mm], tm, ALU.add)
                            cnt += 1
                    for mm in range(4):
                        kb.copy("act", mg_t[:, mg * 4 + mm, :], acc[mm])
                if self.debug:
                    kb.dma("sp", self.dmg[:, :, tok].rr("k p t -> p k t"), mg_t)
                    kb.dma("sp", self.dsg[:, tok], sig[(cnt - 1) % 2])
                for mg in range(4):
                    wo = ws.next(("w_out", l, 0, 16, mg * 512))
                    for mm in range(4):
                        m = mg * 4 + mm
                        pb = self.bank()
                        for k in range(KC):
                            kb.mm(pb, wo[:, k, mm * 128:(mm + 1) * 128], mg_t[:, k, :], start=(k == 0), stop=(k == KC - 1))
                        kb.stt("dve", xt[:, m, :], pb, self.modT[:, 32 + m, j:j + 1], xt[:, m, :], ALU.mult, ALU.add)
                self.norm_mod(xt, hT, self.A2, 48, j, sq, tmp, rstd)
                for s in range(4):
                    ab = actbuf[s % 2]
                    for fb in range(4):
                        w1 = ws.next(("w_ff1", l, 0, 16, s * 2048 + fb * 512))
                        for mm in range(4):
                            f = fb * 4 + mm
                            pb = self.bank()
                            for k in range(KC):
                                kb.mm(pb, w1[:, k, mm * 128:(mm + 1) * 128], hT[:, k, :], start=(k == 0), stop=(k == KC - 1))
                            r = rl[f % 2]
                            kb.act(r, pb, AF.Relu)
                            kb.tt("dve", ab[:, f, :], r, r, ALU.mult)
                    for mg in range(4):
                        w2 = ws.next(("w_ff2", l, s * 2048, 16, mg * 512))
                        for mm in range(4):
                            m = mg * 4 + mm
                            pb = self.bank()
                            for f in range(KC):
                                kb.mm(pb, w2[:, f, mm * 128:(mm + 1) * 128], ab[:, f, :], start=(f == 0), stop=(f == KC - 1))
                            kb.stt("dve", xt[:, m, :], pb, self.modT[:, 80 + m, j:j + 1], xt[:, m, :], ALU.mult, ALU.add)
                if not last:
                    kb.dma("sp", self.xT[:, :, tok].rr("k p t -> p k t"), xt)
                else:
                    for g in range(4):
                        for kq in range(4):
                            pb = self.bank()
                            for kk in range(4):
                                kb.tr(pb[:, kk * 128:(kk + 1) * 128], xt[:, kq * 4 + kk, g * 128:(g + 1) * 128], self.ident)
                            kb.copy("act" if kq % 2 else "dve", acc[kq], pb)
                            if ti < 2:
                                dst = self.O["yp"][ti * 512 + g * 128: ti * 512 + (g + 1) * 128, kq * 512:(kq + 1) * 512]
                            else:
                                dst = self.O["ys"][(ti - 2) * 512 + g * 128:(ti - 2) * 512 + (g + 1) * 128, kq * 512:(kq + 1) * 512]
                            kb.dma("sp", dst, acc[kq])
            kb.barrier()

    def phaseB(self, l):
        kb = self.kb
        with contextlib.ExitStack() as st:
            stg = kb.sb(st, "b_stg", [128, 128], F32)
            I = self.I
            kb.memset("dve", stg, 0.0)
            kb.dma("sp", stg[0:4, :], I["ret_gn"][l].rr("(k p) -> k p", p=128))
            kb.dma("sp", stg[4:20, :], I["lru_conv_w"][l].rr("j (n p) -> (j n) p", p=128))
            kb.dma("sp", stg[20:24, :], I["lru_conv_b"][l].rr("(k p) -> k p", p=128))
            kb.dma("sp", stg[24:32, :], I["lru_ba"][l].rr("d (n p) -> (d n) p", p=128))
            kb.dma("sp", stg[32:40, :], I["lru_bx"][l].rr("d (n p) -> (d n) p", p=128))
            kb.dma("sp", stg[40:48, :], I["lru_lambda"][l].rr("d (n p) -> (d n) p", p=128))
            kb.dma("sp", stg[48:52, :], I["ssm_d"][l].rr("(k p) -> k p", p=128))
            kb.dma("sp", stg[52:56, :], I["ssm_b_glu"][l].rr("(k p) -> k p", p=128))
            kb.dma("sp", stg[56:57, :], I["att_q_norm"][l:l + 1, :])
            kb.dma("sp", stg[57:58, :], I["att_k_norm"][l:l + 1, :])
            kb.dma("sp", stg[58:66, :], I["slru"][l].rr("d (n p) -> (d n) p", p=128))
            pt2 = kb.sb(st, "b_pt2", [128, 128], F32)
            pb = self.bank()
            kb.tr(pb[:, 0:128], stg, self.ident)
            kb.copy("dve", pt2, pb[:, 0:128])
            bc = kb.sb(st, "b_bc", [128, 12], F32)
            kb.dma("sp", bc[:, 0:8], T(I["ret_decay_logit"].ap[l].rearrange("d h -> (d h)").partition_broadcast(128), I["ret_decay_logit"].dep))
            kb.dma("sp", bc[:, 8:12], T(I["att_sink"].ap[l].partition_broadcast(128), I["att_sink"].dep))
            identb = kb.sb(st, "b_identb", [128, 128], BF16)
            kb.copy("dve", identb, self.ident)
            kb.barrier()
            self.B_ret(l, pt2, bc, identb)
            self.B_lru(l, pt2)
            self.B_att(l, pt2, bc, identb)
            self.B_ssm(l, pt2, identb)

    def B_ret(self, l, pt2, bc, identb):
        kb = self.kb
        I = self.I
        LM = NST
        with contextlib.ExitStack() as st:
            sbf = lambda n, s, d: kb.sb(st, "r_" + n, s, d)
            pf, pbk_, mf, mb = [sbf(n, [128, 128], F32) for n in ("pf", "pb", "mf", "mb")]
            for t_, n_ in ((pf, "k_pf"), (pbk_, "k_pb"), (mf, "k_mf"), (mb, "k_mb")):
                kb.dma("sp", t_, I[n_])
            tp1 = sbf("tp1", [128, LM], F32)
            tm = sbf("tm", [128, LM], F32)
            cols = sbf("cols", [128, 4], F32)
            kb.dma("sp", tp1, I["k_tp1"])
            kb.dma("sp", tm, I["k_tm"])
            kb.dma("sp", cols, I["k_cols"])
            lg = sbf("lg", [128, 8], F32)
            kb.act(lg, bc[:, 0:8], AF.Exp, scale=-1.0)
            kb.ts("dve", lg, lg, 1.0, ALU.add)
            kb.act(lg, lg, AF.Ln)
            kb.ts("dve", lg, lg, -1.0, ALU.mult)
            lnsc = sbf("lnsc", [128, 1], F32)
            kb.memset("dve", lnsc, float(math.log(128.0 ** -0.5)))
            Dm = sbf("Dm", [128, 128], F32)
            e2 = sbf("e2", [128, 128], F32)
            wcol = sbf("wcol", [128, 4], F32)
            wfT = sbf("wfT", [128, LM], F32)
            wbT = sbf("wbT", [128, LM], F32)
            S0 = sbf("S0", [128, 2, 128], F32)
            S = sbf("S", [128, 128], F32)
            Hq, Hk, Hv, Hqf, Hqb, Hout = [sbf(n, [128, LM], BF16) for n in ("Hq", "Hk", "Hv", "Hqf", "Hqb", "Hout")]
            Ktok, Vtok, Kwf, Kwb, Sfb, Sbb = [sbf(n, [128, LM // 128, 128], BF16) for n in ("Kt", "Vt", "Kwf", "Kwb", "Sfb", "Sbb")]
            F0, F1, F2, F3 = [sbf(n, [128, LM], F32) for n in ("F0", "F1", "F2", "F3")]
            Pt = [sbf(f"P{i}", [128, 128], BF16) for i in range(2)]
            for h in range(4):
                lgf = lg[:, h:h + 1]
                lgb = lg[:, 4 + h:5 + h]
                kb.act(Dm, pf, AF.Exp, scale=lgf, bias=lnsc)
                kb.tt("dve", Dm, Dm, mf, ALU.mult)
                kb.act(e2, pbk_, AF.Exp, scale=lgb, bias=lnsc)
                kb.tt("dve", e2, e2, mb, ALU.mult)
                kb.tt("dve", Dm, Dm, e2, ALU.add)
                kb.act(wcol[:, 0:1], cols[:, 0:1], AF.Exp, scale=lgf)
                kb.act(wcol[:, 1:2], cols[:, 1:2], AF.Exp, scale=lgb)
                kb.act(wcol[:, 2:3], lgf, AF.Exp, scale=128.0)
                kb.act(wcol[:, 3:4], lgb, AF.Exp, scale=128.0)
                kb.act(wfT, tp1, AF.Exp, scale=lgf, bias=lnsc)
                kb.act(wbT, tm, AF.Exp, scale=lgb, bias=lnsc)
                kb.dma("sp", S0, I["sret"][l, :, h].rr("a d e -> d a e"))
                for (t0, L, is_ctx, si) in SEQS:
                    n = L // 128
                    tk = slice(t0, t0 + L)
                    kb.dma("pool", Hq[:, 0:L], self.mix[h, :, tk])
                    kb.dma("pool", Hk[:, 0:L], self.mix[4 + h, :, tk])
                    kb.dma("pool", Hv[:, 0:L], self.mix[8 + h, :, tk])
                    kb.dma("sp", F0[:, 0:L], self.mix[12 + h, :, tk])
                    for c in range(n):
                        cs_ = slice(c * 128, (c + 1) * 128)
                        kb.tr(self.pbf[:, 0:128], Hk[:, cs_], identb)
                        kb.copy("act", Ktok[:, c, :], self.pbf[:, 0:128])
                        kb.tr(self.pbf[:, 128:256], Hv[:, cs_], identb)
                        kb.copy("dve", Vtok[:, c, :], self.pbf[:, 128:256])
                    kb.ts("dve", Kwf[:, 0:n, :], Ktok[:, 0:n, :], wcol[:, 0:1], ALU.mult)
                    kb.ts("dve", Kwb[:, 0:n, :], Ktok[:, 0:n, :], wcol[:, 1:2], ALU.mult)
                    for d in range(2):
                        if is_ctx:
                            kb.memset("dve", S, 0.0)
                        else:
                            kb.copy("dve", S, S0[:, d, :])
                        Sx = Sfb if d == 0 else Sbb
                        Kw = Kwf if d == 0 else Kwb
                        order = range(n) if d == 0 else range(n - 1, -1, -1)
                        for c in order:
                            kb.copy("act", Sx[:, c, :], S)
                            pk = self.bank()
                            kb.mm(pk[:, 0:128], Kw[:, c, :], Vtok[:, c, :])
                            kb.stt("dve", S, S, wcol[:, 2 + d:3 + d], pk[:, 0:128], ALU.mult, ALU.add)
                        if is_ctx:
                            kb.dma("sp", self.O["nret"][si, l, d, h], S)
                    kb.tt("dve", Hqf[:, 0:L], Hq[:, 0:L], wfT[:, 0:L], ALU.mult)
                    kb.tt("dve", Hqb[:, 0:L], Hq[:, 0:L], wbT[:, 0:L], ALU.mult)
                    cb_ = min(512, L)
                    cnt = 0
                    for blk in range(L // cb_):
                        po = self.bank()
                        for cc in range(cb_ // 128):
                            c = blk * (cb_ // 128) + cc
                            cs_ = slice(c * 128, (c + 1) * 128)
                            psc = self.bank()
                            kb.mm(psc[:, 0:128], Hk[:, cs_], Hq[:, cs_])
                            P = Pt[cnt % 2]
                            cnt += 1
                            kb.tt("dve", P, psc[:, 0:128], Dm, ALU.mult)
                            oc = po[:, cc * 128:(cc + 1) * 128]
                            kb.mm(oc, Vtok[:, c, :], P, start=True, stop=False)
                            kb.mm(oc, Sfb[:, c, :], Hqf[:, cs_], start=False, stop=False)
                            kb.mm(oc, Sbb[:, c, :], Hqb[:, cs_], start=False, stop=True)
                        bs = slice(blk * cb_, (blk + 1) * cb_)
                        kb.copy("act", F1[:, bs], po[:, 0:cb_])
                        pm_ = self.bank()
                        kb.mm(pm_[:, 0:cb_], self.invf, F1[:, bs])
                        kb.tt("dve", F2[:, bs], F1[:, bs], pm_[:, 0:cb_], ALU.subtract)
                        kb.act(F3[:, bs], F2[:, bs], AF.Square)
                        pv = self.bank()
                        kb.mm(pv[:, 0:cb_], self.invf, F3[:, bs])
                        kb.ts("dve", F3[:, bs], pv[:, 0:cb_], EPS, ALU.add)
                        kb.act(F3[:, bs], F3[:, bs], AF.Sqrt)
                        kb.recip(F3[:, bs], F3[:, bs])
                        kb.tt("dve", F2[:, bs], F2[:, bs], F3[:, bs], ALU.mult)
                        kb.act(F0[:, bs], F0[:, bs], AF.Silu)
                        kb.stt("dve", Hout[:, bs], F2[:, bs], pt2[:, h:h + 1], F0[:, bs], ALU.mult, ALU.mult)
                    kb.dma("sp", self.oT[h, :, tk], Hout[:, 0:L])
            kb.barrier()

    def gelu_tanh(self, out, x, t1, e="dve"):
        kb = self.kb
        kb.act(t1, x, AF.Square)
        kb.ts(e, t1, t1, 0.044715, ALU.mult, 1.0, ALU.add)
        kb.tt(e, t1, t1, x, ALU.mult)
        kb.act(t1, t1, AF.Sigmoid, scale=1.5957691216057308)
        kb.tt(e, out, x, t1, ALU.mult)

    def B_lru(self, l, pt2):
        kb = self.kb
        I = self.I
        LM = NST
        with contextlib.ExitStack() as st:
            sbf = lambda n, s, d: kb.sb(st, "l_" + n, s, d)
            nls = sbf("nls", [128, 8], F32)
            kb.act(nls, pt2[:, 40:48], AF.Exp, scale=-1.0)
            kb.ts("dve", nls, nls, 1.0, ALU.add)
            kb.act(nls, nls, AF.Ln)
            kb.ts("dve", nls, nls, -8.0, ALU.mult)
            wa = sbf("wa", [128, 2, 128], BF16)
            wx = sbf("wx", [128, 2, 128], BF16)
            F = [sbf(f"F{i}", [128, LM], F32) for i in range(9)]
            H0 = sbf("H0", [128, LM], BF16)
            H1 = sbf("H1", [128, LM], BF16)
            fin = sbf("fin", [128, 2], F32)
            for nb in range(4):
                kb.dma("pool", wa, I["lru_wa"][l, :, nb].rr("a d e -> d a e"))
                kb.dma("pool", wx, I["lru_wx"][l, :, nb].rr("a d e -> d a e"))
                cw = lambda j: pt2[:, 4 + j * 4 + nb: 5 + j * 4 + nb]
                for (t0, L, is_ctx, si) in SEQS:
                    tk = slice(t0, t0 + L)
                    x, gate, xc = F[0][:, 0:L], F[1][:, 0:L], F[2][:, 0:L]
                    kb.dma("sp", x, self.mix[16 + nb, :, tk])
                    kb.dma("sp", gate, self.mix[20 + nb, :, tk])
                    kb.ts("dve", xc, x, cw(2), ALU.mult, pt2[:, 20 + nb:21 + nb], ALU.add)
                    kb.stt("dve", xc[:, 2:L], x[:, 0:L - 2], cw(0), xc[:, 2:L], ALU.mult, ALU.add)
                    kb.stt("dve", xc[:, 1:L], x[:, 0:L - 1], cw(1), xc[:, 1:L], ALU.mult, ALU.add)
                    kb.stt("dve", xc[:, 0:L - 1], x[:, 1:L], cw(3), xc[:, 0:L - 1], ALU.mult, ALU.add)
                    kb.copy("act", H0[:, 0:L], xc)
                    cb_ = min(512, L)
                    hs = [F[7][:, 0:L], F[8][:, 0:L]]
                    for d in range(2):
                        r, ig, a, sq_ = F[3][:, 0:L], F[4][:, 0:L], F[5][:, 0:L], F[6][:, 0:L]
                        for blk in range(L // cb_):
                            bs = slice(blk * cb_, (blk + 1) * cb_)
                            p1 = self.bank()
                            kb.mm(p1[:, 0:cb_], wa[:, d, :], H0[:, bs])
                            kb.act(r[:, bs], p1[:, 0:cb_], AF.Sigmoid, bias=pt2[:, 24 + d * 4 + nb:25 + d * 4 + nb])
                            p2 = self.bank()
                            kb.mm(p2[:, 0:cb_], wx[:, d, :], H0[:, bs])
                            kb.act(ig[:, bs], p2[:, 0:cb_], AF.Sigmoid, bias=pt2[:, 32 + d * 4 + nb:33 + d * 4 + nb])
                        kb.act(a, r, AF.Exp, scale=nls[:, d * 4 + nb:d * 4 + nb + 1])
                        kb.tt("dve", sq_, a, a, ALU.mult)
                        kb.ts("dve", sq_, sq_, -1.0, ALU.mult, 1.0, ALU.add)
                        kb.act(sq_, sq_, AF.Relu)
                        kb.act(sq_, sq_, AF.Sqrt)
                        kb.tt("dve", ig, ig, xc, ALU.mult)
                        kb.tt("dve", ig, ig, sq_, ALU.mult)
                        if is_ctx:
                            init = 0.0
                        else:
                            init = pt2[:, 58 + d * 4 + nb:59 + d * 4 + nb]
                        hd = hs[d]
                        if d == 0:
                            kb.scan(hd, a, ig, init)
                            kb.copy("act", fin[:, 0:1], hd[:, L - 1:L])
                        else:
                            kb.scan(hd.rev(), a.rev(), ig.rev(), init)
                            kb.copy("act", fin[:, 1:2], hd[:, 0:1])
                    if is_ctx:
                        kb.dma("sp", self.O["nlru"][si, l, :, nb * 128:(nb + 1) * 128].rr("d p -> p d"), fin, allow_slow_non_contiguous=True)
                    kb.tt("dve", hs[0], hs[0], hs[1], ALU.add)
                    self.gelu_tanh(F[4][:, 0:L], gate, F[3][:, 0:L])
                    kb.tt("dve", H1[:, 0:L], F[4][:, 0:L], hs[0], ALU.mult)
                    kb.dma("sp", self.oT[4 + nb, :, tk], H1[:, 0:L])
            kb.barrier()

    def rms_feat(self, x, L, gain, w1):
        kb = self.kb
        cb_ = min(512, L)
        for blk in range(L // cb_):
            bs = slice(blk * cb_, (blk + 1) * cb_)
            kb.act(w1[:, bs], x[:, bs], AF.Square)
            pm_ = self.bank()
            kb.mm(pm_[:, 0:cb_], self.invf, w1[:, bs])
            kb.ts("dve", w1[:, bs], pm_[:, 0:cb_], EPS, ALU.add)
            kb.act(w1[:, bs], w1[:, bs], AF.Sqrt)
            kb.recip(w1[:, bs], w1[:, bs])
            kb.stt("dve", x[:, bs], x[:, bs], gain, w1[:, bs], ALU.mult, ALU.mult)

    def B_att(self, l, pt2, bc, identb):
        kb = self.kb
        I = self.I
        LM = NST
        SC = 128.0 ** -0.5
        with contextlib.ExitStack() as st:
            sbf = lambda n, s, d: kb.sb(st, "t_" + n, s, d)
            ropec = sbf("ropec", [128, LM], F32)
            ropes = sbf("ropes", [128, LM], F32)
            perm = sbf("perm", [128, 128], F32)
            mfb = sbf("mfb", [128, 128], BF16)
            mbb = sbf("mbb", [128, 128], BF16)
            kb.dma("sp", ropec, I["k_ropec"])
            kb.dma("sp", ropes, I["k_ropes"])
            kb.dma("sp", perm, I["k_perm"])
            kb.dma("pool", mfb, I["k_mf"])
            kb.dma("pool", mbb, I["k_mb"])
            esink = sbf("esink", [128, 4], F32)
            kb.act(esink, bc[:, 8:12], AF.Exp)
            Q = [sbf(f"Q{i}", [128, LM], F32) for i in range(2)]
            K = sbf("K", [128, LM], F32)
            V = sbf("V", [128, LM], F32)
            W1 = sbf("W1", [128, LM], F32)
            W2 = sbf("W2", [128, LM], F32)
            Hq = [sbf(f"Hq{i}", [128, LM], BF16) for i in range(2)]
            Hk = sbf("Hk", [128, LM], BF16)
            Ho = sbf("Ho", [128, LM], BF16)
            vtok = sbf("vtok", [128, LM // 128, 128], BF16)
            kst = sbf("kst", [128, 2, 128], F32)
            vst = sbf("vst", [128, 2, 128], F32)
            cks = sbf("cks", [128, 2, 128], F32)
            ckT = sbf("ckT", [128, 256], BF16)
            cvb = sbf("cvb", [128, 2, 128], BF16)
            PL = sbf("PL", [128, LM // 128, 384], BF16)
            PC = sbf("PC", [128, 2, LM], BF16)
            den = sbf("den", [128, 512], F32)
            for kvh in range(2):
                kb.dma("sp", cks, I["ck"][l, :, kvh, :].rr("(c t) d -> t c d", t=128))
                kb.dma("pool", cvb, I["cv"][l, :, kvh, :].rr("(c t) d -> t c d", t=128))
                for c in range(2):
                    pb = self.bank()
                    kb.tr(pb[:, 0:128], cks[:, c, :], self.ident)
                    kb.copy("act", ckT[:, c * 128:(c + 1) * 128], pb[:, 0:128])
                for (t0, L, is_ctx, si) in SEQS:
                    n = L // 128
                    tk = slice(t0, t0 + L)
                    cb_ = min(512, L)
                    nblk = L // cb_
                    kb.dma("sp", Q[0][:, 0:L], self.mix[24 + 2 * kvh, :, tk])
                    kb.dma("sp", Q[1][:, 0:L], self.mix[25 + 2 * kvh, :, tk])
                    kb.dma("sp", K[:, 0:L], self.mix[28 + kvh, :, tk])
                    kb.dma("sp", V[:, 0:L], self.mix[30 + kvh, :, tk])
                    self.rms_feat(Q[0], L, pt2[:, 56:57], W1)
                    self.rms_feat(Q[1], L, pt2[:, 56:57], W1)
                    self.rms_feat(K, L, pt2[:, 57:58], W1)
                    if is_ctx:
                        for c in range(n):
                            cs_ = slice(c * 128, (c + 1) * 128)
                            pb = self.bank()
                            kb.tr(pb[:, 0:128], K[:, cs_], self.ident)
                            kb.copy("act", kst[:, c, :], pb[:, 0:128])
                            pb2 = self.bank()
                            kb.tr(pb2[:, 0:128], V[:, cs_], self.ident)
                            kb.copy("dve", vst[:, c, :], pb2[:, 0:128])
                        kb.copy("act", vtok[:, 0:n, :], vst[:, 0:n, :])
                        kb.dma("sp", self.O["nk"][si, l, :, kvh, :].rr("(c t) d -> t c d", t=128), kst)
                        kb.dma("sp", self.O["nv"][si, l, :, kvh, :].rr("(c t) d -> t c d", t=128), vst)
                    else:
                        for X in (Q[0], Q[1], K):
                            for blk in range(nblk):
                                bs = slice(blk * cb_, (blk + 1) * cb_)
                                pp = self.bank()
                                kb.mm(pp[:, 0:cb_], perm, X[:, bs])
                                kb.tt("dve", W1[:, bs], X[:, bs], ropec[:, bs], ALU.mult)
                                kb.tt("dve", W2[:, bs], pp[:, 0:cb_], ropes[:, bs], ALU.mult)
                                kb.tt("dve", X[:, bs], W1[:, bs], W2[:, bs], ALU.add)
                        for c in range(n):
                            pb = self.bank()
                            kb.tr(pb[:, 0:128], V[:, c * 128:(c + 1) * 128], self.ident)
                            kb.copy("act" if c % 2 else "dve", vtok[:, c, :], pb[:, 0:128])
                    kb.copy("act", Hq[0][:, 0:L], Q[0][:, 0:L])
                    kb.copy("act", Hq[1][:, 0:L], Q[1][:, 0:L])
                    kb.copy("dve", Hk[:, 0:L], K[:, 0:L])
                    for hq in range(2):
                        qh = 2 * kvh + hq
                        q = Hq[hq]
                        es = esink[:, qh:qh + 1]
                        if is_ctx:
                            for c in range(2):
                                ps_ = self.bank()
                                kb.mm(ps_[:, 0:L], Hk[:, c * 128:(c + 1) * 128], q[:, 0:L])
                                kb.act(PC[:, c, 0:L], ps_[:, 0:L], AF.Exp, scale=SC)
                            po = self.bank()
                            pd = self.bank()
                            for c in range(2):
                                kb.mm(po[:, 0:L], vtok[:, c, :], PC[:, c, 0:L], start=(c == 0), stop=(c == 1))
                                kb.mm(pd[:, 0:L], self.ones_bf, PC[:, c, 0:L], start=(c == 0), stop=(c == 1))
                            kb.ts("dve", den[:, 0:L], pd[:, 0:L], es, ALU.add)
                            kb.recip(den[:, 0:L], den[:, 0:L])
                            kb.tt("dve", Ho[:, 0:L], po[:, 0:L], den[:, 0:L], ALU.mult)
                        else:
                            for jc in range(n):
                                b0 = max(jc - 1, 0)
                                b1 = min(jc + 1, n - 1)
                                ncol = (b1 - b0 + 1) * 128
                                s0 = (b0 - (jc - 1)) * 128
                                ps_ = self.bank()
                                kb.mm(ps_[:, 0:ncol], Hk[:, jc * 128:(jc + 1) * 128], q[:, b0 * 128:(b1 + 1) * 128])
                                kb.act(PL[:, jc, s0:s0 + ncol], ps_[:, 0:ncol], AF.Exp, scale=SC)
                                if jc >= 1:
                                    kb.tt("dve", PL[:, jc, 0:128], PL[:, jc, 0:128], mfb, ALU.mult)
                                if jc <= n - 2:
                                    kb.tt("dve", PL[:, jc, 256:384], PL[:, jc, 256:384], mbb, ALU.mult)
                            for c in range(2):
                                for blk in range(nblk):
                                    bs = slice(blk * cb_, (blk + 1) * cb_)
                                    ps_ = self.bank()
                                    kb.mm(ps_, ckT[:, c * 128:(c + 1) * 128], q[:, bs])
                                    kb.act(PC[:, c, bs], ps_, AF.Exp, scale=SC)
                            for blk in range(nblk):
                                bs = slice(blk * cb_, (blk + 1) * cb_)
                                po = self.bank()
                                pd = self.bank()
                                for ii in range(4):
                                    i = blk * 4 + ii
                                    terms = []
                                    for jc in (i - 1, i, i + 1):
                                        if 0 <= jc < n:
                                            sl = (i - (jc - 1)) * 128
                                            terms.append((vtok[:, jc, :], PL[:, jc, sl:sl + 128]))
                                    for c in range(2):
                                        terms.append((cvb[:, c, :], PC[:, c, i * 128:(i + 1) * 128]))
                                    cs_ = slice(ii * 128, (ii + 1) * 128)
                                    for idx, (lh, rh) in enumerate(terms):
                                        kb.mm(po[:, cs_], lh, rh, start=(idx == 0), stop=(idx == len(terms) - 1))
                                        kb.mm(pd[:, cs_], self.ones_bf, rh, start=(idx == 0), stop=(idx == len(terms) - 1))
                                kb.ts("dve", den, pd, es, ALU.add)
                                kb.recip(den, den)
                                kb.tt("dve", Ho[:, bs], po, den, ALU.mult)
                        kb.dma("sp", self.oT[8 + qh, :, tk], Ho[:, 0:L])
            kb.barrier()

    def B_ssm(self, l, pt2, identb):
        kb = self.kb
        I = self.I
        LM = NST
        with contextlib.ExitStack() as st:
            sbf = lambda n, s, d: kb.sb(st, "s_" + n, s, d)
            stg = sbf("stg", [128, 128], F32)
            kb.dma("sp", stg[0:32, :], I["ssm_a_re"][l].rr("d (q g) p -> (d q) (g p)", g=2))
            kb.dma("sp", stg[32:64, :], I["ssm_a_im"][l].rr("d (q g) p -> (d q) (g p)", g=2))
            kb.dma("sp", stg[64:96, :], I["sre"][l].rr("d (q g) p -> (d q) (g p)", g=2))
            kb.dma("sp", stg[96:128, :], I["sim"][l].rr("d (q g) p -> (d q) (g p)", g=2))
            ct = sbf("ct", [128, 128], F32)
            pb = self.bank()
            kb.tr(pb[:, 0:128], stg, self.ident)
            kb.copy("dve", ct, pb[:, 0:128])
            dt = sbf("dt", [128, 32], F32)
            ld = I["ssm_log_dt"]
            for d in range(2):
                for g in range(2):
                    kb.dma("sp", dt[g * 64:(g + 1) * 64, d * 16:(d + 1) * 16],
                           T(ld.ap[l, d, g::2].partition_broadcast(64), ld.dep), allow_slow_non_contiguous=True)
            kb.act(dt, dt, AF.Exp)
            are, aim, h0r, h0i = ct[:, 0:32], ct[:, 32:64], ct[:, 64:96], ct[:, 96:128]
            tb = lambda n: sbf(n, [128, 32], F32)
            mag, fr, sn, cs, t1, t2, t3, wre, wim, nwim, gir, gii = [tb(n) for n in
                ("mag", "fr", "sn", "cs", "t1", "t2", "t3", "wre", "wim", "nwim", "gir", "gii")]
            hpi = sbf("hpi", [128, 1], F32)
            kb.memset("dve", hpi, float(math.pi / 2))
            kb.tt("dve", mag, are, dt, ALU.mult)
            kb.act(mag, mag, AF.Exp)
            kb.tt("dve", fr, aim, dt, ALU.mult)
            kb.ts("dve", fr, fr, float(1.0 / TWO_PI), ALU.mult)
            kb.ts("dve", t1, fr, MAGIC, ALU.add, MAGIC, ALU.subtract)
            kb.tt("dve", fr, fr, t1, ALU.subtract)
            kb.act(sn, fr, AF.Sin, scale=float(TWO_PI))
            kb.act(t1, fr, AF.Abs)
            kb.act(cs, t1, AF.Sin, scale=float(-TWO_PI), bias=hpi)
            kb.tt("dve", t1, mag, cs, ALU.mult)
            kb.ts("dve", t1, t1, -1.0, ALU.add)
            kb.tt("dve", t2, mag, sn, ALU.mult)
            kb.tt("dve", t3, are, are, ALU.mult)
            kb.tt("dve", wre, aim, aim, ALU.mult)
            kb.tt("dve", t3, t3, wre, ALU.add)
            kb.recip(t3, t3)
            kb.tt("dve", wre, t1, are, ALU.mult)
            kb.tt("dve", wim, t2, aim, ALU.mult)
            kb.tt("dve", wre, wre, wim, ALU.add)
            kb.tt("dve", wre, wre, t3, ALU.mult)
            kb.tt("dve", wim, t2, are, ALU.mult)
            kb.tt("dve", nwim, t1, aim, ALU.mult)
            kb.tt("dve", wim, wim, nwim, ALU.subtract)
            kb.tt("dve", wim, wim, t3, ALU.mult)
            kb.ts("dve", nwim, wim, -1.0, ALU.mult)
            kb.tt("dve", gir, cs, h0r, ALU.mult)
            kb.tt("dve", t1, sn, h0i, ALU.mult)
            kb.tt("dve", gir, gir, t1, ALU.subtract)
            kb.tt("dve", gii, sn, h0r, ALU.mult)
            kb.tt("dve", t1, cs, h0i, ALU.mult)
            kb.tt("dve", gii, gii, t1, ALU.add)
            Bre = sbf("Bre", [128, 32, 16], F32)
            Bim = sbf("Bim", [128, 32, 16], F32)
            for d in range(2):
                kb.dma("sp", Bre[:, d * 16:(d + 1) * 16, :], I["ssm_b_re"][l, d].rr("(q g) p c -> (g p) q c", g=2))
                kb.dma("sp", Bim[:, d * 16:(d + 1) * 16, :], I["ssm_b_im"][l, d].rr("(q g) p c -> (g p) q c", g=2))
            Zre = [sbf(f"Zre{r}", [128, 128], F32) for r in range(4)]
            Zim = [sbf(f"Zim{r}", [128, 128], F32) for r in range(4)]
            LCre = [sbf(f"LCre{r}", [128, 128], BF16) for r in range(4)]
            LCim = [sbf(f"LCim{r}", [128, 128], BF16) for r in range(4)]
            for r in range(4):
                kb.memset("dve", Zre[r], 0.0)
                kb.memset("dve", Zim[r], 0.0)
                kb.memset("dve", LCre[r], 0.0)
                kb.memset("dve", LCim[r], 0.0)
            Zc_re = sbf("Zc_re", [32, 128], F32)
            Zc_im = sbf("Zc_im", [32, 128], F32)
            kb.memset("dve", Zc_re, 0.0)
            kb.memset("dve", Zc_im, 0.0)
            LBre = sbf("LBre", [128, 128], BF16)
            LBim = sbf("LBim", [128, 128], BF16)
            tpos = sbf("tpos", [128, LM], F32)
            kb.dma("sp", tpos, I["k_tpos"])
            CS = sbf("CS", [128, LM], F32)
            SN = sbf("SN", [128, LM], F32)
            X1, X2, BA, BB = [sbf(n, [128, NTOK], F32) for n in ("X1", "X2", "BA", "BB")]
            G1, HR, HI = BA, X2, X1
            hrb = sbf("hrb", [128, NTOK], BF16)
            hib = sbf("hib", [128, NTOK], BF16)
            SQ = []
            for (t0_, L_, c_, si_) in SEQS:
                tk_ = slice(t0_, t0_ + L_)
                SQ.append({n_: kb.sub(t_[:, tk_]) for n_, t_ in (("X1", X1), ("X2", X2), ("A", BA), ("B", BB), ("hrb", hrb), ("hib", hib))})
            fcols = [sbf(f"fcol{i}", [128, 2], F32) for i in range(5)]
            TG1 = SQ[4]["A"]
            TG2 = SQ[4]["B"]
            SU = sbf("SU", [128, NTOK], BF16)
            Y = sbf("Y", [128, NTOK], F32)
            ybt = [sbf(f"ybt{i}", [128, 4, TT], BF16) for i in range(2)]
            ybs = hrb
            FIN = sbf("FIN", [128, 4, 2, 32], F32)
            fcol = sbf("fcol", [128, 2], F32)
            wglu = sbf("wglu", [128, 4, 512], BF16)
            kb.dma("pool", wglu, I["ssm_w_glu"][l].rr("(k p) c -> p k c", p=128))
            for j in range(4):
                kb.dma("pool", SU, self.mix[32 + j, :, :])
                first = True
                for d in range(2):
                    for r in range(4):
                        q = 4 * j + r
                        dq = d * 16 + q
                        for g in range(2):
                            rows = slice(g * 64, (g + 1) * 64)
                            cols = slice(r * 32 + g * 16, r * 32 + g * 16 + 16)
                            kb.ts("dve", Zre[r][rows, cols], Bre[rows, dq, :], wre[rows, dq:dq + 1], ALU.mult)
                            kb.stt("dve", Zre[r][rows, cols], Bim[rows, dq, :], nwim[rows, dq:dq + 1], Zre[r][rows, cols], ALU.mult, ALU.add)
                            kb.ts("dve", Zim[r][rows, cols], Bim[rows, dq, :], wre[rows, dq:dq + 1], ALU.mult)
                            kb.stt("dve", Zim[r][rows, cols], Bre[rows, dq, :], wim[rows, dq:dq + 1], Zim[r][rows, cols], ALU.mult, ALU.add)
                        pz = self.bank()
                        kb.tr(pz[:, 0:128], Zre[r], self.ident)
                        kb.copy("act", LBre, pz[:, 0:128])
                        pz = self.bank()
                        kb.tr(pz[:, 0:128], Zim[r], self.ident)
                        kb.copy("act", LBim, pz[:, 0:128])
                        for g in range(2):
                            kb.dma("sp", Zc_re[g * 16:(g + 1) * 16, g * 64:(g + 1) * 64], I["ssm_c_re"][l, d, 2 * q + g])
                            kb.dma("sp", Zc_im[g * 16:(g + 1) * 16, g * 64:(g + 1) * 64], I["ssm_c_im"][l, d, 2 * q + g])
                        pz = self.bank()
                        kb.tr(pz[:, 0:32], Zc_re, self.ident[0:32, 0:32])
                        kb.copy("act", LCre[r][:, r * 32:(r + 1) * 32], pz[:, 0:32])
                        pz = self.bank()
                        kb.tr(pz[:, 0:32], Zc_im, self.ident[0:32, 0:32])
                        kb.ts("dve", LCim[r][:, r * 32:(r + 1) * 32], pz[:, 0:32], -1.0, ALU.mult)
                        kb.ts("dve", TG1, tpos, fr[:, dq:dq + 1], ALU.mult)
                        kb.ts("dve", TG2, TG1, MAGIC, ALU.add, MAGIC, ALU.subtract)
                        kb.tt("dve", TG1, TG1, TG2, ALU.subtract)
                        kb.act(SN, TG1, AF.Sin, scale=float(TWO_PI))
                        kb.act(TG2, TG1, AF.Abs)
                        kb.act(CS, TG2, AF.Sin, scale=float(-TWO_PI), bias=hpi)
                        chains = []
                        for (t0, L, is_ctx, si) in SEQS:
                            chains.append(self._ssm_chain(d, r, dq, t0, L, is_ctx, si, first, SQ[si if is_ctx else 4],
                                                          CS, SN, LBre, LBim, LCre, LCim, SU, Y, mag, gir, gii, FIN, fcols[si if is_ctx else 4]))
                        alive = list(chains)
                        while alive:
                            nxt = []
                            for g in alive:
                                try:
                                    next(g)
                                    nxt.append(g)
                                except StopIteration:
                                    pass
                            alive = nxt
                        first = False
                for (t0, L, is_ctx, si) in SEQS:
                    tk = slice(t0, t0 + L)
                    Bs = SQ[si if is_ctx else 4]
                    kb.dma("sp", Bs["X2"], self.mix[32 + j, :, tk])
                    kb.stt("dve", Y[:, tk], Bs["X2"], pt2[:, 48 + j:49 + j], Y[:, tk], ALU.mult, ALU.add)
                    self.gelu_tanh(Bs["hrb"], Y[:, tk], Bs["X1"])
                    kb.dma("sp", self.yb[j, :, tk], Bs["hrb"])
            for blk in range(NT):
                bs = slice(blk * TT, (blk + 1) * TT)
                yt = ybt[blk % 2]
                kb.dma("sp", yt, self.yb[:, :, bs].rr("k p t -> p k t"))
                for m in range(4):
                    pg = self.bank()
                    for kk in range(4):
                        kb.mm(pg, wglu[:, kk, m * 128:(m + 1) * 128], yt[:, kk, :], start=(kk == 0), stop=(kk == 3))
                    gsl = SQ[4]["A"][:, m * TT:(m + 1) * TT]
                    kb.act(gsl, pg, AF.Sigmoid, bias=pt2[:, 52 + m:53 + m])
                    osl = SQ[4]["hrb"][:, m * TT:(m + 1) * TT]
                    kb.tt("dve", osl, yt[:, m, :], gsl, ALU.mult)
                    kb.dma("sp", self.oT[12 + m, :, bs], osl)
            for si in range(4):
                for c in range(2):
                    pz = self.bank()
                    kb.tr(pz[0:32, 0:128], FIN[:, si, c, :], self.ident)
                    kb.copy("act", stg[0:32, :], pz[0:32, 0:128])
                    dst = self.O["nre" if c == 0 else "nim"][si, l].rr("d (q g) p -> (d q) (g p)", g=2)
                    kb.dma("sp", dst, stg[0:32, :])
            kb.barrier()


    def _ssm_chain(self, d, r, dq, t0, L, is_ctx, si, first, B, CS, SN, LBre, LBim, LCre, LCim, SU, Y, mag, gir, gii, FIN, fcol):
        kb = self.kb
        cb_ = min(512, L)
        nblk = L // cb_
        if d == 0:
            CSf, SNf = CS[:, 0:L], SN[:, 0:L]
        else:
            CSf, SNf = CS[:, 0:L].rev(), SN[:, 0:L].rev()
        X1, X2, A, Bq, hrb, hib = (B[k] for k in ("X1", "X2", "A", "B", "hrb", "hib"))
        for blk in range(nblk):
            bs = slice(blk * cb_, (blk + 1) * cb_)
            ts_ = slice(t0 + blk * cb_, t0 + (blk + 1) * cb_)
            pr = self.bank()
            pi_ = self.bank()
            kb.mm(pr[:, 0:cb_], LBre, SU[:, ts_])
            kb.mm(pi_[:, 0:cb_], LBim, SU[:, ts_])
            kb.copy("act", X1[:, bs], pr[:, 0:cb_])
            kb.copy("act", X2[:, bs], pi_[:, 0:cb_])
        yield
        kb.tt("dve", A, X1, CSf, ALU.mult)
        kb.tt("dve", Bq, X2, SNf, ALU.mult)
        yield
        kb.tt("dve", A, A, Bq, ALU.add)
        yield
        kb.tt("dve", Bq, X2, CSf, ALU.mult)
        kb.tt("dve", X1, X1, SNf, ALU.mult)
        yield
        kb.tt("dve", Bq, Bq, X1, ALU.subtract)
        yield
        magb = mag[:, dq:dq + 1].bc([128, L])
        ir = 0.0 if is_ctx else gir[:, dq:dq + 1]
        ii_ = 0.0 if is_ctx else gii[:, dq:dq + 1]
        if d == 0:
            kb.scan(X2, magb, A, ir)
            kb.scan(X1, magb, Bq, ii_)
        else:
            kb.scan(X2.rev(), magb, A.rev(), ir)
            kb.scan(X1.rev(), magb, Bq.rev(), ii_)
        yield
        kb.tt("dve", A, X2, CSf, ALU.mult)
        kb.tt("dve", Bq, X1, SNf, ALU.mult)
        yield
        kb.tt("dve", hrb, A, Bq, ALU.subtract)
        lc = L - 1 if d == 0 else 0
        if is_ctx:
            kb.tt("dve", FIN[:, si, 0, dq:dq + 1], A[:, lc:lc + 1], Bq[:, lc:lc + 1], ALU.subtract)
        yield
        kb.tt("dve", A, X1, CSf, ALU.mult)
        kb.tt("dve", Bq, X2, SNf, ALU.mult)
        yield
        kb.tt("dve", hib, A, Bq, ALU.add)
        if is_ctx:
            kb.tt("dve", FIN[:, si, 1, dq:dq + 1], A[:, lc:lc + 1], Bq[:, lc:lc + 1], ALU.add)
        yield
        for blk in range(nblk):
            bs = slice(blk * cb_, (blk + 1) * cb_)
            ts_ = slice(t0 + blk * cb_, t0 + (blk + 1) * cb_)
            py = self.bank()
            kb.mm(py[:, 0:cb_], LCre[r], hrb[:, bs], start=True, stop=False)
            kb.mm(py[:, 0:cb_], LCim[r], hib[:, bs], start=False, stop=True)
            if first:
                kb.copy("act", Y[:, ts_], py[:, 0:cb_])
            else:
                kb.tt("dve", Y[:, ts_], Y[:, ts_], py[:, 0:cb_], ALU.add)
        yield

    def run(self):
        self.setup()
        self.phase0()
        for l in range(self.depth):
            self.phaseM(l)
            self.phaseA(l)
            self.phaseB(l)
            self.phaseC(l, last=(l == self.depth - 1))
        outs = list(self.O.values())
        self.kb.barrier()


def build_program(depth=DEPTH, debug=False):
    nc = bass.Bass("TRN2", target_bir_lowering=False)
    with contextlib.ExitStack() as stack:
        kb = KB(nc, stack)
        prog = Prog(kb, stack, depth, debug)
        prog.run()
    return nc, kb


_CACHE = {}


def make_in_maps(inp):
    consts = host_consts()
    f = lambda a: np.ascontiguousarray(np.asarray(a, dtype=np.float32))
    shared = {}
    for n in ("w_mod", "b_mod", "norm1", "w_in", "ret_decay_logit", "ret_gn", "lru_conv_w", "lru_conv_b", "lru_wa", "lru_ba",
              "lru_wx", "lru_bx", "lru_lambda", "att_q_norm", "att_k_norm", "att_sink", "ssm_a_re", "ssm_a_im", "ssm_log_dt",
              "ssm_b_re", "ssm_b_im", "ssm_c_re", "ssm_c_im", "ssm_d", "ssm_w_glu", "ssm_b_glu", "w_out", "norm2", "w_ff1", "w_ff2"):
        shared[n] = f(inp[n])
    shared["w_br"] = f(inp["w_br"]).reshape(DEPTH * 4, 512, D)
    shared.update(consts)
    xp = f(inp["x_prompt"])
    xs = f(inp["x_sample"])
    maps = []
    for c in range(8):
        m = dict(shared)
        m["xp"] = xp[4 * c:4 * c + 4].reshape(NPT, D)
        m["xs"] = xs[c]
        m["ck"] = f(inp["cache_attn_k"][c])
        m["cv"] = f(inp["cache_attn_v"][c])
        m["sret"] = f(inp["state_ret"][c])
        m["slru"] = f(inp["state_lru"][c])
        m["sre"] = f(inp["state_ssm_re"][c])
        m["sim"] = f(inp["state_ssm_im"][c])
        m["cvec"] = np.stack([f(inp["c_ctx"]), f(inp["c"])[c]], axis=0)
        maps.append(m)
    return maps


def kernel(**inputs):
    if "nc" not in _CACHE:
        _CACHE["nc"] = build_program()[0]
    nc = _CACHE["nc"]
    maps = make_in_maps(inputs)
    res = run_bass_kernel_spmd(nc, maps, core_ids=list(range(8)))
    R = res.results
    y_prompt = np.concatenate([r["yp"].reshape(4, 256, D) for r in R], axis=0)
    y_sample = np.stack([r["ys"] for r in R], axis=0)
    cat = lambda n: np.concatenate([r[n] for r in R], axis=0)
    return (y_prompt, y_sample, cat("nk"), cat("nv"), cat("nret"), cat("nlru"), cat("nre"), cat("nim"))
```
